# Optimizing a Trainium2 kernel written in Bass

```python
import math
import jax, jax.numpy as jnp
from jax import lax
import numpy as np

D_MODEL = 2048
BATCH = 4
SEQ = 8192
DEPTH = 1

GDN_HEADS = 8
GDN_HEAD_DIM = 128
GDN_WIDTH = GDN_HEADS * GDN_HEAD_DIM
GDN_CONV = 4
GDN_CHUNK = 64
MOBA_HEADS = 8
MOBA_HEAD_DIM = 128
MOBA_WIDTH = MOBA_HEADS * MOBA_HEAD_DIM
MOBA_BLOCK = 256
MOBA_TOPK = 3
MOBA_Q_CHUNK = 16
REL_BUCKETS = 32
REL_MAX_DIST = 2048
D_FF = ((8 * D_MODEL // 3 + 255) // 256) * 256
PROJ_SPLITS = (GDN_WIDTH,) * 4 + (GDN_HEADS,) * 2 + (MOBA_WIDTH,) * 3 + (D_MODEL,) * 2
D_PROJ = sum(PROJ_SPLITS)
RMS_EPS = 1e-6
NEG_INF = -1e30

kernel_name = 'hybrid_gdn_moba_gated_block'


def _rmsnorm(x, w):
    x32 = x.astype(jnp.float32)
    y = x32 * lax.rsqrt(jnp.mean(x32 * x32, axis=-1, keepdims=True) + RMS_EPS)
    return (y * w.astype(jnp.float32)).astype(x.dtype)


def _l2norm(x):
    return x * lax.rsqrt(jnp.sum(x * x, axis=-1, keepdims=True) + RMS_EPS)


def _causal_short_conv(u, w):
    K = w.shape[0]
    T = u.shape[1]
    up = jnp.pad(u, ((0, 0), (K - 1, 0), (0, 0)))
    y = up[:, 0:T] * w[0]
    for j in range(1, K):
        y = y + up[:, j:j + T] * w[j]
    return jax.nn.silu(y)


def _chunk_gated_delta_rule(q, k, v, g, beta):
    B, H, T, Dk = q.shape
    Dv = v.shape[-1]
    C = GDN_CHUNK
    N = T // C
    q = q * (Dk ** -0.5)
    q, k, v = (t.reshape(B, H, N, C, t.shape[-1]) for t in (q, k, v))
    g = jnp.cumsum(g.reshape(B, H, N, C), axis=-1)
    beta = beta.reshape(B, H, N, C)
    incl = jnp.tril(jnp.ones((C, C), dtype=bool))
    strict = jnp.tril(jnp.ones((C, C), dtype=bool), k=-1)
    diff = g[..., :, None] - g[..., None, :]
    decay = jnp.where(incl, jnp.exp(jnp.where(incl, diff, 0.0)), 0.0)
    k_beta = k * beta[..., None]
    L = jnp.where(strict, jnp.einsum('bhnik,bhnjk->bhnij', k_beta, k) * decay, 0.0)
    eye = jnp.eye(C, dtype=q.dtype)
    rhs = jnp.concatenate([k_beta * jnp.exp(g)[..., None], v * beta[..., None]], axis=-1)
    wu = lax.linalg.triangular_solve(eye + L, rhs, left_side=True, lower=True)
    w, u = wu[..., :Dk], wu[..., Dk:]
    A = jnp.where(incl, jnp.einsum('bhnik,bhnjk->bhnij', q, k) * decay, 0.0)
    g_last = g[..., -1]
    q_dec = q * jnp.exp(g)[..., None]
    k_dec = k * jnp.exp(g_last[..., None] - g)[..., None]
    xs = tuple(jnp.moveaxis(t, 2, 0) for t in (q_dec, k_dec, w, u, A, g_last))

    def step(S, inp):
        q_n, k_n, w_n, u_n, A_n, gl_n = inp
        v_new = u_n - jnp.einsum('bhck,bhkv->bhcv', w_n, S)
        o_n = jnp.einsum('bhck,bhkv->bhcv', q_n, S) + jnp.einsum('bhcs,bhsv->bhcv', A_n, v_new)
        S = S * jnp.exp(gl_n)[..., None, None] + jnp.einsum('bhck,bhcv->bhkv', k_n, v_new)
        return S, o_n

    S0 = jnp.zeros((B, H, Dk, Dv), q.dtype)
    _, o = lax.scan(step, S0, xs)
    return jnp.moveaxis(o, 0, 2).reshape(B, H, T, Dv)


def _gated_deltanet(q, k, v, z, beta_logit, a_logit, conv_w, a_log, dt_bias, o_norm_w):
    B, T, _ = q.shape
    H, Dh = GDN_HEADS, GDN_HEAD_DIM
    f32 = jnp.float32
    qkv = _causal_short_conv(jnp.concatenate([q, k, v], axis=-1), conv_w).astype(f32)
    q, k, v = (t.reshape(B, T, H, Dh).transpose(0, 2, 1, 3) for t in jnp.split(qkv, 3, axis=-1))
    q = _l2norm(q)
    k = _l2norm(k)
    beta = jax.nn.sigmoid(beta_logit.astype(f32)).transpose(0, 2, 1)
    g = -jnp.exp(a_log.astype(f32)) * jax.nn.softplus(a_logit.astype(f32) + dt_bias.astype(f32))
    g = g.transpose(0, 2, 1)
    o = _chunk_gated_delta_rule(q, k, v, g, beta).transpose(0, 2, 1, 3)
    o = _rmsnorm(o, o_norm_w) * jax.nn.silu(z.astype(f32).reshape(B, T, H, Dh))
    return o.reshape(B, T, H * Dh).astype(z.dtype)


def _t5_bucket(dist):
    max_exact = REL_BUCKETS // 2
    d = dist.astype(jnp.float32)
    log_ratio = jnp.log(jnp.maximum(d, float(max_exact)) / max_exact) / math.log(REL_MAX_DIST / max_exact)
    large = max_exact + (log_ratio * (REL_BUCKETS - max_exact)).astype(jnp.int32)
    large = jnp.minimum(large, REL_BUCKETS - 1)
    return jnp.where(dist < max_exact, dist, large)


def _moba_attention(q, k, v, q_norm_w, k_norm_w, rel_bias):
    B, T, _ = q.shape
    H, Dh, BS, QC = MOBA_HEADS, MOBA_HEAD_DIM, MOBA_BLOCK, MOBA_Q_CHUNK
    out_dtype = q.dtype

    def heads(t):
        return t.astype(jnp.float32).reshape(B, T, H, Dh).transpose(0, 2, 1, 3)

    q = _rmsnorm(heads(q), q_norm_w)
    k = _rmsnorm(heads(k), k_norm_w)
    v = heads(v)
    Tp = -(-T // BS) * BS
    pad = ((0, 0), (0, 0), (0, Tp - T), (0, 0))
    q, k, v = (jnp.pad(t, pad) for t in (q, k, v))
    NB = Tp // BS
    topk = min(MOBA_TOPK, NB)
    k_blocks = k.reshape(B, H, NB, BS, Dh)
    v_blocks = v.reshape(B, H, NB, BS, Dh)
    k_mean = jnp.mean(k_blocks, axis=3)
    route = jnp.einsum('bhtd,bhnd->bhtn', q, k_mean)
    q_block = jnp.arange(Tp) // BS
    fully_past = jnp.arange(NB)[None, :] < q_block[:, None]
    route = jnp.where(fully_past, route, NEG_INF)
    _, sel = lax.top_k(route, topk)
    n_qc = Tp // QC
    q_c = jnp.moveaxis(q.reshape(B, H, n_qc, QC, Dh), 2, 0)
    sel_c = jnp.moveaxis(sel.reshape(B, H, n_qc, QC, topk), 2, 0)
    rel_t = rel_bias.astype(jnp.float32).T
    b_idx = jnp.arange(B)[:, None, None, None]
    h_idx = jnp.arange(H)[None, :, None, None]
    scale = Dh ** -0.5
    offs = jnp.arange(BS)

    def attend(args):
        c, q_n, sel_n = args
        q_pos = c * QC + jnp.arange(QC)
        blk = (c * QC) // BS
        k_own = lax.dynamic_index_in_dim(k_blocks, blk, axis=2, keepdims=False)
        v_own = lax.dynamic_index_in_dim(v_blocks, blk, axis=2, keepdims=False)
        dist_own = q_pos[:, None] - (blk * BS + offs)[None, :]
        s_own = jnp.einsum('bhqd,bhkd->bhqk', q_n, k_own) * scale + rel_t[:, _t5_bucket(jnp.maximum(dist_own, 0))][None]
        s_own = jnp.where(dist_own >= 0, s_own, NEG_INF)
        k_sel = k_blocks[b_idx, h_idx, sel_n]
        v_sel = v_blocks[b_idx, h_idx, sel_n]
        dist_sel = q_pos[None, None, :, None, None] - (sel_n[..., None] * BS + offs)
        s_sel = jnp.einsum('bhqd,bhqnkd->bhqnk', q_n, k_sel) * scale + rel_t[h_idx[..., None], _t5_bucket(jnp.maximum(dist_sel, 0))]
        s_sel = jnp.where((sel_n < blk)[..., None], s_sel, NEG_INF)
        logits = jnp.concatenate([s_own, s_sel.reshape(B, H, QC, topk * BS)], axis=-1)
        p = jax.nn.softmax(logits, axis=-1)
        p_own = p[..., :BS]
        p_sel = p[..., BS:].reshape(B, H, QC, topk, BS)
        return jnp.einsum('bhqk,bhkd->bhqd', p_own, v_own) + jnp.einsum('bhqnk,bhqnkd->bhqd', p_sel, v_sel)

    o = lax.map(attend, (jnp.arange(n_qc), q_c, sel_c))
    o = jnp.moveaxis(o, 0, 2).reshape(B, H, Tp, Dh)[:, :, :T]
    return o.transpose(0, 2, 1, 3).reshape(B, T, H * Dh).astype(out_dtype)


def setup_inputs(seed: int = 0) -> dict:
    key = jax.random.key(seed)
    ks = jax.random.split(key, 17)
    f32 = jnp.float32

    def normal(k, shape, fan_in):
        return jax.random.normal(k, shape, f32) * fan_in ** -0.5

    def gain(k, shape):
        return 1.0 + 0.1 * jax.random.normal(k, shape, f32)

    x = jax.random.normal(ks[0], (BATCH, SEQ, D_MODEL), f32)
    norm_mix_w = gain(ks[1], (DEPTH, D_MODEL))
    w_in = normal(ks[2], (DEPTH, D_MODEL, D_PROJ), D_MODEL)
    conv_w = normal(ks[3], (DEPTH, GDN_CONV, 3 * GDN_WIDTH), GDN_CONV)
    a_log = jnp.log(jax.random.uniform(ks[4], (DEPTH, GDN_HEADS), f32, 1.0, 16.0))
    dt = jnp.exp(jax.random.uniform(ks[5], (DEPTH, GDN_HEADS), f32, math.log(1e-3), math.log(1e-1)))
    dt_bias = dt + jnp.log(-jnp.expm1(-dt))
    gdn_o_norm_w = gain(ks[6], (DEPTH, GDN_HEAD_DIM))
    q_norm_w = gain(ks[7], (DEPTH, MOBA_HEAD_DIM))
    k_norm_w = gain(ks[8], (DEPTH, MOBA_HEAD_DIM))
    rel_bias = 0.5 * jax.random.normal(ks[9], (REL_BUCKETS, MOBA_HEADS), f32)
    w_branch_gdn = normal(ks[10], (DEPTH, GDN_WIDTH, D_MODEL), GDN_WIDTH)
    w_branch_moba = normal(ks[11], (DEPTH, MOBA_WIDTH, D_MODEL), MOBA_WIDTH)
    w_out = normal(ks[12], (DEPTH, D_MODEL, D_MODEL), D_MODEL)
    norm_ffn_w = gain(ks[13], (DEPTH, D_MODEL))
    w_ffn_gate = normal(ks[14], (DEPTH, D_MODEL, D_FF), D_MODEL)
    w_ffn_up = normal(ks[15], (DEPTH, D_MODEL, D_FF), D_MODEL)
    w_ffn_down = normal(ks[16], (DEPTH, D_FF, D_MODEL), D_FF)
    return {'x': x, 'norm_mix_w': norm_mix_w, 'w_in': w_in, 'conv_w': conv_w, 'a_log': a_log,
            'dt_bias': dt_bias, 'gdn_o_norm_w': gdn_o_norm_w, 'q_norm_w': q_norm_w, 'k_norm_w': k_norm_w,
            'rel_bias': rel_bias, 'w_branch_gdn': w_branch_gdn, 'w_branch_moba': w_branch_moba,
            'w_out': w_out, 'norm_ffn_w': norm_ffn_w, 'w_ffn_gate': w_ffn_gate, 'w_ffn_up': w_ffn_up,
            'w_ffn_down': w_ffn_down}


def reference(x, norm_mix_w, w_in, conv_w, a_log, dt_bias, gdn_o_norm_w, q_norm_w, k_norm_w,
              rel_bias, w_branch_gdn, w_branch_moba, w_out, norm_ffn_w, w_ffn_gate, w_ffn_up, w_ffn_down):
    split_at = np.cumsum(PROJ_SPLITS)[:-1].tolist()
    h = x
    for l in range(DEPTH):
        u = _rmsnorm(h, norm_mix_w[l])
        proj = jnp.einsum('btd,de->bte', u, w_in[l])
        (q_a, k_a, v_a, z_a, beta_a, dec_a, q_b, k_b, v_b, gate_a, gate_b) = jnp.split(proj, split_at, axis=-1)
        y_a = _gated_deltanet(q_a, k_a, v_a, z_a, beta_a, dec_a, conv_w[l], a_log[l], dt_bias[l], gdn_o_norm_w[l])
        y_b = _moba_attention(q_b, k_b, v_b, q_norm_w[l], k_norm_w[l], rel_bias)
        mix = (jax.nn.sigmoid(gate_a) * jnp.einsum('btc,cd->btd', y_a, w_branch_gdn[l])
               + jax.nn.sigmoid(gate_b) * jnp.einsum('btc,cd->btd', y_b, w_branch_moba[l]))
        h = h + jnp.einsum('btd,de->bte', mix, w_out[l])
        u = _rmsnorm(h, norm_ffn_w[l])
        hid = jax.nn.silu(jnp.einsum('btd,df->btf', u, w_ffn_gate[l])) * jnp.einsum('btd,df->btf', u, w_ffn_up[l])
        h = h + jnp.einsum('btf,fd->btd', hid, w_ffn_down[l])
    return h
```

```python
import contextlib
import math
import numpy as np
import ml_dtypes
import concourse.bass as bass
import concourse.mybir as mybir
from concourse.bass_utils import run_bass_kernel_spmd

F32 = mybir.dt.float32
BF16 = mybir.dt.bfloat16
AF = mybir.ActivationFunctionType
ALU = mybir.AluOpType
AX = mybir.AxisListType

D = 2048
DC = 16
H = 8
DH = 128
W = 1024
DFF = 5632
FC = 44
DPROJ = 11280
EPS = 1e-6
NBKT = 32
BS = 256
NEG = -30000.0
LF = 2559
GW = 2432
FAR_DELTA = 1664
C_QA, C_KA, C_VA, C_ZA, C_BETA, C_DEC, C_QB, C_KB, C_VB, C_GA, C_GB = (
    0, 1024, 2048, 3072, 4096, 4104, 4112, 5136, 6160, 7184, 9232)


class Buf:
    __slots__ = ("name", "w", "r", "excl")

    def __init__(self, name="", excl=False):
        self.name = name
        self.w = None
        self.r = []
        self.excl = excl


class Prog:
    ENGS = ("pe", "act", "dve", "pool", "sp")
    NDMA = 8

    def __init__(self, nc):
        self.nc = nc
        self.stream = {e: [] for e in self.ENGS}
        self.count = {e: 0 for e in self.ENGS}
        self.waited = {e: {} for e in self.ENGS}
        self.dma_i = {e: 0 for e in self.ENGS}
        self.n_ins = 0

    def _deps(self, eng, reads, writes):
        need = {}
        for b in reads:
            if b.w is not None:
                need[b.w[0]] = max(need.get(b.w[0], 0), b.w[1])
            if b.excl:
                for t in b.r:
                    if t[0] != eng:
                        need[t[0]] = max(need.get(t[0], 0), t[1])
        for b in writes:
            if b.w is not None:
                need[b.w[0]] = max(need.get(b.w[0], 0), b.w[1])
            for t in b.r:
                need[t[0]] = max(need.get(t[0], 0), t[1])
        waits = []
        wd = self.waited[eng]
        for key, val in need.items():
            if key == eng and eng == "pe":
                continue
            if wd.get(key, 0) >= val:
                continue
            wd[key] = val
            waits.append((key, val))
        return waits

    def _mark(self, tok, reads, writes):
        for b in writes:
            b.w = tok
            b.r = []
        for b in reads:
            if b not in writes:
                b.r.append(tok)
        self.n_ins += 1

    def op(self, eng, name, reads=(), writes=(), **kw):
        waits = self._deps(eng, reads, writes)
        self.count[eng] += 1
        tok = (eng, self.count[eng])
        self.stream[eng].append(("op", (name, kw), waits))
        self._mark(tok, reads, writes)
        return tok

    def dma(self, eng, out, in_, reads=(), writes=()):
        fn = ("dma_start", dict(out=out, in_=in_))
        waits = self._deps(eng, reads, writes)
        i = self.dma_i[eng]
        self.dma_i[eng] += 1
        key = ("dma", eng, i % self.NDMA)
        val = 16 * (i // self.NDMA + 1)
        prev = val - 16
        if prev > 0 and self.waited[eng].get(key, 0) < prev:
            waits.append((key, prev))
            self.waited[eng][key] = prev
        tok = (key, val)
        self.stream[eng].append(("dma", fn, waits, key))
        self._mark(tok, reads, writes)
        return tok

    def barrier(self):
        toks = [(e, self.count[e]) for e in self.ENGS if self.count[e] > 0]
        for e in self.ENGS:
            for s in range(self.NDMA):
                i = self.dma_i[e]
                n = (i - s + self.NDMA - 1) // self.NDMA if i > s else 0
                if n > 0:
                    toks.append((("dma", e, s), 16 * n))
        for e in self.ENGS:
            waits = []
            for key, val in toks:
                if key == e and e == "pe":
                    continue
                if self.waited[e].get(key, 0) >= val:
                    continue
                self.waited[e][key] = val
                waits.append((key, val))
            if waits:
                self.stream[e].append(("wait", None, waits))

    def emit(self):
        nc = self.nc
        with contextlib.ExitStack() as es:
            sems = {}
            for e in self.ENGS:
                sems[e] = es.enter_context(nc.semaphore("p_" + e))
                for s in range(self.NDMA):
                    sems[("dma", e, s)] = es.enter_context(nc.semaphore("d_%s_%d" % (e, s)))
            es.enter_context(nc.allow_non_contiguous_dma(reason="tiny strided constant loads"))
            block = es.enter_context(nc.Block())

            def run(e, handle):
                for item in self.stream[e]:
                    for key, val in item[2]:
                        handle.wait_ge(sems[key], val)
                    if item[0] == "op":
                        getattr(handle, item[1][0])(**item[1][1]).then_inc(sems[e], 1)
                    elif item[0] == "dma":
                        getattr(handle, item[1][0])(**item[1][1]).then_inc(sems[item[3]], 16)

            @block.tensor
            def _(h):
                run("pe", h)

            @block.scalar
            def _(h):
                run("act", h)

            @block.vector
            def _(h):
                run("dve", h)

            @block.gpsimd
            def _(h):
                run("pool", h)

            @block.sync
            def _(h):
                run("sp", h)


class Ring:
    def __init__(self, aps, excl=False):
        self.items = [(ap, Buf(excl=excl)) for ap in aps]
        self.i = 0

    def next(self):
        it = self.items[self.i % len(self.items)]
        self.i += 1
        return it


class Arena:
    def __init__(self, t, width):
        self.t = t
        self.width = width
        self.off = 0

    def alloc(self, n, dtype=F32, parts=128):
        n32 = n if dtype == F32 else (n + 1) // 2
        assert self.off + n32 <= self.width, ("arena overflow", self.off, n32, self.width)
        ap = self.t[0:parts, self.off:self.off + n32]
        self.off += n32
        if dtype != F32:
            ap = ap.bitcast(dtype)[:, 0:n]
        return ap

    def mark(self):
        return self.off

    def release(self, m):
        self.off = m


def t5_bucket_np(dist):
    max_exact = NBKT // 2
    d = dist.astype(np.float32)
    log_ratio = np.log(np.maximum(d, np.float32(max_exact)) / np.float32(max_exact)) / np.float32(
        math.log(2048 / max_exact))
    large = max_exact + (log_ratio.astype(np.float32) * np.float32(NBKT - max_exact)).astype(np.int32)
    large = np.minimum(large, NBKT - 1)
    return np.where(dist < max_exact, dist, large)


def host_consts(T_PRE, T_LOC):
    c = {}
    idx = np.arange(128)
    same = (idx[:, None] // 64) == (idx[None, :] // 64)
    c["ident"] = np.eye(128, dtype=np.float32)
    c["U"] = (same & (idx[:, None] <= idx[None, :])).astype(np.float32)
    c["cm0"] = np.tile((idx[:, None] < 64), (1, 128)).astype(np.float32)
    c["cm1"] = np.tile((idx[:, None] >= 64), (1, 128)).astype(np.float32)
    c["bd"] = same.astype(np.float32)
    c["bigL"] = np.where(same & (idx[:, None] > idx[None, :]), 0.0, -NEG).astype(np.float32)
    c["negU"] = np.where(same & (idx[None, :] >= idx[:, None]), 0.0, NEG).astype(np.float32)
    i = np.arange(LF)
    dist = i - 511
    bk = t5_bucket_np(np.maximum(dist, 0))
    E = np.zeros((33, LF), np.float32)
    E[bk, i] = 1.0
    E[:, dist < 0] = 0.0
    E[32, dist < 0] = 1.0
    c["ebk"] = E
    es = np.zeros((32, 32, 128), np.float32)
    for n in range(32):
        es[n, n, :] = 1.0
    c["esel"] = es.astype(ml_dtypes.bfloat16)
    nqb = T_LOC // BS
    npb = T_PRE // BS
    own = np.zeros((nqb, 32), np.float32)
    for qb in range(nqb):
        own[qb, npb + qb] = 1.0
    c["own"] = own
    return c


def rmask_for(half, T_PRE, T_LOC):
    nqb = T_LOC // BS
    npb = T_PRE // BS
    m = np.zeros((nqb, 32), np.float32)
    for qb in range(nqb):
        m[qb, npb + qb:] = -1e30
        if half == 0:
            m[qb, :npb] = -1e30
    return m


def build_program(T_PRE, T_LOC, debug=False, phases=(1, 2, 3, 4, 5)):
    TT = T_PRE + T_LOC
    TB = 512
    NBLK = TT // TB
    NPREB = T_PRE // TB
    NT = TT // 128
    nc = bass.Bass("TRN2", target_bir_lowering=False)
    P = Prog(nc)
    skind = "ExternalOutput" if debug else "Internal"

    def din(name, shape, dt=F32):
        return nc.dram_tensor(name, list(shape), dt, kind="ExternalInput").ap()

    def dscr(name, shape, dt):
        return nc.dram_tensor(name, list(shape), dt, kind=skind).ap()

    x_in = din("x", [TT, D])
    w_in = din("w_in", [D, DPROJ])
    norm_mix_w = din("norm_mix_w", [1, D])
    conv_w = din("conv_w", [4, 3 * W])
    a_log = din("a_log", [1, H])
    dt_bias = din("dt_bias", [1, H])
    gdn_o_norm_w = din("gdn_o_norm_w", [1, DH])
    q_norm_w = din("q_norm_w", [1, DH])
    k_norm_w = din("k_norm_w", [1, DH])
    rel_bias = din("rel_bias", [NBKT, H])
    w_bg = din("w_branch_gdn", [W, D])
    w_bm = din("w_branch_moba", [W, D])
    w_out = din("w_out", [D, D])
    norm_ffn_w = din("norm_ffn_w", [1, D])
    w_fg = din("w_ffn_gate", [D, DFF])
    w_fu = din("w_ffn_up", [D, DFF])
    w_fd = din("w_ffn_down", [DFF, D])
    c_ident = din("c_ident", [128, 128])
    c_U = din("c_U", [128, 128])
    c_cm0 = din("c_cm0", [128, 128])
    c_cm1 = din("c_cm1", [128, 128])
    c_bd = din("c_bd", [128, 128])
    c_bigL = din("c_bigL", [128, 128])
    c_negU = din("c_negU", [128, 128])
    c_ebk = din("c_ebk", [33, LF])
    c_esel = din("c_esel", [32, 32, 128], BF16)
    c_own = din("c_own", [T_LOC // BS, 32])
    c_rmask = din("c_rmask", [T_LOC // BS, 32])
    y_out = nc.dram_tensor("y", [T_LOC, D], F32, kind="ExternalOutput").ap()

    wb_in = dscr("wb_in", [D, DPROJ], BF16)
    wb_bg = dscr("wb_bg", [W, D], BF16)
    wb_bm = dscr("wb_bm", [W, D], BF16)
    wb_out = dscr("wb_out", [D, D], BF16)
    wb_fg = dscr("wb_fg", [D, DFF], BF16)
    wb_fu = dscr("wb_fu", [D, DFF], BF16)
    wb_fd = dscr("wb_fd", [DFF, D], BF16)
    s_qa = dscr("s_qa", [H, 128, TT], BF16)
    s_ka = dscr("s_ka", [H, 128, TT], BF16)
    s_va = dscr("s_va", [H, 128, TT], BF16)
    s_za = dscr("s_za", [H, 128, T_LOC], BF16)
    s_bd = dscr("s_bd", [TT, 16], F32)
    s_qb = dscr("s_qb", [H, 128, T_LOC], BF16)
    s_kb = dscr("s_kb", [H, 128, TT], BF16)
    s_vb = dscr("s_vb", [TT, W], BF16)
    s_msel = dscr("s_msel", [H, 32, T_LOC], BF16)
    s_sg = dscr("s_sg", [2, DC, 128, T_LOC], BF16)
    s_ya = dscr("s_ya", [H, 128, T_LOC], BF16)
    s_yb = dscr("s_yb", [H, 128, T_LOC], BF16)
    s_frep = dscr("s_frep", [H, 128, LF], F32)
    s_h = dscr("s_h", [T_LOC, D], F32)

    wb_tok = {}
    bufs_w = {}

    with contextlib.ExitStack() as es:
        AW = 51200
        arena_t = es.enter_context(nc.sbuf_tensor("arena", [128, AW], F32))
        A = Arena(arena_t, AW)
        psum = []
        for i in range(8):
            pt = es.enter_context(nc.psum_tensor("ps%d" % i, [128, 512], F32))
            psum.append(pt[:, :])

        def mm(out, lhsT, rhs, start, stop, reads, writes):
            P.op("pe", "matmul", reads, writes, out=out, lhsT=lhsT, rhs=rhs, start=start, stop=stop)

        def act(out, in_, func, reads, writes, **kw):
            P.op("act", "activation", reads, writes, out=out, in_=in_, func=func, **kw)

        def ts(eng, out, in0, s1, s2, op0, op1, reads, writes):
            if op1 is None:
                P.op(eng, "tensor_scalar", reads, writes, out=out, in0=in0, scalar1=s1, scalar2=None, op0=op0)
            else:
                P.op(eng, "tensor_scalar", reads, writes, out=out, in0=in0, scalar1=s1, scalar2=s2, op0=op0, op1=op1)

        def stt(eng, out, in0, scalar, in1, op0, op1, reads, writes):
            P.op(eng, "scalar_tensor_tensor", reads, writes, out=out, in0=in0, scalar=scalar, in1=in1, op0=op0, op1=op1)

        def tt(eng, out, in0, in1, op, reads, writes):
            P.op(eng, "tensor_tensor", reads, writes, out=out, in0=in0, in1=in1, op=op)

        def cp(eng, out, in_, reads, writes):
            if eng == "act":
                act(out, in_, AF.Copy, reads, writes)
            else:
                P.op(eng, "tensor_copy", reads, writes, out=out, in_=in_)

        def memset(eng, ap, val, writes):
            P.op(eng, "memset", (), writes, ap=ap, constant=val)

        def load_const(ap_dram, n, dtype=F32, parts=128):
            t = A.alloc(n, dtype, parts)
            b = Buf()
            P.dma("sp", t, ap_dram, writes=[b])
            return t, b

        ident, b_ident = load_const(c_ident, 128)
        Umat, b_U = load_const(c_U, 128)
        cm0, b_cm0 = load_const(c_cm0, 128)
        cm1, b_cm1 = load_const(c_cm1, 128)
        bdm, b_bdm = load_const(c_bd, 128)
        bigL, b_bigL = load_const(c_bigL, 128)
        negU, b_negU = load_const(c_negU, 128)
        ident_bf = A.alloc(128, BF16)
        b_identbf = Buf()
        cp("dve", ident_bf, ident, [b_ident], [b_identbf])
        ones_bf = A.alloc(128, BF16)
        b_ones = Buf()
        memset("dve", ones_bf, 1.0, [b_ones])
        ones_f = A.alloc(128, F32)
        b_onesf = Buf()
        memset("dve", ones_f, 1.0, [b_onesf])
        inv128_bf = A.alloc(128, BF16)
        b_inv128 = Buf()
        memset("dve", inv128_bf, 1.0 / 128.0, [b_inv128])

        def bcast_load(ap_row, n):
            t = A.alloc(n, F32)
            b = Buf()
            src = bass.AP(tensor=ap_row.tensor, offset=ap_row.offset, ap=[[0, 128], [1, n]])
            P.dma("sp", t, src, writes=[b])
            return t, b

        nmw, b_nmw = bcast_load(norm_mix_w, D)
        alog_b, b_alog = bcast_load(a_log, H)
        dtb_b, b_dtb = bcast_load(dt_bias, H)

        def col_load(ap_row, n=128):
            t = A.alloc(1, F32)
            b = Buf()
            src = bass.AP(tensor=ap_row.tensor, offset=ap_row.offset, ap=[[1, n], [1, 1]])
            P.dma("sp", t, src, writes=[b])
            return t, b
        onw_c, b_onw = col_load(gdn_o_norm_w)
        qnw_c, b_qnw = col_load(q_norm_w)
        knw_c, b_knw = col_load(k_norm_w)
        cw = A.alloc(24 * 4, F32)
        cw3 = cw.rearrange("p (g j) -> p g j", j=4)
        b_cw = Buf()
        for j in range(4):
            src = bass.AP(tensor=conv_w.tensor, offset=conv_w.offset + j * 3 * W, ap=[[1, 128], [128, 24], [1, 1]])
            P.dma("sp", cw3[:, :, j:j + 1], src, writes=[b_cw])
        nA = A.alloc(H, F32)
        b_nA = Buf()
        act(nA, alog_b, AF.Exp, [b_alog], [b_nA])
        ts("dve", nA, nA, -1.0, None, ALU.mult, None, [b_nA], [b_nA])
        hist = A.alloc(24 * 3, F32)
        hist3 = hist.rearrange("p (g j) -> p g j", j=3)
        b_hist = [Buf() for _ in range(24)]
        memset("dve", hist, 0.0, b_hist)
        kmean = A.alloc(H * 32, F32)
        km3 = kmean.rearrange("p (h n) -> p h n", n=32)
        b_kmean = Buf()
        memset("dve", kmean, 0.0, [b_kmean])

        def cast_w(src, dst, cols, cstep, name):
            for c0 in range(0, cols, cstep):
                cn = min(cstep, cols - c0)
                b = Buf()
                P.dma("pool", dst[:, c0:c0 + cn], src[:, c0:c0 + cn], writes=[b])
                bufs_w[(name, c0)] = b

        def store(dst, src, b_src):
            P.dma("pool", dst, src, reads=[b_src])

        def phase1():
            cast_w(w_in, wb_in, DPROJ, 512, "in")
            m0 = A.mark()
            xt_r = Ring([A.alloc(D, F32) for _ in range(2)])
            xn_r = Ring([A.alloc(D, BF16) for _ in range(2)])
            uT_r = Ring([A.alloc(DC * TB, BF16) for _ in range(2)])
            wt_r = Ring([A.alloc(DC * 512, BF16) for _ in range(3)])
            cb_r = Ring([A.alloc(TB + 3, F32) for _ in range(2)])
            acc_r = Ring([A.alloc(TB, F32) for _ in range(2)])
            sil_r = Ring([A.alloc(TB, F32) for _ in range(3)])
            sq_r = Ring([A.alloc(TB, BF16) for _ in range(3)])
            rinv_r = Ring([A.alloc(TB, F32) for _ in range(2)])
            o16_r = Ring([A.alloc(TB, BF16) for _ in range(10)])
            o32_r = Ring([A.alloc(TB, F32) for _ in range(6)])
            small_r = Ring([A.alloc(8, F32) for _ in range(4)])
            rt_r = Ring([A.alloc(4 * 32, F32) for _ in range(2)])
            mx_r = Ring([A.alloc(4 * 8, F32) for _ in range(2)])
            sel_r = Ring([A.alloc(4 * 32, F32) for _ in range(3)])
            ms_r = Ring([A.alloc(TB, BF16, 32) for _ in range(5)])
            rmk_r = Ring([A.alloc(64, F32) for _ in range(2)])
            own_r = Ring([A.alloc(64, F32) for _ in range(2)])
            ps_r = Ring([psum[i] for i in range(6)], excl=True)
            ps_aux = Ring([psum[6], psum[7]], excl=True)
            wb_in3 = wb_in.rearrange("(c p) n -> p c n", p=128)

            def norm_sq(src, b_src):
                sq, b_sq = sq_r.next()
                act(sq, src, AF.Square, [b_src], [b_sq])
                return sq, b_sq

            def norm_rinv(sq, b_sq, kind):
                pa, b_pa = ps_aux.next()
                lw = ones_bf if kind == "l2" else inv128_bf
                mm(pa, lw, sq, True, True, [b_sq, b_ones, b_inv128], [b_pa])
                rinv, b_rinv = rinv_r.next()
                act(rinv, pa, AF.Sqrt, [b_pa], [b_rinv], bias=EPS, scale=1.0)
                P.op("dve", "reciprocal", [b_rinv], [b_rinv], out=rinv, in_=rinv)
                return rinv, b_rinv

            for bi in range(NBLK):
                local = bi >= NPREB
                t0 = bi * TB
                tl0 = t0 - T_PRE
                uT, b_uT = uT_r.next()
                uT3 = uT.rearrange("p (c n) -> p c n", c=DC)
                for s in range(4):
                    xt, b_xt = xt_r.next()
                    r0 = t0 + s * 128
                    P.dma("sp", xt, x_in[r0:r0 + 128, :], writes=[b_xt])
                    xn, b_xn = xn_r.next()
                    sm, b_sm = small_r.next()
                    act(xn, xt, AF.Square, [b_xt], [b_xn, b_sm], accum_out=sm[:, 0:1])
                    act(sm[:, 1:2], sm[:, 0:1], AF.Sqrt, [b_sm], [b_sm], bias=EPS, scale=1.0 / D)
                    P.op("dve", "reciprocal", [b_sm], [b_sm], out=sm[:, 2:3], in_=sm[:, 1:2])
                    stt("dve", xn, xt, sm[:, 2:3], nmw, ALU.mult, ALU.mult, [b_xt, b_sm, b_nmw], [b_xn])
                    for q4 in range(4):
                        ps, b_ps = ps_aux.next()
                        for cc in range(4):
                            c = q4 * 4 + cc
                            mm(ps[:, cc * 128:(cc + 1) * 128], xn[:, c * 128:(c + 1) * 128], ident_bf, True, True,
                               [b_xn, b_identbf], [b_ps])
                        dst = uT3[:, q4 * 4:(q4 + 1) * 4, s * 128:(s + 1) * 128]
                        src = ps.rearrange("p (c n) -> p c n", c=4)
                        cp("act" if q4 % 2 == 0 else "dve", dst, src, [b_ps], [b_uT])

                def load_w(c0, cn):
                    wt, b_wt = wt_r.next()
                    wt3 = wt.rearrange("p (c n) -> p c n", c=DC)
                    deps = [bufs_w[("in", cc0)] for cc0 in range((c0 // 512) * 512, c0 + cn, 512)]
                    P.dma("sp", wt3[:, :, 0:cn], wb_in3[:, :, c0:c0 + cn], reads=deps, writes=[b_wt])
                    return wt3, b_wt

                pending = []
                st_q = []

                def store(dst, src, b_src):
                    st_q.append([0, dst, src, b_src])

                def step_stores(force=False):
                    keep = []
                    for it in st_q:
                        if force or it[0] >= 2:
                            P.dma("sp", it[1], it[2], reads=[it[3]])
                        else:
                            it[0] += 1
                            keep.append(it)
                    st_q[:] = keep

                def step_pending():
                    for gen in list(pending):
                        try:
                            next(gen)
                        except StopIteration:
                            pending.remove(gen)
                    step_stores()

                def flush_pending():
                    while pending:
                        step_pending()
                    step_stores(force=True)

                def fm_groups(c0, ngroups, post):
                    for j0 in range(0, ngroups, 4):
                        ng = min(4, ngroups - j0)
                        wt3, b_wt = load_w(c0 + j0 * 128, ng * 128)
                        for g in range(ng):
                            ps, b_ps = ps_r.next()
                            for c in range(DC):
                                mm(ps, wt3[:, c, g * 128:(g + 1) * 128], uT3[:, c, :], c == 0, c == DC - 1,
                                   [b_wt, b_uT], [b_ps])
                            step_pending()
                            gen = post(j0 + g, ps, b_ps)
                            if gen is not None and hasattr(gen, "__next__"):
                                try:
                                    next(gen)
                                    pending.append(gen)
                                except StopIteration:
                                    pass

                def conv_post(fam, dst_scr):
                    def post(g, ps, b_ps):
                        gi = fam * 8 + g
                        cb, b_cb = cb_r.next()
                        cp("pool", cb[:, 0:3], hist3[:, gi, :], [b_hist[gi]], [b_cb])
                        cp("act", cb[:, 3:TB + 3], ps, [b_ps], [b_cb])
                        cp("pool", hist3[:, gi, :], cb[:, TB:TB + 3], [b_cb], [b_hist[gi]])
                        acc, b_acc = acc_r.next()
                        ts("dve", acc, cb[:, 0:TB], cw3[:, gi, 0:1], None, ALU.mult, None, [b_cb, b_cw], [b_acc])
                        for j in range(1, 4):
                            stt("dve", acc, cb[:, j:j + TB], cw3[:, gi, j:j + 1], acc, ALU.mult, ALU.add,
                                [b_cb, b_cw, b_acc], [b_acc])
                        sil, b_sil = sil_r.next()
                        act(sil, acc, AF.Silu, [b_acc], [b_sil])
                        o16, b_o16 = o16_r.next()
                        if fam == 2:
                            cp("dve", o16, sil, [b_sil], [b_o16])
                        else:
                            sq, b_sq = norm_sq(sil, b_sil)
                            yield
                            yield
                            rinv, b_rinv = norm_rinv(sq, b_sq, "l2")
                            if fam == 0:
                                stt("dve", o16, sil, float(DH ** -0.5), rinv, ALU.mult, ALU.mult,
                                    [b_sil, b_rinv], [b_o16])
                            else:
                                tt("dve", o16, sil, rinv, ALU.mult, [b_sil, b_rinv], [b_o16])
                        store(dst_scr[g, :, t0:t0 + TB], o16, b_o16)
                    return post

                def za_post(g, ps, b_ps):
                    o16, b_o16 = o16_r.next()
                    act(o16, ps, AF.Silu, [b_ps], [b_o16])
                    store(s_za[g, :, tl0:tl0 + TB], o16, b_o16)

                def gate_post(which):
                    def post(g, ps, b_ps):
                        o16, b_o16 = o16_r.next()
                        act(o16, ps, AF.Sigmoid, [b_ps], [b_o16])
                        store(s_sg[which, g, :, tl0:tl0 + TB], o16, b_o16)
                    return post

                def kb_post(g, ps, b_ps):
                    sq, b_sq = norm_sq(ps, b_ps)
                    yield
                    rinv, b_rinv = norm_rinv(sq, b_sq, "rms")
                    o32, b_o32 = o32_r.next()
                    stt("dve", o32, ps, knw_c[:, 0:1], rinv, ALU.mult, ALU.mult, [b_ps, b_rinv, b_knw], [b_o32])
                    o16, b_o16 = o16_r.next()
                    cp("act", o16, o32, [b_o32], [b_o16])
                    store(s_kb[g, :, t0:t0 + TB], o16, b_o16)
                    nb0 = t0 // BS
                    P.op("dve", "tensor_reduce", [b_o32], [b_kmean], out=km3[:, g, nb0:nb0 + 2],
                         in_=o32.rearrange("p (b k) -> p b k", k=BS), axis=AX.X, op=ALU.add)

                def qb_post(g, ps, b_ps):
                    sq, b_sq = norm_sq(ps, b_ps)
                    yield
                    rinv, b_rinv = norm_rinv(sq, b_sq, "rms")
                    o32, b_o32 = o32_r.next()
                    stt("dve", o32, ps, qnw_c[:, 0:1], rinv, ALU.mult, ALU.mult, [b_ps, b_rinv, b_qnw], [b_o32])
                    o16, b_o16 = o16_r.next()
                    cp("act", o16, o32, [b_o32], [b_o16])
                    store(s_qb[g, :, tl0:tl0 + TB], o16, b_o16)
                    yield
                    pa, b_pa = ps_aux.next()
                    for s in range(4):
                        mm(pa[:, s * 32:(s + 1) * 32], o32[:, s * 128:(s + 1) * 128], km3[:, g, :], True, True,
                           [b_o32, b_kmean], [b_pa])
                    rt, b_rt = rt_r.next()
                    rmk, b_rmk = rmk_cur
                    own, b_own = own_cur
                    r3 = "p (b n) -> p b n"
                    for a in range(2):
                        rmk3 = rmk[:, a * 32:(a + 1) * 32].unsqueeze(1).to_broadcast([128, 2, 32])
                        stt("dve", rt[:, a * 64:(a + 1) * 64].rearrange(r3, b=2),
                            pa[:, a * 64:(a + 1) * 64].rearrange(r3, b=2), 1.0 / BS, rmk3,
                            ALU.mult, ALU.add, [b_pa, b_rmk], [b_rt])
                    mx, b_mx = mx_r.next()
                    for s in range(4):
                        P.op("dve", "max", [b_rt], [b_mx], out=mx[:, s * 8:(s + 1) * 8], in_=rt[:, s * 32:(s + 1) * 32])
                    sel, b_sel = sel_r.next()
                    thr = mx.rearrange("p (s k) -> p s k", k=8)[:, :, 2:3].to_broadcast([128, 4, 32])
                    sel3 = sel.rearrange("p (s n) -> p s n", n=32)
                    rt3 = rt.rearrange("p (s n) -> p s n", n=32)
                    tt("dve", sel3, rt3, thr, ALU.is_ge, [b_rt, b_mx], [b_sel])
                    stt("dve", sel, rt, -1e29, sel, ALU.is_gt, ALU.mult, [b_rt, b_sel], [b_sel])
                    for a in range(2):
                        own3 = own[:, a * 32:(a + 1) * 32].unsqueeze(1).to_broadcast([128, 2, 32])
                        sv = sel[:, a * 64:(a + 1) * 64].rearrange(r3, b=2)
                        tt("dve", sv, sv, own3, ALU.add, [b_sel, b_own], [b_sel])
                    ts("dve", sel, sel, -1.0, -NEG, ALU.add, ALU.mult, [b_sel], [b_sel])
                    yield
                    yield
                    pb, b_pb = ps_aux.next()
                    for s in range(4):
                        mm(pb[0:32, s * 128:(s + 1) * 128], sel[:, s * 32:(s + 1) * 32], ident, True, True,
                           [b_sel, b_ident], [b_pb])
                    ms, b_ms = ms_r.next()
                    cp("act", ms, pb[0:32, :], [b_pb], [b_ms])
                    store(s_msel[g, :, tl0:tl0 + TB], ms, b_ms)

                if local or bi == NPREB - 1:
                    fm_groups(C_QA, 8, conv_post(0, s_qa))
                fm_groups(C_KA, 8, conv_post(1, s_ka))
                fm_groups(C_VA, 8, conv_post(2, s_va))
                if local:
                    fm_groups(C_ZA, 8, za_post)
                flush_pending()
                wt3, b_wt = load_w(C_BETA, 16)
                for s in range(4):
                    ps, b_ps = ps_r.next()
                    for c in range(DC):
                        mm(ps[:, 0:16], uT3[:, c, s * 128:(s + 1) * 128], wt3[:, c, 0:16], c == 0, c == DC - 1,
                           [b_wt, b_uT], [b_ps])
                    o32, b_o32 = o32_r.next()
                    cp("act", o32[:, 0:16], ps[:, 0:16], [b_ps], [b_o32])
                    store(s_bd[t0 + s * 128:t0 + (s + 1) * 128, :], o32[:, 0:16], b_o32)
                    step_stores()
                fm_groups(C_KB, 8, kb_post)
                flush_pending()
                for j in range(2):
                    wt3, b_wt = load_w(C_VB + j * 512, 512)
                    for s in range(4):
                        ps, b_ps = ps_r.next()
                        for c in range(DC):
                            mm(ps, uT3[:, c, s * 128:(s + 1) * 128], wt3[:, c, :], c == 0, c == DC - 1,
                               [b_wt, b_uT], [b_ps])
                        o16, b_o16 = o16_r.next()
                        cp("act", o16, ps, [b_ps], [b_o16])
                        store(s_vb[t0 + s * 128:t0 + (s + 1) * 128, j * 512:(j + 1) * 512], o16, b_o16)
                        step_stores()
                if local:
                    qb0 = tl0 // BS
                    rmk, b_rmk = rmk_r.next()
                    src = bass.AP(tensor=c_rmask.tensor, offset=c_rmask.offset + qb0 * 32, ap=[[0, 128], [1, 64]])
                    P.dma("sp", rmk, src, writes=[b_rmk])
                    rmk_cur = (rmk, b_rmk)
                    own, b_own = own_r.next()
                    src2 = bass.AP(tensor=c_own.tensor, offset=c_own.offset + qb0 * 32, ap=[[0, 128], [1, 64]])
                    P.dma("sp", own, src2, writes=[b_own])
                    own_cur = (own, b_own)
                    fm_groups(C_QB, 8, qb_post)
                    fm_groups(C_GA, 16, gate_post(0))
                    fm_groups(C_GB, 16, gate_post(1))
                flush_pending()
            P.barrier()
            A.release(m0)

        if 1 in phases:
            phase1()

        def phase2():
            m0 = A.mark()
            NTL0 = T_PRE // 128
            wk_r = Ring([psum[0], psum[1], psum[2], psum[3]], excl=True)
            vs_b = [(psum[4], Buf(excl=True)), (psum[5], Buf(excl=True))]
            ot_b = [(psum[6], Buf(excl=True)), (psum[7], Buf(excl=True))]

            def T32(n=128):
                return [(A.alloc(n, F32), Buf()) for _ in range(H)]

            def T16(n=128):
                return [(A.alloc(n, BF16), Buf()) for _ in range(H)]
            kbg, vbt, dg1, dg2, X1, X2, dIm, dTm = T32(), T32(), T32(), T32(), T32(), T32(), T32(), T32()
            Lt = [T32(), T32()]
            Mt = [T32(), T32()]
            Yt = T32()
            kdec, negw, ATt, qdec, vnb = T16(), T16(), T16(), T16(), T16()
            S32 = T32()
            Sbf = T16()
            for h in range(H):
                memset("dve", S32[h][0], 0.0, [S32[h][1]])
                memset("pool", Sbf[h][0], 0.0, [Sbf[h][1]])
            in_r = [Ring([A.alloc(H * 128, BF16) for _ in range(2)]) for _ in range(4)]
            bd_r = Ring([A.alloc(16, F32) for _ in range(2)])
            g_r = Ring([A.alloc(96, F32) for _ in range(2)])
            sq_r = Ring([A.alloc(512, BF16) for _ in range(2)])
            rv_r = Ring([A.alloc(512, F32) for _ in range(2)])
            y32_r = Ring([A.alloc(512, F32) for _ in range(2)])
            y16_r = Ring([A.alloc(512, BF16) for _ in range(2)])

            def ld8(ring, scr, c0):
                t, b = ring.next()
                t3 = t.rearrange("p (h n) -> p h n", h=H)
                P.dma("sp", t3, scr[:, :, c0:c0 + 128].rearrange("h p t -> p h t"), writes=[b])
                return t3, b

            for i in range(NT):
                local = i >= NTL0
                t0 = i * 128
                tl0 = t0 - T_PRE
                kT, b_kT = ld8(in_r[0], s_ka, t0)
                vT, b_vT = ld8(in_r[1], s_va, t0)
                if local:
                    qT, b_qT = ld8(in_r[2], s_qa, t0)
                    zT, b_zT = ld8(in_r[3], s_za, tl0)
                bdt, b_bdt = bd_r.next()
                P.dma("sp", bdt, s_bd[t0:t0 + 128, :], writes=[b_bdt])
                g, b_g = g_r.next()
                act(g[:, 0:8], bdt[:, 0:8], AF.Sigmoid, [b_bdt], [b_g])
                tt("dve", g[:, 8:16], bdt[:, 8:16], dtb_b, ALU.add, [b_bdt, b_dtb], [b_g])
                act(g[:, 8:16], g[:, 8:16], AF.Exp, [b_g], [b_g])
                act(g[:, 8:16], g[:, 8:16], AF.Ln, [b_g], [b_g], bias=1.0)
                tt("dve", g[:, 8:16], g[:, 8:16], nA, ALU.mult, [b_g, b_nA], [b_g])
                pg, b_pg = wk_r.next()
                mm(pg[:, 0:8], Umat, g[:, 8:16], True, True, [b_U, b_g], [b_pg])
                mm(pg[:, 8:16], cm0, g[:, 8:16], True, True, [b_cm0, b_g], [b_pg])
                mm(pg[:, 16:24], cm1, g[:, 8:16], True, True, [b_cm1, b_g], [b_pg])
                mm(pg[:, 24:32], bdm, g[:, 8:16], True, True, [b_bdm, b_g], [b_pg])
                cp("dve", g[:, 16:48], pg[:, 0:32], [b_pg], [b_g])
                act(g[:, 48:56], g[:, 16:24], AF.Exp, [b_g], [b_g])
                tt("dve", g[:, 56:64], g[:, 40:48], g[:, 16:24], ALU.subtract, [b_g], [b_g])
                act(g[:, 56:64], g[:, 56:64], AF.Exp, [b_g], [b_g])
                act(g[:, 64:80], g[:, 24:40], AF.Exp, [b_g], [b_g])
                tt("dve", g[:, 80:88], g[:, 0:8], g[:, 48:56], ALU.mult, [b_g], [b_g])
                beta = lambda h: g[:, h:h + 1]
                gcum = lambda h: g[:, 16 + h:17 + h]
                eg = lambda h: g[:, 48 + h:49 + h]
                ekd = lambda h: g[:, 56 + h:57 + h]
                eglB = lambda c, h: g[:, 64 + c * 8 + h:65 + c * 8 + h]
                bg = lambda h: g[:, 80 + h:81 + h]

                import os
                STOP = os.environ.get('P2STOP', 'Z')
                def Q(bank, j):
                    return bank[:, j * 128:(j + 1) * 128]
                GR = [(0, 1, 2, 3), (4, 5, 6, 7)]
                for grp in GR:
                    bk, b_bk = wk_r.next()
                    for h in grp:
                        mm(Q(bk, h % 4), kT[:, h, :], ident_bf, True, True, [b_kT, b_identbf], [b_bk])
                    for h in grp:
                        ts("dve", kbg[h][0], Q(bk, h % 4), bg(h), None, ALU.mult, None, [b_bk, b_g], [kbg[h][1]])
                        ts("dve", kdec[h][0], Q(bk, h % 4), ekd(h), None, ALU.mult, None, [b_bk, b_g], [kdec[h][1]])
                    bv, b_bv = wk_r.next()
                    for h in grp:
                        mm(Q(bv, h % 4), vT[:, h, :], ident_bf, True, True, [b_vT, b_identbf], [b_bv])
                    for h in grp:
                        ts("dve", vbt[h][0], Q(bv, h % 4), beta(h), None, ALU.mult, None, [b_bv, b_g], [vbt[h][1]])
                if STOP < 'B':
                    continue
                for grp in GR:
                    for h in grp:
                        ts("dve", dg1[h][0], ident, gcum(h), None, ALU.mult, None, [b_ident, b_g], [dg1[h][1]])
                    bG, b_bG = wk_r.next()
                    for h in grp:
                        mm(Q(bG, h % 4), ones_f, dg1[h][0], True, True, [b_onesf, dg1[h][1]], [b_bG])
                    for h in grp:
                        stt("dve", X1[h][0], Q(bG, h % 4), gcum(h), bigL, ALU.subtract, ALU.max,
                            [b_bG, b_g, b_bigL], [X1[h][1]])
                        act(dIm[h][0], X1[h][0], AF.Exp, [X1[h][1]], [dIm[h][1]], scale=-1.0)
                        if local:
                            stt("dve", X2[h][0], Q(bG, h % 4), gcum(h), negU, ALU.subtract, ALU.min,
                                [b_bG, b_g, b_negU], [X2[h][1]])
                            act(dTm[h][0], X2[h][0], AF.Exp, [X2[h][1]], [dTm[h][1]])
                    if local:
                        for h in grp:
                            ts("dve", dg2[h][0], ident, eg(h), None, ALU.mult, None, [b_ident, b_g], [dg2[h][1]])
                        bE, b_bE = wk_r.next()
                        for h in grp:
                            mm(Q(bE, h % 4), ones_f, dg2[h][0], True, True, [b_onesf, dg2[h][1]], [b_bE])
                        for h in grp:
                            tt("dve", qdec[h][0], qT[:, h, :], Q(bE, h % 4), ALU.mult, [b_qT, b_bE], [qdec[h][1]])
                if STOP < 'C':
                    continue
                for grp in GR:
                    bK, b_bK = wk_r.next()
                    for h in grp:
                        mm(Q(bK, h % 4), kT[:, h, :], kT[:, h, :], True, True, [b_kT], [b_bK])
                    for h in grp:
                        stt("dve", Lt[0][h][0], Q(bK, h % 4), beta(h), dIm[h][0], ALU.mult, ALU.mult,
                            [b_bK, b_g, dIm[h][1]], [Lt[0][h][1]])
                for grp in GR:
                    bM, b_bM = wk_r.next()
                    for h in grp:
                        mm(Q(bM, h % 4), Lt[0][h][0], ident, True, True, [Lt[0][h][1], b_ident], [b_bM])
                    for h in grp:
                        cp("act", Mt[0][h][0], Q(bM, h % 4), [b_bM], [Mt[0][h][1]])
                        tt("dve", Yt[h][0], ident, Q(bM, h % 4), ALU.subtract, [b_ident, b_bM], [Yt[h][1]])
                if STOP < 'D':
                    continue
                cur = 0
                for k in range(5):
                    nxt = 1 - cur
                    for grp in GR:
                        bL, b_bL = wk_r.next()
                        for h in grp:
                            mm(Q(bL, h % 4), Mt[cur][h][0], Lt[cur][h][0], True, True,
                               [Mt[cur][h][1], Lt[cur][h][1]], [b_bL])
                        for h in grp:
                            cp("act", Lt[nxt][h][0], Q(bL, h % 4), [b_bL], [Lt[nxt][h][1]])
                    if k < 4:
                        for grp in GR:
                            bM, b_bM = wk_r.next()
                            for h in grp:
                                mm(Q(bM, h % 4), Lt[cur][h][0], Mt[cur][h][0], True, True,
                                   [Mt[cur][h][1], Lt[cur][h][1]], [b_bM])
                            for h in grp:
                                cp("dve" if h % 2 else "act", Mt[nxt][h][0], Q(bM, h % 4), [b_bM], [Mt[nxt][h][1]])
                    for grp in GR:
                        bY, b_bY = wk_r.next()
                        for h in grp:
                            mm(Q(bY, h % 4), Lt[nxt][h][0], Yt[h][0], True, True, [Lt[nxt][h][1], Yt[h][1]], [b_bY])
                        for h in grp:
                            tt("dve", Yt[h][0], Yt[h][0], Q(bY, h % 4), ALU.add, [Yt[h][1], b_bY], [Yt[h][1]])
                    cur = nxt
                if STOP < 'E':
                    continue
                for grp in GR:
                    bW, b_bW = wk_r.next()
                    for h in grp:
                        mm(Q(bW, h % 4), kbg[h][0], Yt[h][0], True, True, [kbg[h][1], Yt[h][1]], [b_bW])
                    for h in grp:
                        act(negw[h][0], Q(bW, h % 4), AF.Copy, [b_bW], [negw[h][1]], scale=-1.0)
                    if local:
                        bA, b_bA = wk_r.next()
                        for h in grp:
                            mm(Q(bA, h % 4), kT[:, h, :], qT[:, h, :], True, True, [b_kT, b_qT], [b_bA])
                        for h in grp:
                            tt("dve", ATt[h][0], Q(bA, h % 4), dTm[h][0], ALU.mult, [b_bA, dTm[h][1]], [ATt[h][1]])
                if STOP < 'F':
                    continue
                for c in range(2):
                    cs = slice(64 * c, 64 * c + 64)
                    for gi_, grp in enumerate(GR):
                        vb_, b_vb_ = vs_b[gi_]
                        for h in grp:
                            vq = Q(vb_, h % 4)
                            mm(vq[cs, :], Yt[h][0][:, cs], vbt[h][0], True, False, [Yt[h][1], vbt[h][1]], [b_vb_])
                            mm(vq[cs, :], negw[h][0][:, cs], Sbf[h][0], False, True, [negw[h][1], Sbf[h][1]], [b_vb_])
                        for h in grp:
                            cp("act", vnb[h][0][cs, :], Q(vb_, h % 4)[cs, :], [b_vb_], [vnb[h][1]])
                    for gi_, grp in enumerate(GR):
                        vb_, b_vb_ = vs_b[gi_]
                        ob, b_ob = ot_b[gi_]
                        for h in grp:
                            if local:
                                oc = ob[:, (h % 4) * 128 + 64 * c:(h % 4) * 128 + 64 * c + 64]
                                mm(oc, Sbf[h][0], qdec[h][0][:, cs], True, False, [Sbf[h][1], qdec[h][1]], [b_ob])
                                mm(oc, vnb[h][0][cs, :], ATt[h][0][cs, cs], False, True, [vnb[h][1], ATt[h][1]], [b_ob])
                            mm(Q(vb_, h % 4), kdec[h][0][cs, :], vnb[h][0][cs, :], True, True,
                               [kdec[h][1], vnb[h][1]], [b_vb_])
                        for h in grp:
                            stt("dve", S32[h][0], S32[h][0], eglB(c, h), Q(vb_, h % 4), ALU.mult, ALU.add,
                                [S32[h][1], b_g, b_vb_], [S32[h][1]])
                            cp("act", Sbf[h][0], S32[h][0], [S32[h][1]], [Sbf[h][1]])
                if STOP < 'G':
                    continue
                if local:
                    for hb in range(2):
                        ob, b_ob = ot_b[hb]
                        sq, b_sq = sq_r.next()
                        act(sq, ob, AF.Square, [b_ob], [b_sq])
                        pn, b_pn = wk_r.next()
                        mm(pn, inv128_bf, sq, True, True, [b_sq, b_inv128], [b_pn])
                        rv, b_rv = rv_r.next()
                        act(rv, pn, AF.Sqrt, [b_pn], [b_rv], bias=EPS, scale=1.0)
                        P.op("dve", "reciprocal", [b_rv], [b_rv], out=rv, in_=rv)
                        y32, b_y32 = y32_r.next()
                        stt("dve", y32, ob, onw_c[:, 0:1], rv, ALU.mult, ALU.mult, [b_ob, b_onw, b_rv], [b_y32])
                        y16, b_y16 = y16_r.next()
                        zsl = zT[:, hb * 4:(hb + 1) * 4, :]
                        tt("dve", y16.rearrange("p (h n) -> p h n", h=4), y32.rearrange("p (h n) -> p h n", h=4), zsl,
                           ALU.mult, [b_y32, b_zT], [b_y16])
                        P.dma("pool", s_ya[hb * 4:(hb + 1) * 4, :, tl0:tl0 + 128].rearrange("h p t -> p h t"),
                              y16.rearrange("p (h n) -> p h n", h=4), reads=[b_y16])
            P.barrier()
            A.release(m0)

        if 2 in phases:
            phase2()

        def phase3():
            m0 = A.mark()
            NQT = T_LOC // 512
            scale = float(DH ** -0.5)
            ebk_t = A.alloc(LF, F32, 33)
            b_ebk = Buf()
            P.dma("sp", ebk_t, c_ebk, writes=[b_ebk])
            esel_t = A.alloc(32 * 128, BF16, 32)
            b_esel = Buf()
            P.dma("sp", esel_t, c_esel.rearrange("r n k -> r (n k)"), writes=[b_esel])
            esel3 = esel_t.rearrange("r (n k) -> r n k", k=128)
            relt = A.alloc(H, F32, 32)
            b_relt = Buf()
            P.dma("sp", relt, rel_bias, writes=[b_relt])
            relrep = A.alloc(128, F32, 33)
            b_relrep = Buf()
            memset("dve", relrep, NEG, [b_relrep])
            Fb = A.alloc(LF, F32)
            b_Fb = Buf()
            GT_r = Ring([A.alloc(GW, F32) for _ in range(2)])
            kT_r = Ring([A.alloc(TT, BF16) for _ in range(2)])
            v_r = Ring([A.alloc(NT * 128, BF16) for _ in range(2)])
            q_r = Ring([A.alloc(T_LOC, BF16) for _ in range(2)])
            ms_r = Ring([A.alloc(T_LOC, BF16, 32) for _ in range(2)])
            tmp_r = Ring([A.alloc(512, F32) for _ in range(3)])
            pt_r = Ring([A.alloc(512, BF16) for _ in range(4)])
            rd_r = Ring([A.alloc(512, F32) for _ in range(2)])
            yo_r = Ring([A.alloc(512, BF16) for _ in range(2)])
            cf_r = Ring([A.alloc(1, F32) for _ in range(2)])
            s_ring = Ring([psum[0], psum[1], psum[2], psum[3]], excl=True)
            o_ps, b_o = psum[4], Buf(excl=True)
            d_ps, b_d = psum[5], Buf(excl=True)
            f_ring = Ring([psum[6], psum[7]], excl=True)
            b_frep = Buf()
            for h in range(H):
                cp("dve", relrep[0:32, :], relt[:, h:h + 1].to_broadcast([32, 128]), [b_relt], [b_relrep])
                for c0 in range(0, LF, 512):
                    cn = min(512, LF - c0)
                    pf, b_pf = f_ring.next()
                    mm(pf[:, 0:cn], relrep, ebk_t[:, c0:c0 + cn], True, True, [b_relrep, b_ebk], [b_pf])
                    cp("act", Fb[:, c0:c0 + cn], pf[:, 0:cn], [b_pf], [b_Fb])
                cf, b_cf = cf_r.next()
                cp("dve", cf, Fb[:, LF - 1:LF], [b_Fb], [b_cf])
                P.dma("sp", s_frep[h], Fb, reads=[b_Fb], writes=[b_frep])
                GT, b_GT = GT_r.next()
                src = bass.AP(tensor=s_frep.tensor, offset=s_frep.offset + h * 128 * LF + 127,
                              ap=[[LF - 1, 128], [1, GW]])
                P.dma("sp", GT, src, reads=[b_frep], writes=[b_GT])
                kTh, b_kTh = kT_r.next()
                P.dma("sp", kTh, s_kb[h], writes=[b_kTh])
                vt, b_vt = v_r.next()
                vt3 = vt.rearrange("p (n d) -> p n d", d=128)
                for n0 in range(0, NT, 16):
                    nn = min(16, NT - n0)
                    P.dma("sp", vt3[:, n0:n0 + nn, :],
                          s_vb[n0 * 128:(n0 + nn) * 128, h * 128:(h + 1) * 128].rearrange("(n p) d -> p n d", p=128),
                          writes=[b_vt])
                qh, b_qh = q_r.next()
                P.dma("sp", qh, s_qb[h], writes=[b_qh])
                msh, b_msh = ms_r.next()
                P.dma("sp", msh, s_msel[h], writes=[b_msh])
                for j in range(NQT):
                    q0 = T_PRE + 512 * j
                    ql = 512 * j
                    nsub = (q0 + 512) // 128
                    LA = 2
                    pts = {}

                    def front(ks):
                        k0 = ks * 128
                        n = k0 // BS
                        delta = q0 - k0
                        sp_, b_sp = s_ring.next()
                        mm(sp_, kTh[:, k0:k0 + 128], qh[:, ql:ql + 512], True, False, [b_kTh, b_qh], [b_sp])
                        mm(sp_, esel3[:, n, :], msh[:, ql:ql + 512], False, True, [b_esel, b_msh], [b_sp])
                        pt, b_pt = pt_r.next()
                        if delta >= FAR_DELTA:
                            act(pt, sp_, AF.Exp, [b_sp, b_cf], [b_pt], bias=cf[:, 0:1], scale=scale)
                        else:
                            tmp, b_tmp = tmp_r.next()
                            x0 = delta + 384
                            stt("dve", tmp, sp_, scale, GT[:, x0:x0 + 512], ALU.mult, ALU.add, [b_sp, b_GT], [b_tmp])
                            act(pt, tmp, AF.Exp, [b_tmp], [b_pt])
                        pts[ks] = (pt, b_pt)
                    for ks in range(min(LA, nsub)):
                        front(ks)
                    for ks in range(nsub):
                        if ks + LA < nsub:
                            front(ks + LA)
                        pt, b_pt = pts.pop(ks)
                        mm(o_ps, vt3[:, ks, :], pt, ks == 0, ks == nsub - 1, [b_vt, b_pt], [b_o])
                        mm(d_ps, ones_bf, pt, ks == 0, ks == nsub - 1, [b_ones, b_pt], [b_d])
                    rd, b_rd = rd_r.next()
                    P.op("dve", "reciprocal", [b_d], [b_rd], out=rd, in_=d_ps)
                    yo, b_yo = yo_r.next()
                    tt("dve", yo, o_ps, rd, ALU.mult, [b_o, b_rd], [b_yo])
                    P.dma("pool", s_yb[h, :, ql:ql + 512], yo, reads=[b_yo])
            P.barrier()
            A.release(m0)

        if 3 in phases:
            phase3()

        def phase4():
            NLB = T_LOC // 512
            cast_w(w_bg, wb_bg, D, 512, "bg")
            cast_w(w_bm, wb_bm, D, 512, "bm")
            cast_w(w_out, wb_out, D, 512, "out")
            m0 = A.mark()
            Wg_t = A.alloc(8 * D, BF16)
            Wm_t = A.alloc(8 * D, BF16)
            Wg3 = Wg_t.rearrange("p (c n) -> p c n", c=8)
            Wm3 = Wm_t.rearrange("p (c n) -> p c n", c=8)
            b_Wg, b_Wm = Buf(), Buf()
            for c0 in range(0, D, 512):
                P.dma("sp", Wg3[:, :, c0:c0 + 512], wb_bg.rearrange("(c p) n -> p c n", p=128)[:, :, c0:c0 + 512],
                      reads=[bufs_w[("bg", c0)]], writes=[b_Wg])
                P.dma("sp", Wm3[:, :, c0:c0 + 512], wb_bm.rearrange("(c p) n -> p c n", p=128)[:, :, c0:c0 + 512],
                      reads=[bufs_w[("bm", c0)]], writes=[b_Wm])
            ya_r = Ring([A.alloc(8 * 512, BF16) for _ in range(1)])
            yb_r = Ring([A.alloc(8 * 512, BF16) for _ in range(1)])
            sg_r = Ring([A.alloc(512, BF16) for _ in range(4)])
            t_r = Ring([A.alloc(512, F32) for _ in range(4)])
            mix_r = Ring([A.alloc(DC * 512, BF16) for _ in range(1)])
            wo_r = Ring([A.alloc(DC * 512, BF16) for _ in range(2)])
            xr_r = Ring([A.alloc(512, F32) for _ in range(3)])
            ho_r = Ring([A.alloc(512, F32) for _ in range(3)])
            pab = Ring([psum[i] for i in range(4)], excl=True)
            po = Ring([psum[i] for i in range(4, 8)], excl=True)
            wb_out3 = wb_out.rearrange("(c p) n -> p c n", p=128)
            for tb in range(NLB):
                ql = tb * 512
                ya, b_ya = ya_r.next()
                ya3 = ya.rearrange("p (h n) -> p h n", h=8)
                P.dma("sp", ya3, s_ya[:, :, ql:ql + 512].rearrange("h p t -> p h t"), writes=[b_ya])
                yb, b_yb = yb_r.next()
                yb3 = yb.rearrange("p (h n) -> p h n", h=8)
                P.dma("sp", yb3, s_yb[:, :, ql:ql + 512].rearrange("h p t -> p h t"), writes=[b_yb])
                mix, b_mix = mix_r.next()
                mix3 = mix.rearrange("p (c n) -> p c n", c=DC)
                for dc in range(DC):
                    sga, b_sga = sg_r.next()
                    P.dma("sp", sga, s_sg[0, dc, :, ql:ql + 512], writes=[b_sga])
                    sgb, b_sgb = sg_r.next()
                    P.dma("sp", sgb, s_sg[1, dc, :, ql:ql + 512], writes=[b_sgb])
                    pa, b_pa = pab.next()
                    for kc in range(8):
                        mm(pa, Wg3[:, kc, dc * 128:(dc + 1) * 128], ya3[:, kc, :], kc == 0, kc == 7, [b_Wg, b_ya], [b_pa])
                    pb, b_pb = pab.next()
                    for kc in range(8):
                        mm(pb, Wm3[:, kc, dc * 128:(dc + 1) * 128], yb3[:, kc, :], kc == 0, kc == 7, [b_Wm, b_yb], [b_pb])
                    t1, b_t1 = t_r.next()
                    tt("dve", t1, pa, sga, ALU.mult, [b_pa, b_sga], [b_t1])
                    t2, b_t2 = t_r.next()
                    tt("dve", t2, pb, sgb, ALU.mult, [b_pb, b_sgb], [b_t2])
                    tt("pool", mix3[:, dc, :], t1, t2, ALU.add, [b_t1, b_t2], [b_mix])
                for cg in range(4):
                    wo, b_wo = wo_r.next()
                    wo3 = wo.rearrange("p (c n) -> p c n", c=DC)
                    P.dma("sp", wo3, wb_out3[:, :, cg * 512:(cg + 1) * 512], reads=[bufs_w[("out", cg * 512)]],
                          writes=[b_wo])
                    for s in range(4):
                        r0 = ql + s * 128
                        xr, b_xr = xr_r.next()
                        P.dma("sp", xr, x_in[T_PRE + r0:T_PRE + r0 + 128, cg * 512:(cg + 1) * 512], writes=[b_xr])
                        pq, b_pq = po.next()
                        for kc in range(DC):
                            mm(pq, mix3[:, kc, s * 128:(s + 1) * 128], wo3[:, kc, :], kc == 0, kc == DC - 1,
                               [b_mix, b_wo], [b_pq])
                        ho, b_ho = ho_r.next()
                        tt("dve", ho, pq, xr, ALU.add, [b_pq, b_xr], [b_ho])
                        P.dma("pool", s_h[r0:r0 + 128, cg * 512:(cg + 1) * 512], ho, reads=[b_ho], writes=[b_sh])
            P.barrier()
            A.release(m0)

        b_sh = Buf()
        if 4 in phases:
            phase4()

        def phase5():
            NLB = T_LOC // 512
            cast_w(w_fg, wb_fg, DFF, 512, "fg")
            cast_w(w_fu, wb_fu, DFF, 512, "fu")
            cast_w(w_fd, wb_fd, D, 512, "fd")
            m0 = A.mark()
            nfw, b_nfw = bcast_load(norm_ffn_w, D)
            xt_r = Ring([A.alloc(D, F32) for _ in range(2)])
            xn_r = Ring([A.alloc(D, BF16) for _ in range(2)])
            uT_r = Ring([A.alloc(DC * 512, BF16) for _ in range(1)])
            hid_r = Ring([A.alloc(FC * 512, BF16) for _ in range(1)])
            wg_r = Ring([A.alloc(DC * 256, BF16) for _ in range(2)])
            wu_r = Ring([A.alloc(DC * 256, BF16) for _ in range(2)])
            wd_r = Ring([A.alloc(22 * 512, BF16) for _ in range(2)])
            sl_r = Ring([A.alloc(512, F32) for _ in range(2)])
            hr_r = Ring([A.alloc(512, F32) for _ in range(2)])
            yo_r = Ring([A.alloc(512, F32) for _ in range(2)])
            small_r = Ring([A.alloc(8, F32) for _ in range(4)])
            fr = Ring([psum[i] for i in range(4)], excl=True)
            dn = [(psum[4 + i], Buf(excl=True)) for i in range(4)]
            wb_fg3 = wb_fg.rearrange("(c p) n -> p c n", p=128)
            wb_fu3 = wb_fu.rearrange("(c p) n -> p c n", p=128)
            wb_fd3 = wb_fd.rearrange("(c p) n -> p c n", p=128)
            out_toks = []
            for tb in range(NLB):
                ql = tb * 512
                uT, b_uT = uT_r.next()
                uT3 = uT.rearrange("p (c n) -> p c n", c=DC)
                for s in range(4):
                    xt, b_xt = xt_r.next()
                    r0 = ql + s * 128
                    P.dma("sp", xt, s_h[r0:r0 + 128, :], reads=[b_sh], writes=[b_xt])
                    xn, b_xn = xn_r.next()
                    sm, b_sm = small_r.next()
                    act(xn, xt, AF.Square, [b_xt], [b_xn, b_sm], accum_out=sm[:, 0:1])
                    act(sm[:, 1:2], sm[:, 0:1], AF.Sqrt, [b_sm], [b_sm], bias=EPS, scale=1.0 / D)
                    P.op("dve", "reciprocal", [b_sm], [b_sm], out=sm[:, 2:3], in_=sm[:, 1:2])
                    stt("dve", xn, xt, sm[:, 2:3], nfw, ALU.mult, ALU.mult, [b_xt, b_sm, b_nfw], [b_xn])
                    for q4 in range(4):
                        ps, b_ps = fr.next()
                        for cc in range(4):
                            c = q4 * 4 + cc
                            mm(ps[:, cc * 128:(cc + 1) * 128], xn[:, c * 128:(c + 1) * 128], ident_bf, True, True,
                               [b_xn, b_identbf], [b_ps])
                        dst = uT3[:, q4 * 4:(q4 + 1) * 4, s * 128:(s + 1) * 128]
                        src = ps.rearrange("p (c n) -> p c n", c=4)
                        cp("act" if q4 % 2 == 0 else "dve", dst, src, [b_ps], [b_uT])
                hid, b_hid = hid_r.next()
                hid3 = hid.rearrange("p (c n) -> p c n", c=FC)
                for cg2 in range(FC // 2):
                    c0 = cg2 * 256
                    wg, b_wg = wg_r.next()
                    wg3 = wg.rearrange("p (c n) -> p c n", c=DC)
                    P.dma("sp", wg3, wb_fg3[:, :, c0:c0 + 256], reads=[bufs_w[("fg", (c0 // 512) * 512)]], writes=[b_wg])
                    wu, b_wu = wu_r.next()
                    wu3 = wu.rearrange("p (c n) -> p c n", c=DC)
                    P.dma("sp", wu3, wb_fu3[:, :, c0:c0 + 256], reads=[bufs_w[("fu", (c0 // 512) * 512)]], writes=[b_wu])
                    for g in range(2):
                        fc = cg2 * 2 + g
                        pg_, b_pg_ = fr.next()
                        for c in range(DC):
                            mm(pg_, wg3[:, c, g * 128:(g + 1) * 128], uT3[:, c, :], c == 0, c == DC - 1,
                               [b_wg, b_uT], [b_pg_])
                        pu_, b_pu_ = fr.next()
                        for c in range(DC):
                            mm(pu_, wu3[:, c, g * 128:(g + 1) * 128], uT3[:, c, :], c == 0, c == DC - 1,
                               [b_wu, b_uT], [b_pu_])
                        sl, b_sl = sl_r.next()
                        act(sl, pg_, AF.Silu, [b_pg_], [b_sl])
                        tt("dve", hid3[:, fc, :], sl, pu_, ALU.mult, [b_sl, b_pu_], [b_hid])
                for cg in range(4):
                    for hf in range(2):
                        wd, b_wd = wd_r.next()
                        wd3 = wd.rearrange("p (c n) -> p c n", c=22)
                        P.dma("sp", wd3, wb_fd3[:, hf * 22:(hf + 1) * 22, cg * 512:(cg + 1) * 512],
                              reads=[bufs_w[("fd", cg * 512)]], writes=[b_wd])
                        for s in range(4):
                            pd, b_pd = dn[s]
                            for f in range(22):
                                fc = hf * 22 + f
                                mm(pd, hid3[:, fc, s * 128:(s + 1) * 128], wd3[:, f, :], fc == 0, fc == FC - 1,
                                   [b_hid, b_wd], [b_pd])
                    for s in range(4):
                        pd, b_pd = dn[s]
                        r0 = ql + s * 128
                        hr, b_hr = hr_r.next()
                        P.dma("sp", hr, s_h[r0:r0 + 128, cg * 512:(cg + 1) * 512], reads=[b_sh], writes=[b_hr])
                        yo, b_yo = yo_r.next()
                        tt("dve", yo, pd, hr, ALU.add, [b_pd, b_hr], [b_yo])
                        P.dma("pool", y_out[r0:r0 + 128, cg * 512:(cg + 1) * 512], yo, reads=[b_yo])
            P.barrier()
            A.release(m0)

        if 5 in phases:
            phase5()

        P.barrier()
        P.emit()
    return nc, P


_CACHE = {}


def make_in_maps(inputs, B, T, T_PRE, T_LOC):
    consts = host_consts(T_PRE, T_LOC)
    x = np.asarray(inputs["x"], dtype=np.float32)
    in_maps = []
    nhalf = T // T_LOC
    for b in range(B):
        for half in range(nhalf):
            xc = np.zeros((T_PRE + T_LOC, D), np.float32)
            if half > 0:
                xc[:T_PRE] = x[b, half * T_LOC - T_PRE:half * T_LOC]
            xc[T_PRE:] = x[b, half * T_LOC:(half + 1) * T_LOC]
            m = {"x": xc}
            for k in ("norm_mix_w", "conv_w", "a_log", "dt_bias", "gdn_o_norm_w", "q_norm_w", "k_norm_w",
                      "w_in", "w_branch_gdn", "w_branch_moba", "w_out", "norm_ffn_w", "w_ffn_gate",
                      "w_ffn_up", "w_ffn_down"):
                a = np.asarray(inputs[k], dtype=np.float32)
                m[k] = np.ascontiguousarray(a[0]) if a.shape[0] == 1 and a.ndim == 3 else np.ascontiguousarray(a)
            m["rel_bias"] = np.ascontiguousarray(np.asarray(inputs["rel_bias"], dtype=np.float32))
            for k, v in consts.items():
                m["c_" + k] = v
            m["c_rmask"] = rmask_for(half, T_PRE, T_LOC)
            in_maps.append(m)
    return in_maps


def kernel(**inputs):
    x = np.asarray(inputs["x"])
    B, T, _ = x.shape
    T_LOC = T // 2
    T_PRE = T_LOC
    key = (T_PRE, T_LOC)
    if key not in _CACHE:
        _CACHE[key] = build_program(T_PRE, T_LOC)[0]
    nc = _CACHE[key]
    in_maps = make_in_maps(inputs, B, T, T_PRE, T_LOC)
    res = run_bass_kernel_spmd(nc, in_maps, core_ids=list(range(len(in_maps))))
    out = np.zeros((B, T, D), np.float32)
    i = 0
    for b in range(B):
        for half in range(2):
            out[b, half * T_LOC:(half + 1) * T_LOC] = res.results[i]["y"]
            i += 1
    return out
```

```python
import contextlib
import math
import numpy as np
import ml_dtypes
import concourse.bass as bass
import concourse.mybir as mybir
from concourse.bass_utils import run_bass_kernel_spmd

F32 = mybir.dt.float32
BF16 = mybir.dt.bfloat16
AF = mybir.ActivationFunctionType
ALU = mybir.AluOpType
AX = mybir.AxisListType

D = 2048
DC = 16
H = 8
DH = 128
W = 1024
DFF = 5632
FC = 44
DPROJ = 11280
EPS = 1e-6
NBKT = 32
BS = 256
NEG = -30000.0
LF = 2559
GW = 2432
FAR_DELTA = 1664
C_QA, C_KA, C_VA, C_ZA, C_BETA, C_DEC, C_QB, C_KB, C_VB, C_GA, C_GB = (
    0, 1024, 2048, 3072, 4096, 4104, 4112, 5136, 6160, 7184, 9232)


class Buf:
    __slots__ = ("name", "w", "r", "excl")

    def __init__(self, name="", excl=False):
        self.name = name
        self.w = None
        self.r = []
        self.excl = excl


class Prog:
    ENGS = ("pe", "act", "dve", "pool", "sp")
    NDMA = 8

    def __init__(self, nc):
        self.nc = nc
        self.stream = {e: [] for e in self.ENGS}
        self.count = {e: 0 for e in self.ENGS}
        self.waited = {e: {} for e in self.ENGS}
        self.dma_i = {e: 0 for e in self.ENGS}
        self.n_ins = 0

    def _deps(self, eng, reads, writes):
        need = {}
        for b in reads:
            if b.w is not None:
                need[b.w[0]] = max(need.get(b.w[0], 0), b.w[1])
            if b.excl:
                for t in b.r:
                    if t[0] != eng:
                        need[t[0]] = max(need.get(t[0], 0), t[1])
        for b in writes:
            if b.w is not None:
                need[b.w[0]] = max(need.get(b.w[0], 0), b.w[1])
            for t in b.r:
                need[t[0]] = max(need.get(t[0], 0), t[1])
        waits = []
        wd = self.waited[eng]
        for key, val in need.items():
            if key == eng and eng == "pe":
                continue
            if wd.get(key, 0) >= val:
                continue
            wd[key] = val
            waits.append((key, val))
        return waits

    def _mark(self, tok, reads, writes):
        for b in writes:
            b.w = tok
            b.r = []
        for b in reads:
            if b not in writes:
                b.r.append(tok)
        self.n_ins += 1

    def op(self, eng, name, reads=(), writes=(), **kw):
        waits = self._deps(eng, reads, writes)
        self.count[eng] += 1
        tok = (eng, self.count[eng])
        self.stream[eng].append(("op", (name, kw), waits))
        self._mark(tok, reads, writes)
        return tok

    def dma(self, eng, out, in_, reads=(), writes=()):
        fn = ("dma_start", dict(out=out, in_=in_))
        waits = self._deps(eng, reads, writes)
        i = self.dma_i[eng]
        self.dma_i[eng] += 1
        key = ("dma", eng, i % self.NDMA)
        val = 16 * (i // self.NDMA + 1)
        prev = val - 16
        if prev > 0 and self.waited[eng].get(key, 0) < prev:
            waits.append((key, prev))
            self.waited[eng][key] = prev
        tok = (key, val)
        self.stream[eng].append(("dma", fn, waits, key))
        self._mark(tok, reads, writes)
        return tok

    def barrier(self):
        toks = [(e, self.count[e]) for e in self.ENGS if self.count[e] > 0]
        for e in self.ENGS:
            for s in range(self.NDMA):
                i = self.dma_i[e]
                n = (i - s + self.NDMA - 1) // self.NDMA if i > s else 0
                if n > 0:
                    toks.append((("dma", e, s), 16 * n))
        for e in self.ENGS:
            waits = []
            for key, val in toks:
                if key == e and e == "pe":
                    continue
                if self.waited[e].get(key, 0) >= val:
                    continue
                self.waited[e][key] = val
                waits.append((key, val))
            if waits:
                self.stream[e].append(("wait", None, waits))

    def emit(self):
        nc = self.nc
        with contextlib.ExitStack() as es:
            sems = {}
            for e in self.ENGS:
                sems[e] = es.enter_context(nc.semaphore("p_" + e))
                for s in range(self.NDMA):
                    sems[("dma", e, s)] = es.enter_context(nc.semaphore("d_%s_%d" % (e, s)))
            es.enter_context(nc.allow_non_contiguous_dma(reason="tiny strided constant loads"))
            block = es.enter_context(nc.Block())

            def run(e, handle):
                for item in self.stream[e]:
                    for key, val in item[2]:
                        handle.wait_ge(sems[key], val)
                    if item[0] == "op":
                        getattr(handle, item[1][0])(**item[1][1]).then_inc(sems[e], 1)
                    elif item[0] == "dma":
                        getattr(handle, item[1][0])(**item[1][1]).then_inc(sems[item[3]], 16)

            @block.tensor
            def _(h):
                run("pe", h)

            @block.scalar
            def _(h):
                run("act", h)

            @block.vector
            def _(h):
                run("dve", h)

            @block.gpsimd
            def _(h):
                run("pool", h)

            @block.sync
            def _(h):
                run("sp", h)


class Ring:
    def __init__(self, aps, excl=False):
        self.items = [(ap, Buf(excl=excl)) for ap in aps]
        self.i = 0

    def next(self):
        it = self.items[self.i % len(self.items)]
        self.i += 1
        return it


class Arena:
    def __init__(self, t, width):
        self.t = t
        self.width = width
        self.off = 0

    def alloc(self, n, dtype=F32, parts=128):
        n32 = n if dtype == F32 else (n + 1) // 2
        assert self.off + n32 <= self.width, ("arena overflow", self.off, n32, self.width)
        ap = self.t[0:parts, self.off:self.off + n32]
        self.off += n32
        if dtype != F32:
            ap = ap.bitcast(dtype)[:, 0:n]
        return ap

    def mark(self):
        return self.off

    def release(self, m):
        self.off = m


def t5_bucket_np(dist):
    max_exact = NBKT // 2
    d = dist.astype(np.float32)
    log_ratio = np.log(np.maximum(d, np.float32(max_exact)) / np.float32(max_exact)) / np.float32(
        math.log(2048 / max_exact))
    large = max_exact + (log_ratio.astype(np.float32) * np.float32(NBKT - max_exact)).astype(np.int32)
    large = np.minimum(large, NBKT - 1)
    return np.where(dist < max_exact, dist, large)


def host_consts(T_PRE, T_LOC):
    c = {}
    idx = np.arange(128)
    same = (idx[:, None] // 64) == (idx[None, :] // 64)
    c["ident"] = np.eye(128, dtype=np.float32)
    c["U"] = (same & (idx[:, None] <= idx[None, :])).astype(np.float32)
    c["cm0"] = np.tile((idx[:, None] < 64), (1, 128)).astype(np.float32)
    c["cm1"] = np.tile((idx[:, None] >= 64), (1, 128)).astype(np.float32)
    c["bd"] = same.astype(np.float32)
    c["bigL"] = np.where(same & (idx[:, None] > idx[None, :]), 0.0, -NEG).astype(np.float32)
    c["negU"] = np.where(same & (idx[None, :] >= idx[:, None]), 0.0, NEG).astype(np.float32)
    i = np.arange(LF)
    dist = i - 511
    bk = t5_bucket_np(np.maximum(dist, 0))
    E = np.zeros((33, LF), np.float32)
    E[bk, i] = 1.0
    E[:, dist < 0] = 0.0
    E[32, dist < 0] = 1.0
    c["ebk"] = E
    es = np.zeros((32, 32, 128), np.float32)
    for n in range(32):
        es[n, n, :] = 1.0
    c["esel"] = es.astype(ml_dtypes.bfloat16)
    nqb = T_LOC // BS
    npb = T_PRE // BS
    own = np.zeros((nqb, 32), np.float32)
    for qb in range(nqb):
        own[qb, npb + qb] = 1.0
    c["own"] = own
    return c


def rmask_for(half, T_PRE, T_LOC):
    nqb = T_LOC // BS
    npb = T_PRE // BS
    m = np.zeros((nqb, 32), np.float32)
    for qb in range(nqb):
        m[qb, npb + qb:] = -1e30
        if half == 0:
            m[qb, :npb] = -1e30
    return m


def build_program(T_PRE, T_LOC, debug=False, phases=(1, 2, 3, 4, 5)):
    TT = T_PRE + T_LOC
    TB = 512
    NBLK = TT // TB
    NPREB = T_PRE // TB
    NT = TT // 128
    nc = bass.Bass("TRN2", target_bir_lowering=False)
    P = Prog(nc)
    skind = "ExternalOutput" if debug else "Internal"

    def din(name, shape, dt=F32):
        return nc.dram_tensor(name, list(shape), dt, kind="ExternalInput").ap()

    def dscr(name, shape, dt):
        return nc.dram_tensor(name, list(shape), dt, kind=skind).ap()

    x_in = din("x", [TT, D])
    w_in = din("w_in", [D, DPROJ])
    norm_mix_w = din("norm_mix_w", [1, D])
    conv_w = din("conv_w", [4, 3 * W])
    a_log = din("a_log", [1, H])
    dt_bias = din("dt_bias", [1, H])
    gdn_o_norm_w = din("gdn_o_norm_w", [1, DH])
    q_norm_w = din("q_norm_w", [1, DH])
    k_norm_w = din("k_norm_w", [1, DH])
    rel_bias = din("rel_bias", [NBKT, H])
    w_bg = din("w_branch_gdn", [W, D])
    w_bm = din("w_branch_moba", [W, D])
    w_out = din("w_out", [D, D])
    norm_ffn_w = din("norm_ffn_w", [1, D])
    w_fg = din("w_ffn_gate", [D, DFF])
    w_fu = din("w_ffn_up", [D, DFF])
    w_fd = din("w_ffn_down", [DFF, D])
    c_ident = din("c_ident", [128, 128])
    c_U = din("c_U", [128, 128])
    c_cm0 = din("c_cm0", [128, 128])
    c_cm1 = din("c_cm1", [128, 128])
    c_bd = din("c_bd", [128, 128])
    c_bigL = din("c_bigL", [128, 128])
    c_negU = din("c_negU", [128, 128])
    c_ebk = din("c_ebk", [33, LF])
    c_esel = din("c_esel", [32, 32, 128], BF16)
    c_own = din("c_own", [T_LOC // BS, 32])
    c_rmask = din("c_rmask", [T_LOC // BS, 32])
    y_out = nc.dram_tensor("y", [T_LOC, D], F32, kind="ExternalOutput").ap()

    wb_in = dscr("wb_in", [D, DPROJ], BF16)
    wb_bg = dscr("wb_bg", [W, D], BF16)
    wb_bm = dscr("wb_bm", [W, D], BF16)
    wb_out = dscr("wb_out", [D, D], BF16)
    wb_fg = dscr("wb_fg", [D, DFF], BF16)
    wb_fu = dscr("wb_fu", [D, DFF], BF16)
    wb_fd = dscr("wb_fd", [DFF, D], BF16)
    s_qa = dscr("s_qa", [H, 128, TT], BF16)
    s_ka = dscr("s_ka", [H, 128, TT], BF16)
    s_va = dscr("s_va", [H, 128, TT], BF16)
    s_za = dscr("s_za", [H, 128, T_LOC], BF16)
    s_bd = dscr("s_bd", [TT, 16], F32)
    s_qb = dscr("s_qb", [H, 128, T_LOC], BF16)
    s_kb = dscr("s_kb", [H, 128, TT], BF16)
    s_vb = dscr("s_vb", [TT, W], BF16)
    s_msel = dscr("s_msel", [H, 32, T_LOC], BF16)
    s_sg = dscr("s_sg", [2, DC, 128, T_LOC], BF16)
    s_ya = dscr("s_ya", [H, 128, T_LOC], BF16)
    s_yb = dscr("s_yb", [H, 128, T_LOC], BF16)
    s_frep = dscr("s_frep", [H, 128, LF], F32)
    s_h = dscr("s_h", [T_LOC, D], F32)

    wb_tok = {}
    bufs_w = {}

    with contextlib.ExitStack() as es:
        AW = 51200
        arena_t = es.enter_context(nc.sbuf_tensor("arena", [128, AW], F32))
        A = Arena(arena_t, AW)
        psum = []
        for i in range(8):
            pt = es.enter_context(nc.psum_tensor("ps%d" % i, [128, 512], F32))
            psum.append(pt[:, :])

        def mm(out, lhsT, rhs, start, stop, reads, writes):
            P.op("pe", "matmul", reads, writes, out=out, lhsT=lhsT, rhs=rhs, start=start, stop=stop)

        def act(out, in_, func, reads, writes, **kw):
            P.op("act", "activation", reads, writes, out=out, in_=in_, func=func, **kw)

        def ts(eng, out, in0, s1, s2, op0, op1, reads, writes):
            if op1 is None:
                P.op(eng, "tensor_scalar", reads, writes, out=out, in0=in0, scalar1=s1, scalar2=None, op0=op0)
            else:
                P.op(eng, "tensor_scalar", reads, writes, out=out, in0=in0, scalar1=s1, scalar2=s2, op0=op0, op1=op1)

        def stt(eng, out, in0, scalar, in1, op0, op1, reads, writes):
            P.op(eng, "scalar_tensor_tensor", reads, writes, out=out, in0=in0, scalar=scalar, in1=in1, op0=op0, op1=op1)

        def tt(eng, out, in0, in1, op, reads, writes):
            P.op(eng, "tensor_tensor", reads, writes, out=out, in0=in0, in1=in1, op=op)

        def cp(eng, out, in_, reads, writes):
            if eng == "act":
                act(out, in_, AF.Copy, reads, writes)
            else:
                P.op(eng, "tensor_copy", reads, writes, out=out, in_=in_)

        def memset(eng, ap, val, writes):
            P.op(eng, "memset", (), writes, ap=ap, constant=val)

        def load_const(ap_dram, n, dtype=F32, parts=128):
            t = A.alloc(n, dtype, parts)
            b = Buf()
            P.dma("sp", t, ap_dram, writes=[b])
            return t, b

        ident, b_ident = load_const(c_ident, 128)
        Umat, b_U = load_const(c_U, 128)
        cm0, b_cm0 = load_const(c_cm0, 128)
        cm1, b_cm1 = load_const(c_cm1, 128)
        bdm, b_bdm = load_const(c_bd, 128)
        bigL, b_bigL = load_const(c_bigL, 128)
        negU, b_negU = load_const(c_negU, 128)
        ident_bf = A.alloc(128, BF16)
        b_identbf = Buf()
        cp("dve", ident_bf, ident, [b_ident], [b_identbf])
        ones_bf = A.alloc(128, BF16)
        b_ones = Buf()
        memset("dve", ones_bf, 1.0, [b_ones])
        ones_f = A.alloc(128, F32)
        b_onesf = Buf()
        memset("dve", ones_f, 1.0, [b_onesf])
        inv128_bf = A.alloc(128, BF16)
        b_inv128 = Buf()
        memset("dve", inv128_bf, 1.0 / 128.0, [b_inv128])

        def bcast_load(ap_row, n):
            t = A.alloc(n, F32)
            b = Buf()
            src = bass.AP(tensor=ap_row.tensor, offset=ap_row.offset, ap=[[0, 128], [1, n]])
            P.dma("sp", t, src, writes=[b])
            return t, b

        nmw, b_nmw = bcast_load(norm_mix_w, D)
        alog_b, b_alog = bcast_load(a_log, H)
        dtb_b, b_dtb = bcast_load(dt_bias, H)

        def col_load(ap_row, n=128):
            t = A.alloc(1, F32)
            b = Buf()
            src = bass.AP(tensor=ap_row.tensor, offset=ap_row.offset, ap=[[1, n], [1, 1]])
            P.dma("sp", t, src, writes=[b])
            return t, b
        onw_c, b_onw = col_load(gdn_o_norm_w)
        qnw_c, b_qnw = col_load(q_norm_w)
        knw_c, b_knw = col_load(k_norm_w)
        cw = A.alloc(24 * 4, F32)
        cw3 = cw.rearrange("p (g j) -> p g j", j=4)
        b_cw = Buf()
        for j in range(4):
            src = bass.AP(tensor=conv_w.tensor, offset=conv_w.offset + j * 3 * W, ap=[[1, 128], [128, 24], [1, 1]])
            P.dma("sp", cw3[:, :, j:j + 1], src, writes=[b_cw])
        nA = A.alloc(H, F32)
        b_nA = Buf()
        act(nA, alog_b, AF.Exp, [b_alog], [b_nA])
        ts("dve", nA, nA, -1.0, None, ALU.mult, None, [b_nA], [b_nA])
        hist = A.alloc(24 * 3, F32)
        hist3 = hist.rearrange("p (g j) -> p g j", j=3)
        b_hist = [Buf() for _ in range(24)]
        memset("dve", hist, 0.0, b_hist)
        kmean = A.alloc(H * 32, F32)
        km3 = kmean.rearrange("p (h n) -> p h n", n=32)
        b_kmean = Buf()
        memset("dve", kmean, 0.0, [b_kmean])

        def cast_w(src, dst, cols, cstep, name):
            for c0 in range(0, cols, cstep):
                cn = min(cstep, cols - c0)
                b = Buf()
                P.dma("pool", dst[:, c0:c0 + cn], src[:, c0:c0 + cn], writes=[b])
                bufs_w[(name, c0)] = b

        def store(dst, src, b_src):
            P.dma("pool", dst, src, reads=[b_src])

        def phase1():
            cast_w(w_in, wb_in, DPROJ, 512, "in")
            m0 = A.mark()
            xt_r = Ring([A.alloc(D, F32) for _ in range(2)])
            xn_r = Ring([A.alloc(D, BF16) for _ in range(2)])
            uT_r = Ring([A.alloc(DC * TB, BF16) for _ in range(2)])
            wt_r = Ring([A.alloc(DC * 512, BF16) for _ in range(3)])
            cb_r = Ring([A.alloc(TB + 3, F32) for _ in range(2)])
            acc_r = Ring([A.alloc(TB, F32) for _ in range(2)])
            sil_r = Ring([A.alloc(TB, F32) for _ in range(3)])
            sq_r = Ring([A.alloc(TB, BF16) for _ in range(3)])
            rinv_r = Ring([A.alloc(TB, F32) for _ in range(2)])
            o16_r = Ring([A.alloc(TB, BF16) for _ in range(10)])
            o32_r = Ring([A.alloc(TB, F32) for _ in range(6)])
            small_r = Ring([A.alloc(8, F32) for _ in range(4)])
            rt_r = Ring([A.alloc(4 * 32, F32) for _ in range(2)])
            mx_r = Ring([A.alloc(4 * 8, F32) for _ in range(2)])
            sel_r = Ring([A.alloc(4 * 32, F32) for _ in range(3)])
            ms_r = Ring([A.alloc(TB, BF16, 32) for _ in range(5)])
            rmk_r = Ring([A.alloc(64, F32) for _ in range(2)])
            own_r = Ring([A.alloc(64, F32) for _ in range(2)])
            ps_r = Ring([psum[i] for i in range(6)], excl=True)
            ps_aux = Ring([psum[6], psum[7]], excl=True)
            wb_in3 = wb_in.rearrange("(c p) n -> p c n", p=128)

            def norm_sq(src, b_src):
                sq, b_sq = sq_r.next()
                act(sq, src, AF.Square, [b_src], [b_sq])
                return sq, b_sq

            def norm_rinv(sq, b_sq, kind):
                pa, b_pa = ps_aux.next()
                lw = ones_bf if kind == "l2" else inv128_bf
                mm(pa, lw, sq, True, True, [b_sq, b_ones, b_inv128], [b_pa])
                rinv, b_rinv = rinv_r.next()
                act(rinv, pa, AF.Ln, [b_pa], [b_rinv], bias=EPS, scale=1.0)
                act(rinv, rinv, AF.Exp, [b_rinv], [b_rinv], scale=-0.5)
                return rinv, b_rinv

            for bi in range(NBLK):
                local = bi >= NPREB
                t0 = bi * TB
                tl0 = t0 - T_PRE
                uT, b_uT = uT_r.next()
                uT3 = uT.rearrange("p (c n) -> p c n", c=DC)
                for s in range(4):
                    xt, b_xt = xt_r.next()
                    r0 = t0 + s * 128
                    P.dma("sp", xt, x_in[r0:r0 + 128, :], writes=[b_xt])
                    xn, b_xn = xn_r.next()
                    sm, b_sm = small_r.next()
                    act(xn, xt, AF.Square, [b_xt], [b_xn, b_sm], accum_out=sm[:, 0:1])
                    act(sm[:, 1:2], sm[:, 0:1], AF.Sqrt, [b_sm], [b_sm], bias=EPS, scale=1.0 / D)
                    P.op("dve", "reciprocal", [b_sm], [b_sm], out=sm[:, 2:3], in_=sm[:, 1:2])
                    stt("dve", xn, xt, sm[:, 2:3], nmw, ALU.mult, ALU.mult, [b_xt, b_sm, b_nmw], [b_xn])
                    for q4 in range(4):
                        ps, b_ps = ps_aux.next()
                        for cc in range(4):
                            c = q4 * 4 + cc
                            mm(ps[:, cc * 128:(cc + 1) * 128], xn[:, c * 128:(c + 1) * 128], ident_bf, True, True,
                               [b_xn, b_identbf], [b_ps])
                        dst = uT3[:, q4 * 4:(q4 + 1) * 4, s * 128:(s + 1) * 128]
                        src = ps.rearrange("p (c n) -> p c n", c=4)
                        cp("act" if q4 % 2 == 0 else "dve", dst, src, [b_ps], [b_uT])

                def load_w(c0, cn):
                    wt, b_wt = wt_r.next()
                    wt3 = wt.rearrange("p (c n) -> p c n", c=DC)
                    deps = [bufs_w[("in", cc0)] for cc0 in range((c0 // 512) * 512, c0 + cn, 512)]
                    P.dma("sp", wt3[:, :, 0:cn], wb_in3[:, :, c0:c0 + cn], reads=deps, writes=[b_wt])
                    return wt3, b_wt

                pending = []
                def step_pending():
                    for gen in list(pending):
                        try:
                            next(gen)
                        except StopIteration:
                            pending.remove(gen)

                def flush_pending():
                    while pending:
                        step_pending()

                def fm_groups(c0, ngroups, post):
                    for j0 in range(0, ngroups, 4):
                        ng = min(4, ngroups - j0)
                        wt3, b_wt = load_w(c0 + j0 * 128, ng * 128)
                        for g in range(ng):
                            ps, b_ps = ps_r.next()
                            for c in range(DC):
                                mm(ps, wt3[:, c, g * 128:(g + 1) * 128], uT3[:, c, :], c == 0, c == DC - 1,
                                   [b_wt, b_uT], [b_ps])
                            step_pending()
                            gen = post(j0 + g, ps, b_ps)
                            if gen is not None and hasattr(gen, "__next__"):
                                try:
                                    next(gen)
                                    pending.append(gen)
                                except StopIteration:
                                    pass

                def conv_post(fam, dst_scr):
                    def post(g, ps, b_ps):
                        gi = fam * 8 + g
                        cb, b_cb = cb_r.next()
                        cp("pool", cb[:, 0:3], hist3[:, gi, :], [b_hist[gi]], [b_cb])
                        cp("act", cb[:, 3:TB + 3], ps, [b_ps], [b_cb])
                        cp("pool", hist3[:, gi, :], cb[:, TB:TB + 3], [b_cb], [b_hist[gi]])
                        acc, b_acc = acc_r.next()
                        ts("dve", acc, cb[:, 0:TB], cw3[:, gi, 0:1], None, ALU.mult, None, [b_cb, b_cw], [b_acc])
                        for j in range(1, 4):
                            stt("dve", acc, cb[:, j:j + TB], cw3[:, gi, j:j + 1], acc, ALU.mult, ALU.add,
                                [b_cb, b_cw, b_acc], [b_acc])
                        sil, b_sil = sil_r.next()
                        act(sil, acc, AF.Silu, [b_acc], [b_sil])
                        o16, b_o16 = o16_r.next()
                        if fam == 2:
                            cp("dve", o16, sil, [b_sil], [b_o16])
                        else:
                            sq, b_sq = norm_sq(sil, b_sil)
                            yield
                            yield
                            rinv, b_rinv = norm_rinv(sq, b_sq, "l2")
                            if fam == 0:
                                stt("dve", o16, sil, float(DH ** -0.5), rinv, ALU.mult, ALU.mult,
                                    [b_sil, b_rinv], [b_o16])
                            else:
                                tt("dve", o16, sil, rinv, ALU.mult, [b_sil, b_rinv], [b_o16])
                        store(dst_scr[g, :, t0:t0 + TB], o16, b_o16)
                    return post

                def za_post(g, ps, b_ps):
                    o16, b_o16 = o16_r.next()
                    act(o16, ps, AF.Silu, [b_ps], [b_o16])
                    store(s_za[g, :, tl0:tl0 + TB], o16, b_o16)

                def gate_post(which):
                    def post(g, ps, b_ps):
                        o16, b_o16 = o16_r.next()
                        act(o16, ps, AF.Sigmoid, [b_ps], [b_o16])
                        store(s_sg[which, g, :, tl0:tl0 + TB], o16, b_o16)
                    return post

                def kb_post(g, ps, b_ps):
                    sq, b_sq = norm_sq(ps, b_ps)
                    yield
                    rinv, b_rinv = norm_rinv(sq, b_sq, "rms")
                    o32, b_o32 = o32_r.next()
                    stt("dve", o32, ps, knw_c[:, 0:1], rinv, ALU.mult, ALU.mult, [b_ps, b_rinv, b_knw], [b_o32])
                    o16, b_o16 = o16_r.next()
                    cp("act", o16, o32, [b_o32], [b_o16])
                    store(s_kb[g, :, t0:t0 + TB], o16, b_o16)
                    nb0 = t0 // BS
                    P.op("dve", "tensor_reduce", [b_o32], [b_kmean], out=km3[:, g, nb0:nb0 + 2],
                         in_=o32.rearrange("p (b k) -> p b k", k=BS), axis=AX.X, op=ALU.add)

                def qb_post(g, ps, b_ps):
                    sq, b_sq = norm_sq(ps, b_ps)
                    yield
                    rinv, b_rinv = norm_rinv(sq, b_sq, "rms")
                    o32, b_o32 = o32_r.next()
                    stt("dve", o32, ps, qnw_c[:, 0:1], rinv, ALU.mult, ALU.mult, [b_ps, b_rinv, b_qnw], [b_o32])
                    o16, b_o16 = o16_r.next()
                    cp("act", o16, o32, [b_o32], [b_o16])
                    store(s_qb[g, :, tl0:tl0 + TB], o16, b_o16)
                    yield
                    pa, b_pa = ps_aux.next()
                    for s in range(4):
                        mm(pa[:, s * 32:(s + 1) * 32], o32[:, s * 128:(s + 1) * 128], km3[:, g, :], True, True,
                           [b_o32, b_kmean], [b_pa])
                    rt, b_rt = rt_r.next()
                    rmk, b_rmk = rmk_cur
                    own, b_own = own_cur
                    r3 = "p (b n) -> p b n"
                    for a in range(2):
                        rmk3 = rmk[:, a * 32:(a + 1) * 32].unsqueeze(1).to_broadcast([128, 2, 32])
                        stt("dve", rt[:, a * 64:(a + 1) * 64].rearrange(r3, b=2),
                            pa[:, a * 64:(a + 1) * 64].rearrange(r3, b=2), 1.0 / BS, rmk3,
                            ALU.mult, ALU.add, [b_pa, b_rmk], [b_rt])
                    mx, b_mx = mx_r.next()
                    for s in range(4):
                        P.op("dve", "max", [b_rt], [b_mx], out=mx[:, s * 8:(s + 1) * 8], in_=rt[:, s * 32:(s + 1) * 32])
                    sel, b_sel = sel_r.next()
                    thr = mx.rearrange("p (s k) -> p s k", k=8)[:, :, 2:3].to_broadcast([128, 4, 32])
                    sel3 = sel.rearrange("p (s n) -> p s n", n=32)
                    rt3 = rt.rearrange("p (s n) -> p s n", n=32)
                    tt("dve", sel3, rt3, thr, ALU.is_ge, [b_rt, b_mx], [b_sel])
                    stt("dve", sel, rt, -1e29, sel, ALU.is_gt, ALU.mult, [b_rt, b_sel], [b_sel])
                    for a in range(2):
                        own3 = own[:, a * 32:(a + 1) * 32].unsqueeze(1).to_broadcast([128, 2, 32])
                        sv = sel[:, a * 64:(a + 1) * 64].rearrange(r3, b=2)
                        tt("dve", sv, sv, own3, ALU.add, [b_sel, b_own], [b_sel])
                    ts("dve", sel, sel, -1.0, -NEG, ALU.add, ALU.mult, [b_sel], [b_sel])
                    yield
                    yield
                    pb, b_pb = ps_aux.next()
                    for s in range(4):
                        mm(pb[0:32, s * 128:(s + 1) * 128], sel[:, s * 32:(s + 1) * 32], ident, True, True,
                           [b_sel, b_ident], [b_pb])
                    ms, b_ms = ms_r.next()
                    cp("act", ms, pb[0:32, :], [b_pb], [b_ms])
                    store(s_msel[g, :, tl0:tl0 + TB], ms, b_ms)

                if local or bi == NPREB - 1:
                    fm_groups(C_QA, 8, conv_post(0, s_qa))
                fm_groups(C_KA, 8, conv_post(1, s_ka))
                fm_groups(C_VA, 8, conv_post(2, s_va))
                if local:
                    fm_groups(C_ZA, 8, za_post)
                flush_pending()
                wt3, b_wt = load_w(C_BETA, 16)
                for s in range(4):
                    ps, b_ps = ps_r.next()
                    for c in range(DC):
                        mm(ps[:, 0:16], uT3[:, c, s * 128:(s + 1) * 128], wt3[:, c, 0:16], c == 0, c == DC - 1,
                           [b_wt, b_uT], [b_ps])
                    o32, b_o32 = o32_r.next()
                    cp("act", o32[:, 0:16], ps[:, 0:16], [b_ps], [b_o32])
                    store(s_bd[t0 + s * 128:t0 + (s + 1) * 128, :], o32[:, 0:16], b_o32)
                fm_groups(C_KB, 8, kb_post)
                flush_pending()
                for j in range(2):
                    wt3, b_wt = load_w(C_VB + j * 512, 512)
                    for s in range(4):
                        ps, b_ps = ps_r.next()
                        for c in range(DC):
                            mm(ps, uT3[:, c, s * 128:(s + 1) * 128], wt3[:, c, :], c == 0, c == DC - 1,
                               [b_wt, b_uT], [b_ps])
                        o16, b_o16 = o16_r.next()
                        cp("act", o16, ps, [b_ps], [b_o16])
                        store(s_vb[t0 + s * 128:t0 + (s + 1) * 128, j * 512:(j + 1) * 512], o16, b_o16)
                if local:
                    qb0 = tl0 // BS
                    rmk, b_rmk = rmk_r.next()
                    src = bass.AP(tensor=c_rmask.tensor, offset=c_rmask.offset + qb0 * 32, ap=[[0, 128], [1, 64]])
                    P.dma("sp", rmk, src, writes=[b_rmk])
                    rmk_cur = (rmk, b_rmk)
                    own, b_own = own_r.next()
                    src2 = bass.AP(tensor=c_own.tensor, offset=c_own.offset + qb0 * 32, ap=[[0, 128], [1, 64]])
                    P.dma("sp", own, src2, writes=[b_own])
                    own_cur = (own, b_own)
                    fm_groups(C_QB, 8, qb_post)
                    fm_groups(C_GA, 16, gate_post(0))
                    fm_groups(C_GB, 16, gate_post(1))
                flush_pending()
            P.barrier()
            A.release(m0)

        if 1 in phases:
            phase1()

        def phase2():
            m0 = A.mark()
            NTL0 = T_PRE // 128
            wk_r = Ring([psum[0], psum[1], psum[2], psum[3]], excl=True)
            vs_b = [(psum[4], Buf(excl=True)), (psum[5], Buf(excl=True))]
            ot_b = [(psum[6], Buf(excl=True)), (psum[7], Buf(excl=True))]

            def T32(n=128):
                return [(A.alloc(n, F32), Buf()) for _ in range(H)]

            def T16(n=128):
                return [(A.alloc(n, BF16), Buf()) for _ in range(H)]
            kbg, vbt, dg1, dg2, X1, X2, dIm, dTm = T32(), T32(), T32(), T32(), T32(), T32(), T32(), T32()
            Lt = [T32(), T32()]
            Mt = [T32(), T32()]
            Yt = T32()
            kdec, negw, ATt, qdec, vnb = T16(), T16(), T16(), T16(), T16()
            S32 = T32()
            Sbf = T16()
            for h in range(H):
                memset("dve", S32[h][0], 0.0, [S32[h][1]])
                memset("pool", Sbf[h][0], 0.0, [Sbf[h][1]])
            in_r = [Ring([A.alloc(H * 128, BF16) for _ in range(2)]) for _ in range(4)]
            bd_r = Ring([A.alloc(16, F32) for _ in range(2)])
            g_r = Ring([A.alloc(96, F32) for _ in range(2)])
            sq_r = Ring([A.alloc(512, BF16) for _ in range(2)])
            rv_r = Ring([A.alloc(512, F32) for _ in range(2)])
            y32_r = Ring([A.alloc(512, F32) for _ in range(2)])
            y16_r = Ring([A.alloc(512, BF16) for _ in range(2)])

            def ld8(ring, scr, c0):
                t, b = ring.next()
                t3 = t.rearrange("p (h n) -> p h n", h=H)
                P.dma("sp", t3, scr[:, :, c0:c0 + 128].rearrange("h p t -> p h t"), writes=[b])
                return t3, b

            for i in range(NT):
                local = i >= NTL0
                t0 = i * 128
                tl0 = t0 - T_PRE
                kT, b_kT = ld8(in_r[0], s_ka, t0)
                vT, b_vT = ld8(in_r[1], s_va, t0)
                if local:
                    qT, b_qT = ld8(in_r[2], s_qa, t0)
                    zT, b_zT = ld8(in_r[3], s_za, tl0)
                bdt, b_bdt = bd_r.next()
                P.dma("sp", bdt, s_bd[t0:t0 + 128, :], writes=[b_bdt])
                g, b_g = g_r.next()
                act(g[:, 0:8], bdt[:, 0:8], AF.Sigmoid, [b_bdt], [b_g])
                tt("dve", g[:, 8:16], bdt[:, 8:16], dtb_b, ALU.add, [b_bdt, b_dtb], [b_g])
                act(g[:, 8:16], g[:, 8:16], AF.Exp, [b_g], [b_g])
                act(g[:, 8:16], g[:, 8:16], AF.Ln, [b_g], [b_g], bias=1.0)
                tt("dve", g[:, 8:16], g[:, 8:16], nA, ALU.mult, [b_g, b_nA], [b_g])
                pg, b_pg = wk_r.next()
                mm(pg[:, 0:8], Umat, g[:, 8:16], True, True, [b_U, b_g], [b_pg])
                mm(pg[:, 8:16], cm0, g[:, 8:16], True, True, [b_cm0, b_g], [b_pg])
                mm(pg[:, 16:24], cm1, g[:, 8:16], True, True, [b_cm1, b_g], [b_pg])
                mm(pg[:, 24:32], bdm, g[:, 8:16], True, True, [b_bdm, b_g], [b_pg])
                cp("dve", g[:, 16:48], pg[:, 0:32], [b_pg], [b_g])
                act(g[:, 48:56], g[:, 16:24], AF.Exp, [b_g], [b_g])
                tt("dve", g[:, 56:64], g[:, 40:48], g[:, 16:24], ALU.subtract, [b_g], [b_g])
                act(g[:, 56:64], g[:, 56:64], AF.Exp, [b_g], [b_g])
                act(g[:, 64:80], g[:, 24:40], AF.Exp, [b_g], [b_g])
                tt("dve", g[:, 80:88], g[:, 0:8], g[:, 48:56], ALU.mult, [b_g], [b_g])
                beta = lambda h: g[:, h:h + 1]
                gcum = lambda h: g[:, 16 + h:17 + h]
                eg = lambda h: g[:, 48 + h:49 + h]
                ekd = lambda h: g[:, 56 + h:57 + h]
                eglB = lambda c, h: g[:, 64 + c * 8 + h:65 + c * 8 + h]
                bg = lambda h: g[:, 80 + h:81 + h]

                import os
                STOP = os.environ.get('P2STOP', 'Z')
                def Q(bank, j):
                    return bank[:, j * 128:(j + 1) * 128]
                GR = [(0, 1, 2, 3), (4, 5, 6, 7)]
                for grp in GR:
                    bk, b_bk = wk_r.next()
                    for h in grp:
                        mm(Q(bk, h % 4), kT[:, h, :], ident_bf, True, True, [b_kT, b_identbf], [b_bk])
                    for h in grp:
                        ts("dve", kbg[h][0], Q(bk, h % 4), bg(h), None, ALU.mult, None, [b_bk, b_g], [kbg[h][1]])
                        ts("dve", kdec[h][0], Q(bk, h % 4), ekd(h), None, ALU.mult, None, [b_bk, b_g], [kdec[h][1]])
                    bv, b_bv = wk_r.next()
                    for h in grp:
                        mm(Q(bv, h % 4), vT[:, h, :], ident_bf, True, True, [b_vT, b_identbf], [b_bv])
                    for h in grp:
                        ts("dve", vbt[h][0], Q(bv, h % 4), beta(h), None, ALU.mult, None, [b_bv, b_g], [vbt[h][1]])
                if STOP < 'B':
                    continue
                for grp in GR:
                    for h in grp:
                        ts("dve", dg1[h][0], ident, gcum(h), None, ALU.mult, None, [b_ident, b_g], [dg1[h][1]])
                    bG, b_bG = wk_r.next()
                    for h in grp:
                        mm(Q(bG, h % 4), ones_f, dg1[h][0], True, True, [b_onesf, dg1[h][1]], [b_bG])
                    for h in grp:
                        stt("dve", X1[h][0], Q(bG, h % 4), gcum(h), bigL, ALU.subtract, ALU.max,
                            [b_bG, b_g, b_bigL], [X1[h][1]])
                        act(dIm[h][0], X1[h][0], AF.Exp, [X1[h][1]], [dIm[h][1]], scale=-1.0)
                        if local:
                            stt("dve", X2[h][0], Q(bG, h % 4), gcum(h), negU, ALU.subtract, ALU.min,
                                [b_bG, b_g, b_negU], [X2[h][1]])
                            act(dTm[h][0], X2[h][0], AF.Exp, [X2[h][1]], [dTm[h][1]])
                    if local:
                        for h in grp:
                            ts("dve", dg2[h][0], ident, eg(h), None, ALU.mult, None, [b_ident, b_g], [dg2[h][1]])
                        bE, b_bE = wk_r.next()
                        for h in grp:
                            mm(Q(bE, h % 4), ones_f, dg2[h][0], True, True, [b_onesf, dg2[h][1]], [b_bE])
                        for h in grp:
                            tt("dve", qdec[h][0], qT[:, h, :], Q(bE, h % 4), ALU.mult, [b_qT, b_bE], [qdec[h][1]])
                if STOP < 'C':
                    continue
                for grp in GR:
                    bK, b_bK = wk_r.next()
                    for h in grp:
                        mm(Q(bK, h % 4), kT[:, h, :], kT[:, h, :], True, True, [b_kT], [b_bK])
                    for h in grp:
                        stt("dve", Lt[0][h][0], Q(bK, h % 4), beta(h), dIm[h][0], ALU.mult, ALU.mult,
                            [b_bK, b_g, dIm[h][1]], [Lt[0][h][1]])
                for grp in GR:
                    bM, b_bM = wk_r.next()
                    for h in grp:
                        mm(Q(bM, h % 4), Lt[0][h][0], ident, True, True, [Lt[0][h][1], b_ident], [b_bM])
                    for h in grp:
                        cp("act", Mt[0][h][0], Q(bM, h % 4), [b_bM], [Mt[0][h][1]])
                        tt("dve", Yt[h][0], ident, Q(bM, h % 4), ALU.subtract, [b_ident, b_bM], [Yt[h][1]])
                if STOP < 'D':
                    continue
                cur = 0
                for k in range(5):
                    nxt = 1 - cur
                    for grp in GR:
                        bL, b_bL = wk_r.next()
                        for h in grp:
                            mm(Q(bL, h % 4), Mt[cur][h][0], Lt[cur][h][0], True, True,
                               [Mt[cur][h][1], Lt[cur][h][1]], [b_bL])
                        for h in grp:
                            cp("act", Lt[nxt][h][0], Q(bL, h % 4), [b_bL], [Lt[nxt][h][1]])
                    if k < 4:
                        for grp in GR:
                            bM, b_bM = wk_r.next()
                            for h in grp:
                                mm(Q(bM, h % 4), Lt[cur][h][0], Mt[cur][h][0], True, True,
                                   [Mt[cur][h][1], Lt[cur][h][1]], [b_bM])
                            for h in grp:
                                cp("dve" if h % 2 else "act", Mt[nxt][h][0], Q(bM, h % 4), [b_bM], [Mt[nxt][h][1]])
                    for grp in GR:
                        bY, b_bY = wk_r.next()
                        for h in grp:
                            mm(Q(bY, h % 4), Lt[nxt][h][0], Yt[h][0], True, True, [Lt[nxt][h][1], Yt[h][1]], [b_bY])
                        for h in grp:
                            tt("dve", Yt[h][0], Yt[h][0], Q(bY, h % 4), ALU.add, [Yt[h][1], b_bY], [Yt[h][1]])
                    cur = nxt
                if STOP < 'E':
                    continue
                for grp in GR:
                    bW, b_bW = wk_r.next()
                    for h in grp:
                        mm(Q(bW, h % 4), kbg[h][0], Yt[h][0], True, True, [kbg[h][1], Yt[h][1]], [b_bW])
                    for h in grp:
                        act(negw[h][0], Q(bW, h % 4), AF.Copy, [b_bW], [negw[h][1]], scale=-1.0)
                    if local:
                        bA, b_bA = wk_r.next()
                        for h in grp:
                            mm(Q(bA, h % 4), kT[:, h, :], qT[:, h, :], True, True, [b_kT, b_qT], [b_bA])
                        for h in grp:
                            tt("dve", ATt[h][0], Q(bA, h % 4), dTm[h][0], ALU.mult, [b_bA, dTm[h][1]], [ATt[h][1]])
                if STOP < 'F':
                    continue
                for c in range(2):
                    cs = slice(64 * c, 64 * c + 64)
                    for gi_, grp in enumerate(GR):
                        vb_, b_vb_ = vs_b[gi_]
                        for h in grp:
                            vq = Q(vb_, h % 4)
                            mm(vq[cs, :], Yt[h][0][:, cs], vbt[h][0], True, False, [Yt[h][1], vbt[h][1]], [b_vb_])
                            mm(vq[cs, :], negw[h][0][:, cs], Sbf[h][0], False, True, [negw[h][1], Sbf[h][1]], [b_vb_])
                        for h in grp:
                            cp("act", vnb[h][0][cs, :], Q(vb_, h % 4)[cs, :], [b_vb_], [vnb[h][1]])
                    for gi_, grp in enumerate(GR):
                        vb_, b_vb_ = vs_b[gi_]
                        ob, b_ob = ot_b[gi_]
                        for h in grp:
                            if local:
                                oc = ob[:, (h % 4) * 128 + 64 * c:(h % 4) * 128 + 64 * c + 64]
                                mm(oc, Sbf[h][0], qdec[h][0][:, cs], True, False, [Sbf[h][1], qdec[h][1]], [b_ob])
                                mm(oc, vnb[h][0][cs, :], ATt[h][0][cs, cs], False, True, [vnb[h][1], ATt[h][1]], [b_ob])
                            mm(Q(vb_, h % 4), kdec[h][0][cs, :], vnb[h][0][cs, :], True, True,
                               [kdec[h][1], vnb[h][1]], [b_vb_])
                        for h in grp:
                            stt("dve", S32[h][0], S32[h][0], eglB(c, h), Q(vb_, h % 4), ALU.mult, ALU.add,
                                [S32[h][1], b_g, b_vb_], [S32[h][1]])
                            cp("act", Sbf[h][0], S32[h][0], [S32[h][1]], [Sbf[h][1]])
                if STOP < 'G':
                    continue
                if local:
                    for hb in range(2):
                        ob, b_ob = ot_b[hb]
                        sq, b_sq = sq_r.next()
                        act(sq, ob, AF.Square, [b_ob], [b_sq])
                        pn, b_pn = wk_r.next()
                        mm(pn, inv128_bf, sq, True, True, [b_sq, b_inv128], [b_pn])
                        rv, b_rv = rv_r.next()
                        act(rv, pn, AF.Sqrt, [b_pn], [b_rv], bias=EPS, scale=1.0)
                        P.op("dve", "reciprocal", [b_rv], [b_rv], out=rv, in_=rv)
                        y32, b_y32 = y32_r.next()
                        stt("dve", y32, ob, onw_c[:, 0:1], rv, ALU.mult, ALU.mult, [b_ob, b_onw, b_rv], [b_y32])
                        y16, b_y16 = y16_r.next()
                        zsl = zT[:, hb * 4:(hb + 1) * 4, :]
                        tt("dve", y16.rearrange("p (h n) -> p h n", h=4), y32.rearrange("p (h n) -> p h n", h=4), zsl,
                           ALU.mult, [b_y32, b_zT], [b_y16])
                        P.dma("pool", s_ya[hb * 4:(hb + 1) * 4, :, tl0:tl0 + 128].rearrange("h p t -> p h t"),
                              y16.rearrange("p (h n) -> p h n", h=4), reads=[b_y16])
            P.barrier()
            A.release(m0)

        if 2 in phases:
            phase2()

        def phase3():
            m0 = A.mark()
            NQT = T_LOC // 512
            scale = float(DH ** -0.5)
            ebk_t = A.alloc(LF, F32, 33)
            b_ebk = Buf()
            P.dma("sp", ebk_t, c_ebk, writes=[b_ebk])
            esel_t = A.alloc(32 * 128, BF16, 32)
            b_esel = Buf()
            P.dma("sp", esel_t, c_esel.rearrange("r n k -> r (n k)"), writes=[b_esel])
            esel3 = esel_t.rearrange("r (n k) -> r n k", k=128)
            relt = A.alloc(H, F32, 32)
            b_relt = Buf()
            P.dma("sp", relt, rel_bias, writes=[b_relt])
            relrep = A.alloc(128, F32, 33)
            b_relrep = Buf()
            memset("dve", relrep, NEG, [b_relrep])
            Fb = A.alloc(LF, F32)
            b_Fb = Buf()
            GT_r = Ring([A.alloc(GW, F32) for _ in range(2)])
            kT_r = Ring([A.alloc(TT, BF16) for _ in range(2)])
            v_r = Ring([A.alloc(NT * 128, BF16) for _ in range(2)])
            q_r = Ring([A.alloc(T_LOC, BF16) for _ in range(2)])
            ms_r = Ring([A.alloc(T_LOC, BF16, 32) for _ in range(2)])
            tmp_r = Ring([A.alloc(512, F32) for _ in range(3)])
            pt_r = Ring([A.alloc(512, BF16) for _ in range(4)])
            rd_r = Ring([A.alloc(512, F32) for _ in range(2)])
            yo_r = Ring([A.alloc(512, BF16) for _ in range(2)])
            cf_r = Ring([A.alloc(1, F32) for _ in range(2)])
            s_ring = Ring([psum[0], psum[1], psum[2], psum[3]], excl=True)
            o_ps, b_o = psum[4], Buf(excl=True)
            d_ps, b_d = psum[5], Buf(excl=True)
            f_ring = Ring([psum[6], psum[7]], excl=True)
            b_frep = Buf()
            for h in range(H):
                cp("dve", relrep[0:32, :], relt[:, h:h + 1].to_broadcast([32, 128]), [b_relt], [b_relrep])
                for c0 in range(0, LF, 512):
                    cn = min(512, LF - c0)
                    pf, b_pf = f_ring.next()
                    mm(pf[:, 0:cn], relrep, ebk_t[:, c0:c0 + cn], True, True, [b_relrep, b_ebk], [b_pf])
                    cp("act", Fb[:, c0:c0 + cn], pf[:, 0:cn], [b_pf], [b_Fb])
                cf, b_cf = cf_r.next()
                cp("dve", cf, Fb[:, LF - 1:LF], [b_Fb], [b_cf])
                P.dma("sp", s_frep[h], Fb, reads=[b_Fb], writes=[b_frep])
                GT, b_GT = GT_r.next()
                src = bass.AP(tensor=s_frep.tensor, offset=s_frep.offset + h * 128 * LF + 127,
                              ap=[[LF - 1, 128], [1, GW]])
                P.dma("sp", GT, src, reads=[b_frep], writes=[b_GT])
                kTh, b_kTh = kT_r.next()
                P.dma("sp", kTh, s_kb[h], writes=[b_kTh])
                vt, b_vt = v_r.next()
                vt3 = vt.rearrange("p (n d) -> p n d", d=128)
                for n0 in range(0, NT, 16):
                    nn = min(16, NT - n0)
                    P.dma("sp", vt3[:, n0:n0 + nn, :],
                          s_vb[n0 * 128:(n0 + nn) * 128, h * 128:(h + 1) * 128].rearrange("(n p) d -> p n d", p=128),
                          writes=[b_vt])
                qh, b_qh = q_r.next()
                P.dma("sp", qh, s_qb[h], writes=[b_qh])
                msh, b_msh = ms_r.next()
                P.dma("sp", msh, s_msel[h], writes=[b_msh])
                for j in range(NQT):
                    q0 = T_PRE + 512 * j
                    ql = 512 * j
                    nsub = (q0 + 512) // 128
                    LA = 2
                    pts = {}

                    def front(ks):
                        k0 = ks * 128
                        n = k0 // BS
                        delta = q0 - k0
                        sp_, b_sp = s_ring.next()
                        mm(sp_, kTh[:, k0:k0 + 128], qh[:, ql:ql + 512], True, False, [b_kTh, b_qh], [b_sp])
                        mm(sp_, esel3[:, n, :], msh[:, ql:ql + 512], False, True, [b_esel, b_msh], [b_sp])
                        pt, b_pt = pt_r.next()
                        if delta >= FAR_DELTA:
                            act(pt, sp_, AF.Exp, [b_sp, b_cf], [b_pt], bias=cf[:, 0:1], scale=scale)
                        else:
                            tmp, b_tmp = tmp_r.next()
                            x0 = delta + 384
                            stt("dve", tmp, sp_, scale, GT[:, x0:x0 + 512], ALU.mult, ALU.add, [b_sp, b_GT], [b_tmp])
                            act(pt, tmp, AF.Exp, [b_tmp], [b_pt])
                        pts[ks] = (pt, b_pt)
                    for ks in range(min(LA, nsub)):
                        front(ks)
                    for ks in range(nsub):
                        if ks + LA < nsub:
                            front(ks + LA)
                        pt, b_pt = pts.pop(ks)
                        mm(o_ps, vt3[:, ks, :], pt, ks == 0, ks == nsub - 1, [b_vt, b_pt], [b_o])
                        mm(d_ps, ones_bf, pt, ks == 0, ks == nsub - 1, [b_ones, b_pt], [b_d])
                    rd, b_rd = rd_r.next()
                    P.op("dve", "reciprocal", [b_d], [b_rd], out=rd, in_=d_ps)
                    yo, b_yo = yo_r.next()
                    tt("dve", yo, o_ps, rd, ALU.mult, [b_o, b_rd], [b_yo])
                    P.dma("pool", s_yb[h, :, ql:ql + 512], yo, reads=[b_yo])
            P.barrier()
            A.release(m0)

        if 3 in phases:
            phase3()

        def phase4():
            NLB = T_LOC // 512
            cast_w(w_bg, wb_bg, D, 512, "bg")
            cast_w(w_bm, wb_bm, D, 512, "bm")
            cast_w(w_out, wb_out, D, 512, "out")
            m0 = A.mark()
            Wg_t = A.alloc(8 * D, BF16)
            Wm_t = A.alloc(8 * D, BF16)
            Wg3 = Wg_t.rearrange("p (c n) -> p c n", c=8)
            Wm3 = Wm_t.rearrange("p (c n) -> p c n", c=8)
            b_Wg, b_Wm = Buf(), Buf()
            for c0 in range(0, D, 512):
                P.dma("sp", Wg3[:, :, c0:c0 + 512], wb_bg.rearrange("(c p) n -> p c n", p=128)[:, :, c0:c0 + 512],
                      reads=[bufs_w[("bg", c0)]], writes=[b_Wg])
                P.dma("sp", Wm3[:, :, c0:c0 + 512], wb_bm.rearrange("(c p) n -> p c n", p=128)[:, :, c0:c0 + 512],
                      reads=[bufs_w[("bm", c0)]], writes=[b_Wm])
            ya_r = Ring([A.alloc(8 * 512, BF16) for _ in range(1)])
            yb_r = Ring([A.alloc(8 * 512, BF16) for _ in range(1)])
            sg_r = Ring([A.alloc(512, BF16) for _ in range(4)])
            t_r = Ring([A.alloc(512, F32) for _ in range(4)])
            mix_r = Ring([A.alloc(DC * 512, BF16) for _ in range(1)])
            wo_r = Ring([A.alloc(DC * 512, BF16) for _ in range(2)])
            xr_r = Ring([A.alloc(512, F32) for _ in range(3)])
            ho_r = Ring([A.alloc(512, F32) for _ in range(3)])
            pab = Ring([psum[i] for i in range(4)], excl=True)
            po = Ring([psum[i] for i in range(4, 8)], excl=True)
            wb_out3 = wb_out.rearrange("(c p) n -> p c n", p=128)
            for tb in range(NLB):
                ql = tb * 512
                ya, b_ya = ya_r.next()
                ya3 = ya.rearrange("p (h n) -> p h n", h=8)
                P.dma("sp", ya3, s_ya[:, :, ql:ql + 512].rearrange("h p t -> p h t"), writes=[b_ya])
                yb, b_yb = yb_r.next()
                yb3 = yb.rearrange("p (h n) -> p h n", h=8)
                P.dma("sp", yb3, s_yb[:, :, ql:ql + 512].rearrange("h p t -> p h t"), writes=[b_yb])
                mix, b_mix = mix_r.next()
                mix3 = mix.rearrange("p (c n) -> p c n", c=DC)
                for dc in range(DC):
                    sga, b_sga = sg_r.next()
                    P.dma("sp", sga, s_sg[0, dc, :, ql:ql + 512], writes=[b_sga])
                    sgb, b_sgb = sg_r.next()
                    P.dma("sp", sgb, s_sg[1, dc, :, ql:ql + 512], writes=[b_sgb])
                    pa, b_pa = pab.next()
                    for kc in range(8):
                        mm(pa, Wg3[:, kc, dc * 128:(dc + 1) * 128], ya3[:, kc, :], kc == 0, kc == 7, [b_Wg, b_ya], [b_pa])
                    pb, b_pb = pab.next()
                    for kc in range(8):
                        mm(pb, Wm3[:, kc, dc * 128:(dc + 1) * 128], yb3[:, kc, :], kc == 0, kc == 7, [b_Wm, b_yb], [b_pb])
                    t1, b_t1 = t_r.next()
                    tt("dve", t1, pa, sga, ALU.mult, [b_pa, b_sga], [b_t1])
                    t2, b_t2 = t_r.next()
                    tt("dve", t2, pb, sgb, ALU.mult, [b_pb, b_sgb], [b_t2])
                    tt("pool", mix3[:, dc, :], t1, t2, ALU.add, [b_t1, b_t2], [b_mix])
                for cg in range(4):
                    wo, b_wo = wo_r.next()
                    wo3 = wo.rearrange("p (c n) -> p c n", c=DC)
                    P.dma("sp", wo3, wb_out3[:, :, cg * 512:(cg + 1) * 512], reads=[bufs_w[("out", cg * 512)]],
                          writes=[b_wo])
                    for s in range(4):
                        r0 = ql + s * 128
                        xr, b_xr = xr_r.next()
                        P.dma("sp", xr, x_in[T_PRE + r0:T_PRE + r0 + 128, cg * 512:(cg + 1) * 512], writes=[b_xr])
                        pq, b_pq = po.next()
                        for kc in range(DC):
                            mm(pq, mix3[:, kc, s * 128:(s + 1) * 128], wo3[:, kc, :], kc == 0, kc == DC - 1,
                               [b_mix, b_wo], [b_pq])
                        ho, b_ho = ho_r.next()
                        tt("dve", ho, pq, xr, ALU.add, [b_pq, b_xr], [b_ho])
                        P.dma("pool", s_h[r0:r0 + 128, cg * 512:(cg + 1) * 512], ho, reads=[b_ho], writes=[b_sh])
            P.barrier()
            A.release(m0)

        b_sh = Buf()
        if 4 in phases:
            phase4()

        def phase5():
            NLB = T_LOC // 512
            cast_w(w_fg, wb_fg, DFF, 512, "fg")
            cast_w(w_fu, wb_fu, DFF, 512, "fu")
            cast_w(w_fd, wb_fd, D, 512, "fd")
            m0 = A.mark()
            nfw, b_nfw = bcast_load(norm_ffn_w, D)
            xt_r = Ring([A.alloc(D, F32) for _ in range(2)])
            xn_r = Ring([A.alloc(D, BF16) for _ in range(2)])
            uT_r = Ring([A.alloc(DC * 512, BF16) for _ in range(1)])
            hid_r = Ring([A.alloc(FC * 512, BF16) for _ in range(1)])
            wg_r = Ring([A.alloc(DC * 256, BF16) for _ in range(2)])
            wu_r = Ring([A.alloc(DC * 256, BF16) for _ in range(2)])
            wd_r = Ring([A.alloc(22 * 512, BF16) for _ in range(2)])
            sl_r = Ring([A.alloc(512, F32) for _ in range(2)])
            hr_r = Ring([A.alloc(512, F32) for _ in range(2)])
            yo_r = Ring([A.alloc(512, F32) for _ in range(2)])
            small_r = Ring([A.alloc(8, F32) for _ in range(4)])
            fr = Ring([psum[i] for i in range(4)], excl=True)
            dn = [(psum[4 + i], Buf(excl=True)) for i in range(4)]
            wb_fg3 = wb_fg.rearrange("(c p) n -> p c n", p=128)
            wb_fu3 = wb_fu.rearrange("(c p) n -> p c n", p=128)
            wb_fd3 = wb_fd.rearrange("(c p) n -> p c n", p=128)
            out_toks = []
            for tb in range(NLB):
                ql = tb * 512
                uT, b_uT = uT_r.next()
                uT3 = uT.rearrange("p (c n) -> p c n", c=DC)
                for s in range(4):
                    xt, b_xt = xt_r.next()
                    r0 = ql + s * 128
                    P.dma("sp", xt, s_h[r0:r0 + 128, :], reads=[b_sh], writes=[b_xt])
                    xn, b_xn = xn_r.next()
                    sm, b_sm = small_r.next()
                    act(xn, xt, AF.Square, [b_xt], [b_xn, b_sm], accum_out=sm[:, 0:1])
                    act(sm[:, 1:2], sm[:, 0:1], AF.Sqrt, [b_sm], [b_sm], bias=EPS, scale=1.0 / D)
                    P.op("dve", "reciprocal", [b_sm], [b_sm], out=sm[:, 2:3], in_=sm[:, 1:2])
                    stt("dve", xn, xt, sm[:, 2:3], nfw, ALU.mult, ALU.mult, [b_xt, b_sm, b_nfw], [b_xn])
                    for q4 in range(4):
                        ps, b_ps = fr.next()
                        for cc in range(4):
                            c = q4 * 4 + cc
                            mm(ps[:, cc * 128:(cc + 1) * 128], xn[:, c * 128:(c + 1) * 128], ident_bf, True, True,
                               [b_xn, b_identbf], [b_ps])
                        dst = uT3[:, q4 * 4:(q4 + 1) * 4, s * 128:(s + 1) * 128]
                        src = ps.rearrange("p (c n) -> p c n", c=4)
                        cp("act" if q4 % 2 == 0 else "dve", dst, src, [b_ps], [b_uT])
                hid, b_hid = hid_r.next()
                hid3 = hid.rearrange("p (c n) -> p c n", c=FC)
                for cg2 in range(FC // 2):
                    c0 = cg2 * 256
                    wg, b_wg = wg_r.next()
                    wg3 = wg.rearrange("p (c n) -> p c n", c=DC)
                    P.dma("sp", wg3, wb_fg3[:, :, c0:c0 + 256], reads=[bufs_w[("fg", (c0 // 512) * 512)]], writes=[b_wg])
                    wu, b_wu = wu_r.next()
                    wu3 = wu.rearrange("p (c n) -> p c n", c=DC)
                    P.dma("sp", wu3, wb_fu3[:, :, c0:c0 + 256], reads=[bufs_w[("fu", (c0 // 512) * 512)]], writes=[b_wu])
                    for g in range(2):
                        fc = cg2 * 2 + g
                        pg_, b_pg_ = fr.next()
                        for c in range(DC):
                            mm(pg_, wg3[:, c, g * 128:(g + 1) * 128], uT3[:, c, :], c == 0, c == DC - 1,
                               [b_wg, b_uT], [b_pg_])
                        pu_, b_pu_ = fr.next()
                        for c in range(DC):
                            mm(pu_, wu3[:, c, g * 128:(g + 1) * 128], uT3[:, c, :], c == 0, c == DC - 1,
                               [b_wu, b_uT], [b_pu_])
                        sl, b_sl = sl_r.next()
                        act(sl, pg_, AF.Silu, [b_pg_], [b_sl])
                        tt("dve", hid3[:, fc, :], sl, pu_, ALU.mult, [b_sl, b_pu_], [b_hid])
                for cg in range(4):
                    for hf in range(2):
                        wd, b_wd = wd_r.next()
                        wd3 = wd.rearrange("p (c n) -> p c n", c=22)
                        P.dma("sp", wd3, wb_fd3[:, hf * 22:(hf + 1) * 22, cg * 512:(cg + 1) * 512],
                              reads=[bufs_w[("fd", cg * 512)]], writes=[b_wd])
                        for s in range(4):
                            pd, b_pd = dn[s]
                            for f in range(22):
                                fc = hf * 22 + f
                                mm(pd, hid3[:, fc, s * 128:(s + 1) * 128], wd3[:, f, :], fc == 0, fc == FC - 1,
                                   [b_hid, b_wd], [b_pd])
                    for s in range(4):
                        pd, b_pd = dn[s]
                        r0 = ql + s * 128
                        hr, b_hr = hr_r.next()
                        P.dma("sp", hr, s_h[r0:r0 + 128, cg * 512:(cg + 1) * 512], reads=[b_sh], writes=[b_hr])
                        yo, b_yo = yo_r.next()
                        tt("dve", yo, pd, hr, ALU.add, [b_pd, b_hr], [b_yo])
                        P.dma("pool", y_out[r0:r0 + 128, cg * 512:(cg + 1) * 512], yo, reads=[b_yo])
            P.barrier()
            A.release(m0)

        if 5 in phases:
            phase5()

        P.barrier()
        P.emit()
    return nc, P


_CACHE = {}


def make_in_maps(inputs, B, T, T_PRE, T_LOC):
    consts = host_consts(T_PRE, T_LOC)
    x = np.asarray(inputs["x"], dtype=np.float32)
    in_maps = []
    nhalf = T // T_LOC
    for b in range(B):
        for half in range(nhalf):
            xc = np.zeros((T_PRE + T_LOC, D), np.float32)
            if half > 0:
                xc[:T_PRE] = x[b, half * T_LOC - T_PRE:half * T_LOC]
            xc[T_PRE:] = x[b, half * T_LOC:(half + 1) * T_LOC]
            m = {"x": xc}
            for k in ("norm_mix_w", "conv_w", "a_log", "dt_bias", "gdn_o_norm_w", "q_norm_w", "k_norm_w",
                      "w_in", "w_branch_gdn", "w_branch_moba", "w_out", "norm_ffn_w", "w_ffn_gate",
                      "w_ffn_up", "w_ffn_down"):
                a = np.asarray(inputs[k], dtype=np.float32)
                m[k] = np.ascontiguousarray(a[0]) if a.shape[0] == 1 and a.ndim == 3 else np.ascontiguousarray(a)
            m["rel_bias"] = np.ascontiguousarray(np.asarray(inputs["rel_bias"], dtype=np.float32))
            for k, v in consts.items():
                m["c_" + k] = v
            m["c_rmask"] = rmask_for(half, T_PRE, T_LOC)
            in_maps.append(m)
    return in_maps


def kernel(**inputs):
    x = np.asarray(inputs["x"])
    B, T, _ = x.shape
    T_LOC = T // 2
    T_PRE = T_LOC
    key = (T_PRE, T_LOC)
    if key not in _CACHE:
        _CACHE[key] = build_program(T_PRE, T_LOC)[0]
    nc = _CACHE[key]
    in_maps = make_in_maps(inputs, B, T, T_PRE, T_LOC)
    res = run_bass_kernel_spmd(nc, in_maps, core_ids=list(range(len(in_maps))))
    out = np.zeros((B, T, D), np.float32)
    i = 0
    for b in range(B):
        for half in range(2):
            out[b, half * T_LOC:(half + 1) * T_LOC] = res.results[i]["y"]
            i += 1
    return out
```

```python
import contextlib
import math
import numpy as np
import ml_dtypes
import concourse.bass as bass
import concourse.mybir as mybir
from concourse.bass_utils import run_bass_kernel_spmd

F32 = mybir.dt.float32
BF16 = mybir.dt.bfloat16
AF = mybir.ActivationFunctionType
ALU = mybir.AluOpType
AX = mybir.AxisListType

D = 2048
DC = 16
H = 8
DH = 128
W = 1024
DFF = 5632
FC = 44
DPROJ = 11280
EPS = 1e-6
NBKT = 32
BS = 256
NEG = -30000.0
LF = 2559
GW = 2432
FAR_DELTA = 1664
C_QA, C_KA, C_VA, C_ZA, C_BETA, C_DEC, C_QB, C_KB, C_VB, C_GA, C_GB = (
    0, 1024, 2048, 3072, 4096, 4104, 4112, 5136, 6160, 7184, 9232)


class Buf:
    __slots__ = ("name", "w", "r", "excl")

    def __init__(self, name="", excl=False):
        self.name = name
        self.w = None
        self.r = []
        self.excl = excl


class Prog:
    ENGS = ("pe", "act", "dve", "pool", "sp")
    NDMA = 8

    def __init__(self, nc):
        self.nc = nc
        self.stream = {e: [] for e in self.ENGS}
        self.count = {e: 0 for e in self.ENGS}
        self.waited = {e: {} for e in self.ENGS}
        self.dma_i = {e: 0 for e in self.ENGS}
        self.n_ins = 0

    def _deps(self, eng, reads, writes):
        need = {}
        for b in reads:
            if b.w is not None:
                need[b.w[0]] = max(need.get(b.w[0], 0), b.w[1])
            if b.excl:
                for t in b.r:
                    if t[0] != eng:
                        need[t[0]] = max(need.get(t[0], 0), t[1])
        for b in writes:
            if b.w is not None:
                need[b.w[0]] = max(need.get(b.w[0], 0), b.w[1])
            for t in b.r:
                need[t[0]] = max(need.get(t[0], 0), t[1])
        waits = []
        wd = self.waited[eng]
        for key, val in need.items():
            if key == eng and eng == "pe":
                continue
            if wd.get(key, 0) >= val:
                continue
            wd[key] = val
            waits.append((key, val))
        return waits

    def _mark(self, tok, reads, writes):
        for b in writes:
            b.w = tok
            b.r = []
        for b in reads:
            if b not in writes:
                b.r.append(tok)
        self.n_ins += 1

    def op(self, eng, name, reads=(), writes=(), **kw):
        waits = self._deps(eng, reads, writes)
        self.count[eng] += 1
        tok = (eng, self.count[eng])
        self.stream[eng].append(("op", (name, kw), waits))
        self._mark(tok, reads, writes)
        return tok

    def dma(self, eng, out, in_, reads=(), writes=()):
        fn = ("dma_start", dict(out=out, in_=in_))
        waits = self._deps(eng, reads, writes)
        i = self.dma_i[eng]
        self.dma_i[eng] += 1
        key = ("dma", eng, i % self.NDMA)
        val = 16 * (i // self.NDMA + 1)
        prev = val - 16
        if prev > 0 and self.waited[eng].get(key, 0) < prev:
            waits.append((key, prev))
            self.waited[eng][key] = prev
        tok = (key, val)
        self.stream[eng].append(("dma", fn, waits, key))
        self._mark(tok, reads, writes)
        return tok

    def barrier(self):
        toks = [(e, self.count[e]) for e in self.ENGS if self.count[e] > 0]
        for e in self.ENGS:
            for s in range(self.NDMA):
                i = self.dma_i[e]
                n = (i - s + self.NDMA - 1) // self.NDMA if i > s else 0
                if n > 0:
                    toks.append((("dma", e, s), 16 * n))
        for e in self.ENGS:
            waits = []
            for key, val in toks:
                if key == e and e == "pe":
                    continue
                if self.waited[e].get(key, 0) >= val:
                    continue
                self.waited[e][key] = val
                waits.append((key, val))
            if waits:
                self.stream[e].append(("wait", None, waits))

    def emit(self):
        nc = self.nc
        with contextlib.ExitStack() as es:
            sems = {}
            for e in self.ENGS:
                sems[e] = es.enter_context(nc.semaphore("p_" + e))
                for s in range(self.NDMA):
                    sems[("dma", e, s)] = es.enter_context(nc.semaphore("d_%s_%d" % (e, s)))
            es.enter_context(nc.allow_non_contiguous_dma(reason="tiny strided constant loads"))
            block = es.enter_context(nc.Block())

            def run(e, handle):
                for item in self.stream[e]:
                    for key, val in item[2]:
                        handle.wait_ge(sems[key], val)
                    if item[0] == "op":
                        getattr(handle, item[1][0])(**item[1][1]).then_inc(sems[e], 1)
                    elif item[0] == "dma":
                        getattr(handle, item[1][0])(**item[1][1]).then_inc(sems[item[3]], 16)

            @block.tensor
            def _(h):
                run("pe", h)

            @block.scalar
            def _(h):
                run("act", h)

            @block.vector
            def _(h):
                run("dve", h)

            @block.gpsimd
            def _(h):
                run("pool", h)

            @block.sync
            def _(h):
                run("sp", h)


class Ring:
    def __init__(self, aps, excl=False):
        self.items = [(ap, Buf(excl=excl)) for ap in aps]
        self.i = 0

    def next(self):
        it = self.items[self.i % len(self.items)]
        self.i += 1
        return it


class Arena:
    def __init__(self, t, width):
        self.t = t
        self.width = width
        self.off = 0

    def alloc(self, n, dtype=F32, parts=128):
        n32 = n if dtype == F32 else (n + 1) // 2
        assert self.off + n32 <= self.width, ("arena overflow", self.off, n32, self.width)
        ap = self.t[0:parts, self.off:self.off + n32]
        self.off += n32
        if dtype != F32:
            ap = ap.bitcast(dtype)[:, 0:n]
        return ap

    def mark(self):
        return self.off

    def release(self, m):
        self.off = m


def t5_bucket_np(dist):
    max_exact = NBKT // 2
    d = dist.astype(np.float32)
    log_ratio = np.log(np.maximum(d, np.float32(max_exact)) / np.float32(max_exact)) / np.float32(
        math.log(2048 / max_exact))
    large = max_exact + (log_ratio.astype(np.float32) * np.float32(NBKT - max_exact)).astype(np.int32)
    large = np.minimum(large, NBKT - 1)
    return np.where(dist < max_exact, dist, large)


def host_consts(T_PRE, T_LOC):
    c = {}
    idx = np.arange(128)
    same = (idx[:, None] // 64) == (idx[None, :] // 64)
    c["ident"] = np.eye(128, dtype=np.float32)
    c["U"] = (same & (idx[:, None] <= idx[None, :])).astype(np.float32)
    c["cm0"] = np.tile((idx[:, None] < 64), (1, 128)).astype(np.float32)
    c["cm1"] = np.tile((idx[:, None] >= 64), (1, 128)).astype(np.float32)
    c["bd"] = same.astype(np.float32)
    c["bigL"] = np.where(same & (idx[:, None] > idx[None, :]), 0.0, -NEG).astype(np.float32)
    c["negU"] = np.where(same & (idx[None, :] >= idx[:, None]), 0.0, NEG).astype(np.float32)
    i = np.arange(LF)
    dist = i - 511
    bk = t5_bucket_np(np.maximum(dist, 0))
    E = np.zeros((33, LF), np.float32)
    E[bk, i] = 1.0
    E[:, dist < 0] = 0.0
    E[32, dist < 0] = 1.0
    c["ebk"] = E
    es = np.zeros((32, 32, 128), np.float32)
    for n in range(32):
        es[n, n, :] = 1.0
    c["esel"] = es.astype(ml_dtypes.bfloat16)
    nqb = T_LOC // BS
    npb = T_PRE // BS
    own = np.zeros((nqb, 32), np.float32)
    for qb in range(nqb):
        own[qb, npb + qb] = 1.0
    c["own"] = own
    return c


def rmask_for(half, T_PRE, T_LOC):
    nqb = T_LOC // BS
    npb = T_PRE // BS
    m = np.zeros((nqb, 32), np.float32)
    for qb in range(nqb):
        m[qb, npb + qb:] = -1e30
        if half == 0:
            m[qb, :npb] = -1e30
    return m


def build_program(T_PRE, T_LOC, debug=False, phases=(1, 2, 3, 4, 5)):
    TT = T_PRE + T_LOC
    TB = 512
    NBLK = TT // TB
    NPREB = T_PRE // TB
    NT = TT // 128
    nc = bass.Bass("TRN2", target_bir_lowering=False)
    P = Prog(nc)
    skind = "ExternalOutput" if debug else "Internal"

    def din(name, shape, dt=F32):
        return nc.dram_tensor(name, list(shape), dt, kind="ExternalInput").ap()

    def dscr(name, shape, dt):
        return nc.dram_tensor(name, list(shape), dt, kind=skind).ap()

    x_in = din("x", [TT, D])
    w_in = din("w_in", [D, DPROJ])
    norm_mix_w = din("norm_mix_w", [1, D])
    conv_w = din("conv_w", [4, 3 * W])
    a_log = din("a_log", [1, H])
    dt_bias = din("dt_bias", [1, H])
    gdn_o_norm_w = din("gdn_o_norm_w", [1, DH])
    q_norm_w = din("q_norm_w", [1, DH])
    k_norm_w = din("k_norm_w", [1, DH])
    rel_bias = din("rel_bias", [NBKT, H])
    w_bg = din("w_branch_gdn", [W, D])
    w_bm = din("w_branch_moba", [W, D])
    w_out = din("w_out", [D, D])
    norm_ffn_w = din("norm_ffn_w", [1, D])
    w_fg = din("w_ffn_gate", [D, DFF])
    w_fu = din("w_ffn_up", [D, DFF])
    w_fd = din("w_ffn_down", [DFF, D])
    c_ident = din("c_ident", [128, 128])
    c_U = din("c_U", [128, 128])
    c_cm0 = din("c_cm0", [128, 128])
    c_cm1 = din("c_cm1", [128, 128])
    c_bd = din("c_bd", [128, 128])
    c_bigL = din("c_bigL", [128, 128])
    c_negU = din("c_negU", [128, 128])
    c_ebk = din("c_ebk", [33, LF])
    c_esel = din("c_esel", [32, 32, 128], BF16)
    c_own = din("c_own", [T_LOC // BS, 32])
    c_rmask = din("c_rmask", [T_LOC // BS, 32])
    y_out = nc.dram_tensor("y", [T_LOC, D], F32, kind="ExternalOutput").ap()

    wb_in = dscr("wb_in", [D, DPROJ], BF16)
    wb_bg = dscr("wb_bg", [W, D], BF16)
    wb_bm = dscr("wb_bm", [W, D], BF16)
    wb_out = dscr("wb_out", [D, D], BF16)
    wb_fg = dscr("wb_fg", [D, DFF], BF16)
    wb_fu = dscr("wb_fu", [D, DFF], BF16)
    wb_fd = dscr("wb_fd", [DFF, D], BF16)
    s_qa = dscr("s_qa", [H, 128, TT], BF16)
    s_ka = dscr("s_ka", [H, 128, TT], BF16)
    s_va = dscr("s_va", [H, 128, TT], BF16)
    s_za = dscr("s_za", [H, 128, T_LOC], BF16)
    s_bd = dscr("s_bd", [TT, 16], F32)
    s_qb = dscr("s_qb", [H, 128, T_LOC], BF16)
    s_kb = dscr("s_kb", [H, 128, TT], BF16)
    s_vb = dscr("s_vb", [TT, W], BF16)
    s_msel = dscr("s_msel", [H, 32, T_LOC], BF16)
    s_sg = dscr("s_sg", [2, DC, 128, T_LOC], BF16)
    s_ya = dscr("s_ya", [H, 128, T_LOC], BF16)
    s_yb = dscr("s_yb", [H, 128, T_LOC], BF16)
    s_frep = dscr("s_frep", [H, 128, LF], F32)
    s_h = dscr("s_h", [T_LOC, D], F32)

    wb_tok = {}
    bufs_w = {}

    with contextlib.ExitStack() as es:
        AW = 51200
        arena_t = es.enter_context(nc.sbuf_tensor("arena", [128, AW], F32))
        A = Arena(arena_t, AW)
        psum = []
        for i in range(8):
            pt = es.enter_context(nc.psum_tensor("ps%d" % i, [128, 512], F32))
            psum.append(pt[:, :])

        def mm(out, lhsT, rhs, start, stop, reads, writes):
            P.op("pe", "matmul", reads, writes, out=out, lhsT=lhsT, rhs=rhs, start=start, stop=stop)

        def act(out, in_, func, reads, writes, **kw):
            P.op("act", "activation", reads, writes, out=out, in_=in_, func=func, **kw)

        def ts(eng, out, in0, s1, s2, op0, op1, reads, writes):
            if op1 is None:
                P.op(eng, "tensor_scalar", reads, writes, out=out, in0=in0, scalar1=s1, scalar2=None, op0=op0)
            else:
                P.op(eng, "tensor_scalar", reads, writes, out=out, in0=in0, scalar1=s1, scalar2=s2, op0=op0, op1=op1)

        def stt(eng, out, in0, scalar, in1, op0, op1, reads, writes):
            P.op(eng, "scalar_tensor_tensor", reads, writes, out=out, in0=in0, scalar=scalar, in1=in1, op0=op0, op1=op1)

        def tt(eng, out, in0, in1, op, reads, writes):
            P.op(eng, "tensor_tensor", reads, writes, out=out, in0=in0, in1=in1, op=op)

        def cp(eng, out, in_, reads, writes):
            if eng == "act":
                act(out, in_, AF.Copy, reads, writes)
            else:
                P.op(eng, "tensor_copy", reads, writes, out=out, in_=in_)

        def memset(eng, ap, val, writes):
            P.op(eng, "memset", (), writes, ap=ap, constant=val)

        def load_const(ap_dram, n, dtype=F32, parts=128):
            t = A.alloc(n, dtype, parts)
            b = Buf()
            P.dma("sp", t, ap_dram, writes=[b])
            return t, b

        ident, b_ident = load_const(c_ident, 128)
        Umat, b_U = load_const(c_U, 128)
        cm0, b_cm0 = load_const(c_cm0, 128)
        cm1, b_cm1 = load_const(c_cm1, 128)
        bdm, b_bdm = load_const(c_bd, 128)
        bigL, b_bigL = load_const(c_bigL, 128)
        negU, b_negU = load_const(c_negU, 128)
        ident_bf = A.alloc(128, BF16)
        b_identbf = Buf()
        cp("dve", ident_bf, ident, [b_ident], [b_identbf])
        ones_bf = A.alloc(128, BF16)
        b_ones = Buf()
        memset("dve", ones_bf, 1.0, [b_ones])
        ones_f = A.alloc(128, F32)
        b_onesf = Buf()
        memset("dve", ones_f, 1.0, [b_onesf])
        inv128_bf = A.alloc(128, BF16)
        b_inv128 = Buf()
        memset("dve", inv128_bf, 1.0 / 128.0, [b_inv128])

        def bcast_load(ap_row, n):
            t = A.alloc(n, F32)
            b = Buf()
            src = bass.AP(tensor=ap_row.tensor, offset=ap_row.offset, ap=[[0, 128], [1, n]])
            P.dma("sp", t, src, writes=[b])
            return t, b

        nmw, b_nmw = bcast_load(norm_mix_w, D)
        alog_b, b_alog = bcast_load(a_log, H)
        dtb_b, b_dtb = bcast_load(dt_bias, H)

        def col_load(ap_row, n=128):
            t = A.alloc(1, F32)
            b = Buf()
            src = bass.AP(tensor=ap_row.tensor, offset=ap_row.offset, ap=[[1, n], [1, 1]])
            P.dma("sp", t, src, writes=[b])
            return t, b
        onw_c, b_onw = col_load(gdn_o_norm_w)
        qnw_c, b_qnw = col_load(q_norm_w)
        knw_c, b_knw = col_load(k_norm_w)
        cw = A.alloc(24 * 4, F32)
        cw3 = cw.rearrange("p (g j) -> p g j", j=4)
        b_cw = Buf()
        for j in range(4):
            src = bass.AP(tensor=conv_w.tensor, offset=conv_w.offset + j * 3 * W, ap=[[1, 128], [128, 24], [1, 1]])
            P.dma("sp", cw3[:, :, j:j + 1], src, writes=[b_cw])
        nA = A.alloc(H, F32)
        b_nA = Buf()
        act(nA, alog_b, AF.Exp, [b_alog], [b_nA])
        ts("dve", nA, nA, -1.0, None, ALU.mult, None, [b_nA], [b_nA])
        hist = A.alloc(24 * 3, F32)
        hist3 = hist.rearrange("p (g j) -> p g j", j=3)
        b_hist = [Buf() for _ in range(24)]
        memset("dve", hist, 0.0, b_hist)
        kmean = A.alloc(H * 32, F32)
        km3 = kmean.rearrange("p (h n) -> p h n", n=32)
        b_kmean = Buf()
        memset("dve", kmean, 0.0, [b_kmean])

        def cast_w(src, dst, cols, cstep, name, order=None):
            if (name, 0) in bufs_w:
                return
            starts = list(range(0, cols, cstep))
            if order is not None:
                starts = [starts[i] for i in order] + [c for i, c in enumerate(starts) if i not in order]
            for c0 in starts:
                cn = min(cstep, cols - c0)
                b = Buf()
                P.dma("pool", dst[:, c0:c0 + cn], src[:, c0:c0 + cn], writes=[b])
                bufs_w[(name, c0)] = b

        def store(dst, src, b_src):
            P.dma("pool", dst, src, reads=[b_src])

        def phase1():
            cast_w(w_in, wb_in, DPROJ, 512, "in", order=[2, 3, 4, 5, 8, 10, 11, 12, 13, 14, 0, 1])
            m0 = A.mark()
            xt_r = Ring([A.alloc(D, F32) for _ in range(2)])
            xn_r = Ring([A.alloc(D, BF16) for _ in range(2)])
            uT_r = Ring([A.alloc(DC * TB, BF16) for _ in range(2)])
            wt_r = Ring([A.alloc(DC * 512, BF16) for _ in range(3)])
            cb_r = Ring([A.alloc(TB + 3, F32) for _ in range(2)])
            acc_r = Ring([A.alloc(TB, F32) for _ in range(2)])
            sil_r = Ring([A.alloc(TB, F32) for _ in range(3)])
            sq_r = Ring([A.alloc(TB, BF16) for _ in range(3)])
            rinv_r = Ring([A.alloc(TB, F32) for _ in range(2)])
            o16_r = Ring([A.alloc(TB, BF16) for _ in range(10)])
            o32_r = Ring([A.alloc(TB, F32) for _ in range(6)])
            small_r = Ring([A.alloc(8, F32) for _ in range(4)])
            rt_r = Ring([A.alloc(4 * 32, F32) for _ in range(2)])
            mx_r = Ring([A.alloc(4 * 8, F32) for _ in range(2)])
            sel_r = Ring([A.alloc(4 * 32, F32) for _ in range(3)])
            ms_r = Ring([A.alloc(TB, BF16, 32) for _ in range(5)])
            rmk_r = Ring([A.alloc(64, F32) for _ in range(2)])
            own_r = Ring([A.alloc(64, F32) for _ in range(2)])
            ps_r = Ring([psum[i] for i in range(6)], excl=True)
            ps_aux = Ring([psum[6], psum[7]], excl=True)
            wb_in3 = wb_in.rearrange("(c p) n -> p c n", p=128)

            def norm_sq(src, b_src):
                sq, b_sq = sq_r.next()
                act(sq, src, AF.Square, [b_src], [b_sq])
                return sq, b_sq

            def norm_rinv(sq, b_sq, kind):
                pa, b_pa = ps_aux.next()
                lw = ones_bf if kind == "l2" else inv128_bf
                mm(pa, lw, sq, True, True, [b_sq, b_ones, b_inv128], [b_pa])
                rinv, b_rinv = rinv_r.next()
                act(rinv, pa, AF.Ln, [b_pa], [b_rinv], bias=EPS, scale=1.0)
                act(rinv, rinv, AF.Exp, [b_rinv], [b_rinv], scale=-0.5)
                return rinv, b_rinv

            for bi in range(NBLK):
                local = bi >= NPREB
                t0 = bi * TB
                tl0 = t0 - T_PRE
                uT, b_uT = uT_r.next()
                uT3 = uT.rearrange("p (c n) -> p c n", c=DC)
                for s in range(4):
                    xt, b_xt = xt_r.next()
                    r0 = t0 + s * 128
                    P.dma("sp", xt, x_in[r0:r0 + 128, :], writes=[b_xt])
                    xn, b_xn = xn_r.next()
                    sm, b_sm = small_r.next()
                    act(xn, xt, AF.Square, [b_xt], [b_xn, b_sm], accum_out=sm[:, 0:1])
                    act(sm[:, 1:2], sm[:, 0:1], AF.Sqrt, [b_sm], [b_sm], bias=EPS, scale=1.0 / D)
                    P.op("dve", "reciprocal", [b_sm], [b_sm], out=sm[:, 2:3], in_=sm[:, 1:2])
                    stt("dve", xn, xt, sm[:, 2:3], nmw, ALU.mult, ALU.mult, [b_xt, b_sm, b_nmw], [b_xn])
                    for q4 in range(4):
                        ps, b_ps = ps_aux.next()
                        for cc in range(4):
                            c = q4 * 4 + cc
                            mm(ps[:, cc * 128:(cc + 1) * 128], xn[:, c * 128:(c + 1) * 128], ident_bf, True, True,
                               [b_xn, b_identbf], [b_ps])
                        dst = uT3[:, q4 * 4:(q4 + 1) * 4, s * 128:(s + 1) * 128]
                        src = ps.rearrange("p (c n) -> p c n", c=4)
                        cp("act" if q4 % 2 == 0 else "dve", dst, src, [b_ps], [b_uT])

                def load_w(c0, cn):
                    wt, b_wt = wt_r.next()
                    wt3 = wt.rearrange("p (c n) -> p c n", c=DC)
                    deps = [bufs_w[("in", cc0)] for cc0 in range((c0 // 512) * 512, c0 + cn, 512)]
                    P.dma("sp", wt3[:, :, 0:cn], wb_in3[:, :, c0:c0 + cn], reads=deps, writes=[b_wt])
                    return wt3, b_wt

                pending = []
                def step_pending():
                    for gen in list(pending):
                        try:
                            next(gen)
                        except StopIteration:
                            pending.remove(gen)

                def flush_pending():
                    while pending:
                        step_pending()

                def fm_groups(c0, ngroups, post):
                    for j0 in range(0, ngroups, 4):
                        ng = min(4, ngroups - j0)
                        wt3, b_wt = load_w(c0 + j0 * 128, ng * 128)
                        for g in range(ng):
                            ps, b_ps = ps_r.next()
                            for c in range(DC):
                                mm(ps, wt3[:, c, g * 128:(g + 1) * 128], uT3[:, c, :], c == 0, c == DC - 1,
                                   [b_wt, b_uT], [b_ps])
                            step_pending()
                            gen = post(j0 + g, ps, b_ps)
                            if gen is not None and hasattr(gen, "__next__"):
                                try:
                                    next(gen)
                                    pending.append(gen)
                                except StopIteration:
                                    pass

                def conv_post(fam, dst_scr):
                    def post(g, ps, b_ps):
                        gi = fam * 8 + g
                        cb, b_cb = cb_r.next()
                        cp("pool", cb[:, 0:3], hist3[:, gi, :], [b_hist[gi]], [b_cb])
                        cp("act", cb[:, 3:TB + 3], ps, [b_ps], [b_cb])
                        cp("pool", hist3[:, gi, :], cb[:, TB:TB + 3], [b_cb], [b_hist[gi]])
                        acc, b_acc = acc_r.next()
                        ts("dve", acc, cb[:, 0:TB], cw3[:, gi, 0:1], None, ALU.mult, None, [b_cb, b_cw], [b_acc])
                        for j in range(1, 4):
                            stt("dve", acc, cb[:, j:j + TB], cw3[:, gi, j:j + 1], acc, ALU.mult, ALU.add,
                                [b_cb, b_cw, b_acc], [b_acc])
                        sil, b_sil = sil_r.next()
                        act(sil, acc, AF.Silu, [b_acc], [b_sil])
                        o16, b_o16 = o16_r.next()
                        if fam == 2:
                            cp("dve", o16, sil, [b_sil], [b_o16])
                        else:
                            sq, b_sq = norm_sq(sil, b_sil)
                            yield
                            yield
                            rinv, b_rinv = norm_rinv(sq, b_sq, "l2")
                            if fam == 0:
                                stt("dve", o16, sil, float(DH ** -0.5), rinv, ALU.mult, ALU.mult,
                                    [b_sil, b_rinv], [b_o16])
                            else:
                                tt("dve", o16, sil, rinv, ALU.mult, [b_sil, b_rinv], [b_o16])
                        store(dst_scr[g, :, t0:t0 + TB], o16, b_o16)
                    return post

                def za_post(g, ps, b_ps):
                    o16, b_o16 = o16_r.next()
                    act(o16, ps, AF.Silu, [b_ps], [b_o16])
                    store(s_za[g, :, tl0:tl0 + TB], o16, b_o16)

                def gate_post(which):
                    def post(g, ps, b_ps):
                        o16, b_o16 = o16_r.next()
                        act(o16, ps, AF.Sigmoid, [b_ps], [b_o16])
                        store(s_sg[which, g, :, tl0:tl0 + TB], o16, b_o16)
                    return post

                def kb_post(g, ps, b_ps):
                    sq, b_sq = norm_sq(ps, b_ps)
                    yield
                    rinv, b_rinv = norm_rinv(sq, b_sq, "rms")
                    o32, b_o32 = o32_r.next()
                    stt("dve", o32, ps, knw_c[:, 0:1], rinv, ALU.mult, ALU.mult, [b_ps, b_rinv, b_knw], [b_o32])
                    o16, b_o16 = o16_r.next()
                    cp("act", o16, o32, [b_o32], [b_o16])
                    store(s_kb[g, :, t0:t0 + TB], o16, b_o16)
                    nb0 = t0 // BS
                    P.op("dve", "tensor_reduce", [b_o32], [b_kmean], out=km3[:, g, nb0:nb0 + 2],
                         in_=o32.rearrange("p (b k) -> p b k", k=BS), axis=AX.X, op=ALU.add)

                def qb_post(g, ps, b_ps):
                    sq, b_sq = norm_sq(ps, b_ps)
                    yield
                    rinv, b_rinv = norm_rinv(sq, b_sq, "rms")
                    o32, b_o32 = o32_r.next()
                    stt("dve", o32, ps, qnw_c[:, 0:1], rinv, ALU.mult, ALU.mult, [b_ps, b_rinv, b_qnw], [b_o32])
                    o16, b_o16 = o16_r.next()
                    cp("act", o16, o32, [b_o32], [b_o16])
                    store(s_qb[g, :, tl0:tl0 + TB], o16, b_o16)
                    yield
                    pa, b_pa = ps_aux.next()
                    for s in range(4):
                        mm(pa[:, s * 32:(s + 1) * 32], o32[:, s * 128:(s + 1) * 128], km3[:, g, :], True, True,
                           [b_o32, b_kmean], [b_pa])
                    rt, b_rt = rt_r.next()
                    rmk, b_rmk = rmk_cur
                    own, b_own = own_cur
                    r3 = "p (b n) -> p b n"
                    for a in range(2):
                        rmk3 = rmk[:, a * 32:(a + 1) * 32].unsqueeze(1).to_broadcast([128, 2, 32])
                        stt("dve", rt[:, a * 64:(a + 1) * 64].rearrange(r3, b=2),
                            pa[:, a * 64:(a + 1) * 64].rearrange(r3, b=2), 1.0 / BS, rmk3,
                            ALU.mult, ALU.add, [b_pa, b_rmk], [b_rt])
                    mx, b_mx = mx_r.next()
                    for s in range(4):
                        P.op("dve", "max", [b_rt], [b_mx], out=mx[:, s * 8:(s + 1) * 8], in_=rt[:, s * 32:(s + 1) * 32])
                    sel, b_sel = sel_r.next()
                    thr = mx.rearrange("p (s k) -> p s k", k=8)[:, :, 2:3].to_broadcast([128, 4, 32])
                    sel3 = sel.rearrange("p (s n) -> p s n", n=32)
                    rt3 = rt.rearrange("p (s n) -> p s n", n=32)
                    tt("dve", sel3, rt3, thr, ALU.is_ge, [b_rt, b_mx], [b_sel])
                    stt("dve", sel, rt, -1e29, sel, ALU.is_gt, ALU.mult, [b_rt, b_sel], [b_sel])
                    for a in range(2):
                        own3 = own[:, a * 32:(a + 1) * 32].unsqueeze(1).to_broadcast([128, 2, 32])
                        sv = sel[:, a * 64:(a + 1) * 64].rearrange(r3, b=2)
                        tt("dve", sv, sv, own3, ALU.add, [b_sel, b_own], [b_sel])
                    ts("dve", sel, sel, -1.0, -NEG, ALU.add, ALU.mult, [b_sel], [b_sel])
                    yield
                    yield
                    pb, b_pb = ps_aux.next()
                    for s in range(4):
                        mm(pb[0:32, s * 128:(s + 1) * 128], sel[:, s * 32:(s + 1) * 32], ident, True, True,
                           [b_sel, b_ident], [b_pb])
                    ms, b_ms = ms_r.next()
                    cp("act", ms, pb[0:32, :], [b_pb], [b_ms])
                    store(s_msel[g, :, tl0:tl0 + TB], ms, b_ms)

                if local or bi == NPREB - 1:
                    fm_groups(C_QA, 8, conv_post(0, s_qa))
                fm_groups(C_KA, 8, conv_post(1, s_ka))
                fm_groups(C_VA, 8, conv_post(2, s_va))
                if local:
                    fm_groups(C_ZA, 8, za_post)
                flush_pending()
                wt3, b_wt = load_w(C_BETA, 16)
                for s in range(4):
                    ps, b_ps = ps_r.next()
                    for c in range(DC):
                        mm(ps[:, 0:16], uT3[:, c, s * 128:(s + 1) * 128], wt3[:, c, 0:16], c == 0, c == DC - 1,
                           [b_wt, b_uT], [b_ps])
                    o32, b_o32 = o32_r.next()
                    cp("act", o32[:, 0:16], ps[:, 0:16], [b_ps], [b_o32])
                    store(s_bd[t0 + s * 128:t0 + (s + 1) * 128, :], o32[:, 0:16], b_o32)
                fm_groups(C_KB, 8, kb_post)
                flush_pending()
                for j in range(2):
                    wt3, b_wt = load_w(C_VB + j * 512, 512)
                    for s in range(4):
                        ps, b_ps = ps_r.next()
                        for c in range(DC):
                            mm(ps, uT3[:, c, s * 128:(s + 1) * 128], wt3[:, c, :], c == 0, c == DC - 1,
                               [b_wt, b_uT], [b_ps])
                        o16, b_o16 = o16_r.next()
                        cp("act", o16, ps, [b_ps], [b_o16])
                        store(s_vb[t0 + s * 128:t0 + (s + 1) * 128, j * 512:(j + 1) * 512], o16, b_o16)
                if local:
                    qb0 = tl0 // BS
                    rmk, b_rmk = rmk_r.next()
                    src = bass.AP(tensor=c_rmask.tensor, offset=c_rmask.offset + qb0 * 32, ap=[[0, 128], [1, 64]])
                    P.dma("sp", rmk, src, writes=[b_rmk])
                    rmk_cur = (rmk, b_rmk)
                    own, b_own = own_r.next()
                    src2 = bass.AP(tensor=c_own.tensor, offset=c_own.offset + qb0 * 32, ap=[[0, 128], [1, 64]])
                    P.dma("sp", own, src2, writes=[b_own])
                    own_cur = (own, b_own)
                    fm_groups(C_QB, 8, qb_post)
                    fm_groups(C_GA, 16, gate_post(0))
                    fm_groups(C_GB, 16, gate_post(1))
                flush_pending()
            P.barrier()
            A.release(m0)

        if 1 in phases:
            phase1()

        def phase2():
            m0 = A.mark()
            NTL0 = T_PRE // 128
            wk_r = Ring([psum[0], psum[1], psum[2], psum[3]], excl=True)
            vs_b = [(psum[4], Buf(excl=True)), (psum[5], Buf(excl=True))]
            ot_b = [(psum[6], Buf(excl=True)), (psum[7], Buf(excl=True))]

            def T32(n=128):
                return [(A.alloc(n, F32), Buf()) for _ in range(H)]

            def T16(n=128):
                return [(A.alloc(n, BF16), Buf()) for _ in range(H)]
            kbg, vbt, dg1, dg2, X1, X2, dIm, dTm = T32(), T32(), T32(), T32(), T32(), T32(), T32(), T32()
            Lt = [T32(), T32()]
            Mt = [T32(), T32()]
            Yt = T32()
            kdec, negw, ATt, qdec, vnb = T16(), T16(), T16(), T16(), T16()
            S32 = T32()
            Sbf = T16()
            for h in range(H):
                memset("dve", S32[h][0], 0.0, [S32[h][1]])
                memset("pool", Sbf[h][0], 0.0, [Sbf[h][1]])
            in_r = [Ring([A.alloc(H * 128, BF16) for _ in range(2)]) for _ in range(4)]
            bd_r = Ring([A.alloc(16, F32) for _ in range(2)])
            g_r = Ring([A.alloc(96, F32) for _ in range(2)])
            sq_r = Ring([A.alloc(512, BF16) for _ in range(2)])
            rv_r = Ring([A.alloc(512, F32) for _ in range(2)])
            y32_r = Ring([A.alloc(512, F32) for _ in range(2)])
            y16_r = Ring([A.alloc(512, BF16) for _ in range(2)])

            def ld8(ring, scr, c0):
                t, b = ring.next()
                t3 = t.rearrange("p (h n) -> p h n", h=H)
                P.dma("sp", t3, scr[:, :, c0:c0 + 128].rearrange("h p t -> p h t"), writes=[b])
                return t3, b

            for i in range(NT):
                local = i >= NTL0
                t0 = i * 128
                tl0 = t0 - T_PRE
                kT, b_kT = ld8(in_r[0], s_ka, t0)
                vT, b_vT = ld8(in_r[1], s_va, t0)
                if local:
                    qT, b_qT = ld8(in_r[2], s_qa, t0)
                    zT, b_zT = ld8(in_r[3], s_za, tl0)
                bdt, b_bdt = bd_r.next()
                P.dma("sp", bdt, s_bd[t0:t0 + 128, :], writes=[b_bdt])
                g, b_g = g_r.next()
                act(g[:, 0:8], bdt[:, 0:8], AF.Sigmoid, [b_bdt], [b_g])
                tt("dve", g[:, 8:16], bdt[:, 8:16], dtb_b, ALU.add, [b_bdt, b_dtb], [b_g])
                act(g[:, 8:16], g[:, 8:16], AF.Exp, [b_g], [b_g])
                act(g[:, 8:16], g[:, 8:16], AF.Ln, [b_g], [b_g], bias=1.0)
                tt("dve", g[:, 8:16], g[:, 8:16], nA, ALU.mult, [b_g, b_nA], [b_g])
                pg, b_pg = wk_r.next()
                mm(pg[:, 0:8], Umat, g[:, 8:16], True, True, [b_U, b_g], [b_pg])
                mm(pg[:, 8:16], cm0, g[:, 8:16], True, True, [b_cm0, b_g], [b_pg])
                mm(pg[:, 16:24], cm1, g[:, 8:16], True, True, [b_cm1, b_g], [b_pg])
                mm(pg[:, 24:32], bdm, g[:, 8:16], True, True, [b_bdm, b_g], [b_pg])
                cp("dve", g[:, 16:48], pg[:, 0:32], [b_pg], [b_g])
                act(g[:, 48:56], g[:, 16:24], AF.Exp, [b_g], [b_g])
                tt("dve", g[:, 56:64], g[:, 40:48], g[:, 16:24], ALU.subtract, [b_g], [b_g])
                act(g[:, 56:64], g[:, 56:64], AF.Exp, [b_g], [b_g])
                act(g[:, 64:80], g[:, 24:40], AF.Exp, [b_g], [b_g])
                tt("dve", g[:, 80:88], g[:, 0:8], g[:, 48:56], ALU.mult, [b_g], [b_g])
                beta = lambda h: g[:, h:h + 1]
                gcum = lambda h: g[:, 16 + h:17 + h]
                eg = lambda h: g[:, 48 + h:49 + h]
                ekd = lambda h: g[:, 56 + h:57 + h]
                eglB = lambda c, h: g[:, 64 + c * 8 + h:65 + c * 8 + h]
                bg = lambda h: g[:, 80 + h:81 + h]

                import os
                STOP = os.environ.get('P2STOP', 'Z')
                def Q(bank, j):
                    return bank[:, j * 128:(j + 1) * 128]
                GR = [(0, 1, 2, 3), (4, 5, 6, 7)]
                for grp in GR:
                    bk, b_bk = wk_r.next()
                    for h in grp:
                        mm(Q(bk, h % 4), kT[:, h, :], ident_bf, True, True, [b_kT, b_identbf], [b_bk])
                    for h in grp:
                        ts("dve", kbg[h][0], Q(bk, h % 4), bg(h), None, ALU.mult, None, [b_bk, b_g], [kbg[h][1]])
                        ts("dve", kdec[h][0], Q(bk, h % 4), ekd(h), None, ALU.mult, None, [b_bk, b_g], [kdec[h][1]])
                    bv, b_bv = wk_r.next()
                    for h in grp:
                        mm(Q(bv, h % 4), vT[:, h, :], ident_bf, True, True, [b_vT, b_identbf], [b_bv])
                    for h in grp:
                        ts("dve", vbt[h][0], Q(bv, h % 4), beta(h), None, ALU.mult, None, [b_bv, b_g], [vbt[h][1]])
                if STOP < 'B':
                    continue
                for grp in GR:
                    for h in grp:
                        ts("dve", dg1[h][0], ident, gcum(h), None, ALU.mult, None, [b_ident, b_g], [dg1[h][1]])
                    bG, b_bG = wk_r.next()
                    for h in grp:
                        mm(Q(bG, h % 4), ones_f, dg1[h][0], True, True, [b_onesf, dg1[h][1]], [b_bG])
                    for h in grp:
                        stt("dve", X1[h][0], Q(bG, h % 4), gcum(h), bigL, ALU.subtract, ALU.max,
                            [b_bG, b_g, b_bigL], [X1[h][1]])
                        act(dIm[h][0], X1[h][0], AF.Exp, [X1[h][1]], [dIm[h][1]], scale=-1.0)
                        if local:
                            stt("dve", X2[h][0], Q(bG, h % 4), gcum(h), negU, ALU.subtract, ALU.min,
                                [b_bG, b_g, b_negU], [X2[h][1]])
                            act(dTm[h][0], X2[h][0], AF.Exp, [X2[h][1]], [dTm[h][1]])
                    if local:
                        for h in grp:
                            ts("dve", dg2[h][0], ident, eg(h), None, ALU.mult, None, [b_ident, b_g], [dg2[h][1]])
                        bE, b_bE = wk_r.next()
                        for h in grp:
                            mm(Q(bE, h % 4), ones_f, dg2[h][0], True, True, [b_onesf, dg2[h][1]], [b_bE])
                        for h in grp:
                            tt("dve", qdec[h][0], qT[:, h, :], Q(bE, h % 4), ALU.mult, [b_qT, b_bE], [qdec[h][1]])
                if STOP < 'C':
                    continue
                for grp in GR:
                    bK, b_bK = wk_r.next()
                    for h in grp:
                        mm(Q(bK, h % 4), kT[:, h, :], kT[:, h, :], True, True, [b_kT], [b_bK])
                    for h in grp:
                        stt("dve", Lt[0][h][0], Q(bK, h % 4), beta(h), dIm[h][0], ALU.mult, ALU.mult,
                            [b_bK, b_g, dIm[h][1]], [Lt[0][h][1]])
                for grp in GR:
                    bM, b_bM = wk_r.next()
                    for h in grp:
                        mm(Q(bM, h % 4), Lt[0][h][0], ident, True, True, [Lt[0][h][1], b_ident], [b_bM])
                    for h in grp:
                        cp("act", Mt[0][h][0], Q(bM, h % 4), [b_bM], [Mt[0][h][1]])
                        tt("dve", Yt[h][0], ident, Q(bM, h % 4), ALU.subtract, [b_ident, b_bM], [Yt[h][1]])
                if STOP < 'D':
                    continue
                cur = 0
                for k in range(5):
                    nxt = 1 - cur
                    for grp in GR:
                        bL, b_bL = wk_r.next()
                        for h in grp:
                            mm(Q(bL, h % 4), Mt[cur][h][0], Lt[cur][h][0], True, True,
                               [Mt[cur][h][1], Lt[cur][h][1]], [b_bL])
                        for h in grp:
                            cp("act", Lt[nxt][h][0], Q(bL, h % 4), [b_bL], [Lt[nxt][h][1]])
                    if k < 4:
                        for grp in GR:
                            bM, b_bM = wk_r.next()
                            for h in grp:
                                mm(Q(bM, h % 4), Lt[cur][h][0], Mt[cur][h][0], True, True,
                                   [Mt[cur][h][1], Lt[cur][h][1]], [b_bM])
                            for h in grp:
                                cp("dve" if h % 2 else "act", Mt[nxt][h][0], Q(bM, h % 4), [b_bM], [Mt[nxt][h][1]])
                    for grp in GR:
                        bY, b_bY = wk_r.next()
                        for h in grp:
                            mm(Q(bY, h % 4), Lt[nxt][h][0], Yt[h][0], True, True, [Lt[nxt][h][1], Yt[h][1]], [b_bY])
                        for h in grp:
                            tt("dve", Yt[h][0], Yt[h][0], Q(bY, h % 4), ALU.add, [Yt[h][1], b_bY], [Yt[h][1]])
                    cur = nxt
                if STOP < 'E':
                    continue
                for grp in GR:
                    bW, b_bW = wk_r.next()
                    for h in grp:
                        mm(Q(bW, h % 4), kbg[h][0], Yt[h][0], True, True, [kbg[h][1], Yt[h][1]], [b_bW])
                    for h in grp:
                        act(negw[h][0], Q(bW, h % 4), AF.Copy, [b_bW], [negw[h][1]], scale=-1.0)
                    if local:
                        bA, b_bA = wk_r.next()
                        for h in grp:
                            mm(Q(bA, h % 4), kT[:, h, :], qT[:, h, :], True, True, [b_kT, b_qT], [b_bA])
                        for h in grp:
                            tt("dve", ATt[h][0], Q(bA, h % 4), dTm[h][0], ALU.mult, [b_bA, dTm[h][1]], [ATt[h][1]])
                if STOP < 'F':
                    continue
                for c in range(2):
                    cs = slice(64 * c, 64 * c + 64)
                    for gi_, grp in enumerate(GR):
                        vb_, b_vb_ = vs_b[gi_]
                        for h in grp:
                            vq = Q(vb_, h % 4)
                            mm(vq[cs, :], Yt[h][0][:, cs], vbt[h][0], True, False, [Yt[h][1], vbt[h][1]], [b_vb_])
                            mm(vq[cs, :], negw[h][0][:, cs], Sbf[h][0], False, True, [negw[h][1], Sbf[h][1]], [b_vb_])
                        for h in grp:
                            cp("act", vnb[h][0][cs, :], Q(vb_, h % 4)[cs, :], [b_vb_], [vnb[h][1]])
                    for gi_, grp in enumerate(GR):
                        vb_, b_vb_ = vs_b[gi_]
                        ob, b_ob = ot_b[gi_]
                        for h in grp:
                            if local:
                                oc = ob[:, (h % 4) * 128 + 64 * c:(h % 4) * 128 + 64 * c + 64]
                                mm(oc, Sbf[h][0], qdec[h][0][:, cs], True, False, [Sbf[h][1], qdec[h][1]], [b_ob])
                                mm(oc, vnb[h][0][cs, :], ATt[h][0][cs, cs], False, True, [vnb[h][1], ATt[h][1]], [b_ob])
                            mm(Q(vb_, h % 4), kdec[h][0][cs, :], vnb[h][0][cs, :], True, True,
                               [kdec[h][1], vnb[h][1]], [b_vb_])
                        for h in grp:
                            stt("dve", S32[h][0], S32[h][0], eglB(c, h), Q(vb_, h % 4), ALU.mult, ALU.add,
                                [S32[h][1], b_g, b_vb_], [S32[h][1]])
                            cp("act", Sbf[h][0], S32[h][0], [S32[h][1]], [Sbf[h][1]])
                if STOP < 'G':
                    continue
                if local:
                    for hb in range(2):
                        ob, b_ob = ot_b[hb]
                        sq, b_sq = sq_r.next()
                        act(sq, ob, AF.Square, [b_ob], [b_sq])
                        pn, b_pn = wk_r.next()
                        mm(pn, inv128_bf, sq, True, True, [b_sq, b_inv128], [b_pn])
                        rv, b_rv = rv_r.next()
                        act(rv, pn, AF.Sqrt, [b_pn], [b_rv], bias=EPS, scale=1.0)
                        P.op("dve", "reciprocal", [b_rv], [b_rv], out=rv, in_=rv)
                        y32, b_y32 = y32_r.next()
                        stt("dve", y32, ob, onw_c[:, 0:1], rv, ALU.mult, ALU.mult, [b_ob, b_onw, b_rv], [b_y32])
                        y16, b_y16 = y16_r.next()
                        zsl = zT[:, hb * 4:(hb + 1) * 4, :]
                        tt("dve", y16.rearrange("p (h n) -> p h n", h=4), y32.rearrange("p (h n) -> p h n", h=4), zsl,
                           ALU.mult, [b_y32, b_zT], [b_y16])
                        P.dma("pool", s_ya[hb * 4:(hb + 1) * 4, :, tl0:tl0 + 128].rearrange("h p t -> p h t"),
                              y16.rearrange("p (h n) -> p h n", h=4), reads=[b_y16])
            P.barrier()
            A.release(m0)

        if 2 in phases:
            phase2()

        def phase3():
            cast_w(w_bg, wb_bg, D, 512, "bg")
            cast_w(w_bm, wb_bm, D, 512, "bm")
            cast_w(w_out, wb_out, D, 512, "out")
            cast_w(w_fg, wb_fg, DFF, 512, "fg")
            cast_w(w_fu, wb_fu, DFF, 512, "fu")
            cast_w(w_fd, wb_fd, D, 512, "fd")
            m0 = A.mark()
            NQT = T_LOC // 512
            scale = float(DH ** -0.5)
            ebk_t = A.alloc(LF, F32, 33)
            b_ebk = Buf()
            P.dma("sp", ebk_t, c_ebk, writes=[b_ebk])
            esel_t = A.alloc(32 * 128, BF16, 32)
            b_esel = Buf()
            P.dma("sp", esel_t, c_esel.rearrange("r n k -> r (n k)"), writes=[b_esel])
            esel3 = esel_t.rearrange("r (n k) -> r n k", k=128)
            relt = A.alloc(H, F32, 32)
            b_relt = Buf()
            P.dma("sp", relt, rel_bias, writes=[b_relt])
            relrep = A.alloc(128, F32, 33)
            b_relrep = Buf()
            memset("dve", relrep, NEG, [b_relrep])
            Fb = A.alloc(LF, F32)
            b_Fb = Buf()
            GT_r = Ring([A.alloc(GW, F32) for _ in range(2)])
            kT_r = Ring([A.alloc(TT, BF16) for _ in range(2)])
            v_r = Ring([A.alloc(NT * 128, BF16) for _ in range(2)])
            q_r = Ring([A.alloc(T_LOC, BF16) for _ in range(2)])
            ms_r = Ring([A.alloc(T_LOC, BF16, 32) for _ in range(2)])
            tmp_r = Ring([A.alloc(512, F32) for _ in range(3)])
            pt_r = Ring([A.alloc(512, BF16) for _ in range(4)])
            rd_r = Ring([A.alloc(512, F32) for _ in range(2)])
            yo_r = Ring([A.alloc(512, BF16) for _ in range(2)])
            cf_r = Ring([A.alloc(1, F32) for _ in range(2)])
            s_ring = Ring([psum[0], psum[1], psum[2], psum[3]], excl=True)
            o_ps, b_o = psum[4], Buf(excl=True)
            d_ps, b_d = psum[5], Buf(excl=True)
            f_ring = Ring([psum[6], psum[7]], excl=True)
            b_frep = Buf()
            for h in range(H):
                cp("dve", relrep[0:32, :], relt[:, h:h + 1].to_broadcast([32, 128]), [b_relt], [b_relrep])
                for c0 in range(0, LF, 512):
                    cn = min(512, LF - c0)
                    pf, b_pf = f_ring.next()
                    mm(pf[:, 0:cn], relrep, ebk_t[:, c0:c0 + cn], True, True, [b_relrep, b_ebk], [b_pf])
                    cp("act", Fb[:, c0:c0 + cn], pf[:, 0:cn], [b_pf], [b_Fb])
                cf, b_cf = cf_r.next()
                cp("dve", cf, Fb[:, LF - 1:LF], [b_Fb], [b_cf])
                P.dma("sp", s_frep[h], Fb, reads=[b_Fb], writes=[b_frep])
                GT, b_GT = GT_r.next()
                src = bass.AP(tensor=s_frep.tensor, offset=s_frep.offset + h * 128 * LF + 127,
                              ap=[[LF - 1, 128], [1, GW]])
                P.dma("sp", GT, src, reads=[b_frep], writes=[b_GT])
                kTh, b_kTh = kT_r.next()
                P.dma("sp", kTh, s_kb[h], writes=[b_kTh])
                vt, b_vt = v_r.next()
                vt3 = vt.rearrange("p (n d) -> p n d", d=128)
                for n0 in range(0, NT, 16):
                    nn = min(16, NT - n0)
                    P.dma("sp", vt3[:, n0:n0 + nn, :],
                          s_vb[n0 * 128:(n0 + nn) * 128, h * 128:(h + 1) * 128].rearrange("(n p) d -> p n d", p=128),
                          writes=[b_vt])
                qh, b_qh = q_r.next()
                P.dma("sp", qh, s_qb[h], writes=[b_qh])
                msh, b_msh = ms_r.next()
                P.dma("sp", msh, s_msel[h], writes=[b_msh])
                for j in range(NQT):
                    q0 = T_PRE + 512 * j
                    ql = 512 * j
                    nsub = (q0 + 512) // 128
                    LA = 2
                    pts = {}

                    def front(ks):
                        k0 = ks * 128
                        n = k0 // BS
                        delta = q0 - k0
                        sp_, b_sp = s_ring.next()
                        mm(sp_, kTh[:, k0:k0 + 128], qh[:, ql:ql + 512], True, False, [b_kTh, b_qh], [b_sp])
                        mm(sp_, esel3[:, n, :], msh[:, ql:ql + 512], False, True, [b_esel, b_msh], [b_sp])
                        pt, b_pt = pt_r.next()
                        if delta >= FAR_DELTA:
                            act(pt, sp_, AF.Exp, [b_sp, b_cf], [b_pt], bias=cf[:, 0:1], scale=scale)
                        else:
                            tmp, b_tmp = tmp_r.next()
                            x0 = delta + 384
                            stt("dve", tmp, sp_, scale, GT[:, x0:x0 + 512], ALU.mult, ALU.add, [b_sp, b_GT], [b_tmp])
                            act(pt, tmp, AF.Exp, [b_tmp], [b_pt])
                        pts[ks] = (pt, b_pt)
                    for ks in range(min(LA, nsub)):
                        front(ks)
                    for ks in range(nsub):
                        if ks + LA < nsub:
                            front(ks + LA)
                        pt, b_pt = pts.pop(ks)
                        mm(o_ps, vt3[:, ks, :], pt, ks == 0, ks == nsub - 1, [b_vt, b_pt], [b_o])
                        mm(d_ps, ones_bf, pt, ks == 0, ks == nsub - 1, [b_ones, b_pt], [b_d])
                    rd, b_rd = rd_r.next()
                    P.op("dve", "reciprocal", [b_d], [b_rd], out=rd, in_=d_ps)
                    yo, b_yo = yo_r.next()
                    tt("dve", yo, o_ps, rd, ALU.mult, [b_o, b_rd], [b_yo])
                    P.dma("pool", s_yb[h, :, ql:ql + 512], yo, reads=[b_yo])
            P.barrier()
            A.release(m0)

        if 3 in phases:
            phase3()

        def phase4():
            NLB = T_LOC // 512
            cast_w(w_bg, wb_bg, D, 512, "bg")
            cast_w(w_bm, wb_bm, D, 512, "bm")
            cast_w(w_out, wb_out, D, 512, "out")
            m0 = A.mark()
            Wg_t = A.alloc(8 * D, BF16)
            Wm_t = A.alloc(8 * D, BF16)
            Wg3 = Wg_t.rearrange("p (c n) -> p c n", c=8)
            Wm3 = Wm_t.rearrange("p (c n) -> p c n", c=8)
            b_Wg, b_Wm = Buf(), Buf()
            for c0 in range(0, D, 512):
                P.dma("sp", Wg3[:, :, c0:c0 + 512], wb_bg.rearrange("(c p) n -> p c n", p=128)[:, :, c0:c0 + 512],
                      reads=[bufs_w[("bg", c0)]], writes=[b_Wg])
                P.dma("sp", Wm3[:, :, c0:c0 + 512], wb_bm.rearrange("(c p) n -> p c n", p=128)[:, :, c0:c0 + 512],
                      reads=[bufs_w[("bm", c0)]], writes=[b_Wm])
            ya_r = Ring([A.alloc(8 * 512, BF16) for _ in range(1)])
            yb_r = Ring([A.alloc(8 * 512, BF16) for _ in range(1)])
            sg_r = Ring([A.alloc(512, BF16) for _ in range(4)])
            t_r = Ring([A.alloc(512, F32) for _ in range(4)])
            mix_r = Ring([A.alloc(DC * 512, BF16) for _ in range(1)])
            wo_r = Ring([A.alloc(DC * 512, BF16) for _ in range(2)])
            xr_r = Ring([A.alloc(512, F32) for _ in range(3)])
            ho_r = Ring([A.alloc(512, F32) for _ in range(3)])
            pab = Ring([psum[i] for i in range(4)], excl=True)
            po = Ring([psum[i] for i in range(4, 8)], excl=True)
            wb_out3 = wb_out.rearrange("(c p) n -> p c n", p=128)
            for tb in range(NLB):
                ql = tb * 512
                ya, b_ya = ya_r.next()
                ya3 = ya.rearrange("p (h n) -> p h n", h=8)
                P.dma("sp", ya3, s_ya[:, :, ql:ql + 512].rearrange("h p t -> p h t"), writes=[b_ya])
                yb, b_yb = yb_r.next()
                yb3 = yb.rearrange("p (h n) -> p h n", h=8)
                P.dma("sp", yb3, s_yb[:, :, ql:ql + 512].rearrange("h p t -> p h t"), writes=[b_yb])
                mix, b_mix = mix_r.next()
                mix3 = mix.rearrange("p (c n) -> p c n", c=DC)
                for dc in range(DC):
                    sga, b_sga = sg_r.next()
                    P.dma("sp", sga, s_sg[0, dc, :, ql:ql + 512], writes=[b_sga])
                    sgb, b_sgb = sg_r.next()
                    P.dma("sp", sgb, s_sg[1, dc, :, ql:ql + 512], writes=[b_sgb])
                    pa, b_pa = pab.next()
                    for kc in range(8):
                        mm(pa, Wg3[:, kc, dc * 128:(dc + 1) * 128], ya3[:, kc, :], kc == 0, kc == 7, [b_Wg, b_ya], [b_pa])
                    pb, b_pb = pab.next()
                    for kc in range(8):
                        mm(pb, Wm3[:, kc, dc * 128:(dc + 1) * 128], yb3[:, kc, :], kc == 0, kc == 7, [b_Wm, b_yb], [b_pb])
                    t1, b_t1 = t_r.next()
                    tt("dve", t1, pa, sga, ALU.mult, [b_pa, b_sga], [b_t1])
                    t2, b_t2 = t_r.next()
                    tt("dve", t2, pb, sgb, ALU.mult, [b_pb, b_sgb], [b_t2])
                    tt("pool", mix3[:, dc, :], t1, t2, ALU.add, [b_t1, b_t2], [b_mix])
                for cg in range(4):
                    wo, b_wo = wo_r.next()
                    wo3 = wo.rearrange("p (c n) -> p c n", c=DC)
                    P.dma("sp", wo3, wb_out3[:, :, cg * 512:(cg + 1) * 512], reads=[bufs_w[("out", cg * 512)]],
                          writes=[b_wo])
                    for s in range(4):
                        r0 = ql + s * 128
                        xr, b_xr = xr_r.next()
                        P.dma("sp", xr, x_in[T_PRE + r0:T_PRE + r0 + 128, cg * 512:(cg + 1) * 512], writes=[b_xr])
                        pq, b_pq = po.next()
                        for kc in range(DC):
                            mm(pq, mix3[:, kc, s * 128:(s + 1) * 128], wo3[:, kc, :], kc == 0, kc == DC - 1,
                               [b_mix, b_wo], [b_pq])
                        ho, b_ho = ho_r.next()
                        tt("dve", ho, pq, xr, ALU.add, [b_pq, b_xr], [b_ho])
                        P.dma("pool", s_h[r0:r0 + 128, cg * 512:(cg + 1) * 512], ho, reads=[b_ho], writes=[b_sh])
            P.barrier()
            A.release(m0)

        b_sh = Buf()
        if 4 in phases:
            phase4()

        def phase5():
            NLB = T_LOC // 512
            cast_w(w_fg, wb_fg, DFF, 512, "fg")
            cast_w(w_fu, wb_fu, DFF, 512, "fu")
            cast_w(w_fd, wb_fd, D, 512, "fd")
            m0 = A.mark()
            nfw, b_nfw = bcast_load(norm_ffn_w, D)
            xt_r = Ring([A.alloc(D, F32) for _ in range(2)])
            xn_r = Ring([A.alloc(D, BF16) for _ in range(2)])
            uT_r = Ring([A.alloc(DC * 512, BF16) for _ in range(1)])
            hid_r = Ring([A.alloc(FC * 512, BF16) for _ in range(1)])
            wg_r = Ring([A.alloc(DC * 256, BF16) for _ in range(2)])
            wu_r = Ring([A.alloc(DC * 256, BF16) for _ in range(2)])
            wd_r = Ring([A.alloc(22 * 512, BF16) for _ in range(2)])
            sl_r = Ring([A.alloc(512, F32) for _ in range(2)])
            hr_r = Ring([A.alloc(512, F32) for _ in range(2)])
            yo_r = Ring([A.alloc(512, F32) for _ in range(2)])
            small_r = Ring([A.alloc(8, F32) for _ in range(4)])
            fr = Ring([psum[i] for i in range(4)], excl=True)
            dn = [(psum[4 + i], Buf(excl=True)) for i in range(4)]
            wb_fg3 = wb_fg.rearrange("(c p) n -> p c n", p=128)
            wb_fu3 = wb_fu.rearrange("(c p) n -> p c n", p=128)
            wb_fd3 = wb_fd.rearrange("(c p) n -> p c n", p=128)
            out_toks = []
            for tb in range(NLB):
                ql = tb * 512
                uT, b_uT = uT_r.next()
                uT3 = uT.rearrange("p (c n) -> p c n", c=DC)
                for s in range(4):
                    xt, b_xt = xt_r.next()
                    r0 = ql + s * 128
                    P.dma("sp", xt, s_h[r0:r0 + 128, :], reads=[b_sh], writes=[b_xt])
                    xn, b_xn = xn_r.next()
                    sm, b_sm = small_r.next()
                    act(xn, xt, AF.Square, [b_xt], [b_xn, b_sm], accum_out=sm[:, 0:1])
                    act(sm[:, 1:2], sm[:, 0:1], AF.Sqrt, [b_sm], [b_sm], bias=EPS, scale=1.0 / D)
                    P.op("dve", "reciprocal", [b_sm], [b_sm], out=sm[:, 2:3], in_=sm[:, 1:2])
                    stt("dve", xn, xt, sm[:, 2:3], nfw, ALU.mult, ALU.mult, [b_xt, b_sm, b_nfw], [b_xn])
                    for q4 in range(4):
                        ps, b_ps = fr.next()
                        for cc in range(4):
                            c = q4 * 4 + cc
                            mm(ps[:, cc * 128:(cc + 1) * 128], xn[:, c * 128:(c + 1) * 128], ident_bf, True, True,
                               [b_xn, b_identbf], [b_ps])
                        dst = uT3[:, q4 * 4:(q4 + 1) * 4, s * 128:(s + 1) * 128]
                        src = ps.rearrange("p (c n) -> p c n", c=4)
                        cp("act" if q4 % 2 == 0 else "dve", dst, src, [b_ps], [b_uT])
                hid, b_hid = hid_r.next()
                hid3 = hid.rearrange("p (c n) -> p c n", c=FC)
                for cg2 in range(FC // 2):
                    c0 = cg2 * 256
                    wg, b_wg = wg_r.next()
                    wg3 = wg.rearrange("p (c n) -> p c n", c=DC)
                    P.dma("sp", wg3, wb_fg3[:, :, c0:c0 + 256], reads=[bufs_w[("fg", (c0 // 512) * 512)]], writes=[b_wg])
                    wu, b_wu = wu_r.next()
                    wu3 = wu.rearrange("p (c n) -> p c n", c=DC)
                    P.dma("sp", wu3, wb_fu3[:, :, c0:c0 + 256], reads=[bufs_w[("fu", (c0 // 512) * 512)]], writes=[b_wu])
                    for g in range(2):
                        fc = cg2 * 2 + g
                        pg_, b_pg_ = fr.next()
                        for c in range(DC):
                            mm(pg_, wg3[:, c, g * 128:(g + 1) * 128], uT3[:, c, :], c == 0, c == DC - 1,
                               [b_wg, b_uT], [b_pg_])
                        pu_, b_pu_ = fr.next()
                        for c in range(DC):
                            mm(pu_, wu3[:, c, g * 128:(g + 1) * 128], uT3[:, c, :], c == 0, c == DC - 1,
                               [b_wu, b_uT], [b_pu_])
                        sl, b_sl = sl_r.next()
                        act(sl, pg_, AF.Silu, [b_pg_], [b_sl])
                        tt("dve", hid3[:, fc, :], sl, pu_, ALU.mult, [b_sl, b_pu_], [b_hid])
                for cg in range(4):
                    for hf in range(2):
                        wd, b_wd = wd_r.next()
                        wd3 = wd.rearrange("p (c n) -> p c n", c=22)
                        P.dma("sp", wd3, wb_fd3[:, hf * 22:(hf + 1) * 22, cg * 512:(cg + 1) * 512],
                              reads=[bufs_w[("fd", cg * 512)]], writes=[b_wd])
                        for s in range(4):
                            pd, b_pd = dn[s]
                            for f in range(22):
                                fc = hf * 22 + f
                                mm(pd, hid3[:, fc, s * 128:(s + 1) * 128], wd3[:, f, :], fc == 0, fc == FC - 1,
                                   [b_hid, b_wd], [b_pd])
                    for s in range(4):
                        pd, b_pd = dn[s]
                        r0 = ql + s * 128
                        hr, b_hr = hr_r.next()
                        P.dma("sp", hr, s_h[r0:r0 + 128, cg * 512:(cg + 1) * 512], reads=[b_sh], writes=[b_hr])
                        yo, b_yo = yo_r.next()
                        tt("dve", yo, pd, hr, ALU.add, [b_pd, b_hr], [b_yo])
                        P.dma("pool", y_out[r0:r0 + 128, cg * 512:(cg + 1) * 512], yo, reads=[b_yo])
            P.barrier()
            A.release(m0)

        if 5 in phases:
            phase5()

        P.barrier()
        P.emit()
    return nc, P


_CACHE = {}


def make_in_maps(inputs, B, T, T_PRE, T_LOC):
    consts = host_consts(T_PRE, T_LOC)
    x = np.asarray(inputs["x"], dtype=np.float32)
    in_maps = []
    nhalf = T // T_LOC
    for b in range(B):
        for half in range(nhalf):
            xc = np.zeros((T_PRE + T_LOC, D), np.float32)
            if half > 0:
                xc[:T_PRE] = x[b, half * T_LOC - T_PRE:half * T_LOC]
            xc[T_PRE:] = x[b, half * T_LOC:(half + 1) * T_LOC]
            m = {"x": xc}
            for k in ("norm_mix_w", "conv_w", "a_log", "dt_bias", "gdn_o_norm_w", "q_norm_w", "k_norm_w",
                      "w_in", "w_branch_gdn", "w_branch_moba", "w_out", "norm_ffn_w", "w_ffn_gate",
                      "w_ffn_up", "w_ffn_down"):
                a = np.asarray(inputs[k], dtype=np.float32)
                m[k] = np.ascontiguousarray(a[0]) if a.shape[0] == 1 and a.ndim == 3 else np.ascontiguousarray(a)
            m["rel_bias"] = np.ascontiguousarray(np.asarray(inputs["rel_bias"], dtype=np.float32))
            for k, v in consts.items():
                m["c_" + k] = v
            m["c_rmask"] = rmask_for(half, T_PRE, T_LOC)
            in_maps.append(m)
    return in_maps


def kernel(**inputs):
    x = np.asarray(inputs["x"])
    B, T, _ = x.shape
    T_LOC = T // 2
    T_PRE = T_LOC
    key = (T_PRE, T_LOC)
    if key not in _CACHE:
        _CACHE[key] = build_program(T_PRE, T_LOC)[0]
    nc = _CACHE[key]
    in_maps = make_in_maps(inputs, B, T, T_PRE, T_LOC)
    res = run_bass_kernel_spmd(nc, in_maps, core_ids=list(range(len(in_maps))))
    out = np.zeros((B, T, D), np.float32)
    i = 0
    for b in range(B):
        for half in range(2):
            out[b, half * T_LOC:(half + 1) * T_LOC] = res.results[i]["y"]
            i += 1
    return out
```
